# Optimizing a Trainium2 kernel written in Bass

```python
import jax, jax.numpy as jnp
from jax import lax
import numpy as np

D_MODEL = 2048
BATCH = 1
SEQ = 16384
DEPTH = 1

RET_H = 4
RET_DK = 256
RET_DV = 256
RET_W = RET_H * RET_DK
HG_H = 8
HG_DK = 128
HG_DV = 128
HG_W = HG_H * HG_DK
CHUNK = 128
IN_SIZES = [RET_W] * 4 + [HG_W] * 4 + [D_MODEL] * 2
IN_COLS = sum(IN_SIZES)
D_FF = 5632
CONV_W = 3
ROPE_BASE = 10000.0
NORM_EPS = 1e-6
HEAD_EPS = 1e-5

kernel_name = "hybrid_retention_hgrn2_convffn_adaln"


def rms_norm(x, g, eps=NORM_EPS):
    xf = x.astype(jnp.float32)
    y = xf * lax.rsqrt(jnp.mean(xf * xf, axis=-1, keepdims=True) + eps)
    return (y * g.astype(jnp.float32)).astype(x.dtype)


def modulate(h, shift, scale):
    return h * (1.0 + scale[:, None, :]) + shift[:, None, :]


def rope(t, positions):
    d = t.shape[-1]
    inv_freq = ROPE_BASE ** (-jnp.arange(0, d, 2, dtype=jnp.float32) / d)
    ang = positions.astype(jnp.float32)[..., None] * inv_freq
    cos = jnp.cos(ang)[:, :, None, :]
    sin = jnp.sin(ang)[:, :, None, :]
    t1, t2 = t[..., : d // 2], t[..., d // 2:]
    return jnp.concatenate([t1 * cos - t2 * sin, t2 * cos + t1 * sin], axis=-1)


def to_chunks(t):
    B, S, H, d = t.shape
    return t.reshape(B, S // CHUNK, CHUNK, H, d).transpose(1, 0, 3, 2, 4)


def from_chunks(t):
    n, B, H, C, d = t.shape
    return t.transpose(1, 0, 3, 2, 4).reshape(B, n * C, H, d)


def retention_chunkwise(q, k, v):
    B, S, H, dk = q.shape
    dv = v.shape[-1]
    gamma = 1.0 - jnp.exp2(-5.0 - jnp.arange(H, dtype=jnp.float32))
    log_g = jnp.log(gamma)
    idx = jnp.arange(CHUNK, dtype=jnp.float32)
    diff = idx[:, None] - idx[None, :]
    decay = jnp.where(diff[None] >= 0, jnp.exp(jnp.maximum(diff, 0.0)[None] * log_g[:, None, None]), 0.0)
    xi = jnp.exp((idx[None, :] + 1.0) * log_g[:, None])[:, :, None]
    zeta = jnp.exp((CHUNK - 1.0 - idx[None, :]) * log_g[:, None])[:, :, None]
    g_chunk = jnp.exp(CHUNK * log_g)[:, None, None]
    k = k * (dk ** -0.5)

    def step(R, qkv):
        qc, kc, vc = qkv
        inner = jnp.einsum('bhnd,bhmd->bhnm', qc, kc) * decay
        o = (jnp.einsum('bhnm,bhme->bhne', inner, vc)
             + jnp.einsum('bhnd,bhde->bhne', qc, R) * xi)
        R = g_chunk * R + jnp.einsum('bhmd,bhme->bhde', kc * zeta, vc)
        return R, o

    R0 = jnp.zeros((B, H, dk, dv), jnp.float32)
    _, o = lax.scan(step, R0, (to_chunks(q), to_chunks(k), to_chunks(v)))
    return from_chunks(o)


def hgrn2_chunkwise(q, f, i):
    B, S, H, dk = q.shape
    dv = i.shape[-1]
    k = 1.0 - f
    logf = jnp.log(f)
    t_idx = jnp.arange(CHUNK)
    causal = (t_idx[:, None] >= t_idx[None, :])[:, :, None]

    def step(Sm, inp):
        qc, kc, vc, lf = inp
        b = jnp.cumsum(lf, axis=2)
        dlt = b[:, :, :, None, :] - b[:, :, None, :, :]
        dec = jnp.where(causal, jnp.exp(jnp.minimum(dlt, 0.0)), 0.0)
        A = jnp.einsum('bhtsd,bhsd->bhts', qc[:, :, :, None, :] * dec, kc)
        o = (jnp.einsum('bhts,bhse->bhte', A, vc)
             + jnp.einsum('bhtd,bhde->bhte', qc * jnp.exp(b), Sm))
        b_last = b[:, :, -1:, :]
        Sm = (jnp.exp(b_last[:, :, 0, :])[..., None] * Sm
              + jnp.einsum('bhsd,bhse->bhde', kc * jnp.exp(b_last - b), vc))
        return Sm, o

    S0 = jnp.zeros((B, H, dk, dv), jnp.float32)
    _, o = lax.scan(step, S0, (to_chunks(q), to_chunks(k), to_chunks(i), to_chunks(logf)))
    return from_chunks(o)


def head_layer_norm(o):
    mu = jnp.mean(o, axis=-1, keepdims=True)
    oc = o - mu
    return oc * lax.rsqrt(jnp.mean(oc * oc, axis=-1, keepdims=True) + HEAD_EPS)


def head_rms_norm(o):
    return o * lax.rsqrt(jnp.mean(o * o, axis=-1, keepdims=True) + HEAD_EPS)


def causal_depthwise_conv(u, w, b):
    ch = u.shape[-1]
    y = lax.conv_general_dilated(u, w[:, None, :].astype(u.dtype), window_strides=(1,),
                                 padding=[(CONV_W - 1, 0)],
                                 dimension_numbers=('NWC', 'WIO', 'NWC'),
                                 feature_group_count=ch)
    return y + b.astype(u.dtype)


def setup_inputs(seed: int = 0) -> dict:
    key = jax.random.key(seed)
    ks = jax.random.split(key, 20)
    f32 = jnp.float32
    L = DEPTH
    nrm = lambda k, shape, scale: jax.random.normal(k, shape, f32) * scale
    x = nrm(ks[0], (BATCH, SEQ, D_MODEL), 1.0)
    c = nrm(ks[1], (BATCH, D_MODEL), 1.0)
    offset = jax.random.randint(ks[2], (BATCH, 1), 0, 1024, dtype=jnp.int32)
    positions = offset + jnp.arange(SEQ, dtype=jnp.int32)[None, :]
    return {
        "x": x,
        "c": c,
        "positions": positions,
        "w_ada": nrm(ks[3], (L, D_MODEL, 6 * D_MODEL), D_MODEL ** -0.5),
        "b_ada": nrm(ks[4], (L, 6 * D_MODEL), 0.01),
        "g_norm1": 1.0 + nrm(ks[5], (L, D_MODEL), 0.01),
        "w_in": nrm(ks[6], (L, D_MODEL, IN_COLS), D_MODEL ** -0.5),
        "w_ret_o": nrm(ks[7], (L, RET_H * RET_DV, D_MODEL), (RET_H * RET_DV) ** -0.5),
        "w_hg_o": nrm(ks[8], (L, HG_H * HG_DV, D_MODEL), (HG_H * HG_DV) ** -0.5),
        "w_out": nrm(ks[9], (L, D_MODEL, D_MODEL), D_MODEL ** -0.5),
        "hg_lb": nrm(ks[10], (L + 1, HG_W), 1.0),
        "g_norm2": 1.0 + nrm(ks[11], (L, D_MODEL), 0.01),
        "w_up": nrm(ks[12], (L, D_MODEL, 2 * D_FF), D_MODEL ** -0.5),
        "conv_w": nrm(ks[13], (L, CONV_W, 2 * D_FF), CONV_W ** -0.5),
        "conv_b": nrm(ks[14], (L, 2 * D_FF), 0.01),
        "w_down": nrm(ks[15], (L, D_FF, D_MODEL), D_FF ** -0.5),
        "g_final": 1.0 + nrm(ks[16], (D_MODEL,), 0.01),
    }


def reference(x, c, positions, w_ada, b_ada, g_norm1, w_in, w_ret_o, w_hg_o, w_out,
              hg_lb, g_norm2, w_up, conv_w, conv_b, w_down, g_final):
    B, S, D = x.shape
    f32 = jnp.float32
    split_at = [int(v) for v in np.cumsum(IN_SIZES)[:-1]]
    lb_all = jnp.cumsum(jax.nn.softmax(hg_lb.astype(f32), axis=0), axis=0)

    for l in range(DEPTH):
        mod = jax.nn.silu(c) @ w_ada[l] + b_ada[l]
        shift1, scale1, gate1, shift2, scale2, gate2 = jnp.split(mod, 6, axis=-1)

        h = modulate(rms_norm(x, g_norm1[l]), shift1, scale1)
        p = h @ w_in[l]
        rq, rk, rv, rg, hq, hf, hi, hg, ga, gb = jnp.split(p, split_at, axis=-1)

        q_r = rope(rq.astype(f32).reshape(B, S, RET_H, RET_DK), positions)
        k_r = rope(rk.astype(f32).reshape(B, S, RET_H, RET_DK), positions)
        v_r = rv.astype(f32).reshape(B, S, RET_H, RET_DV)
        o_r = head_layer_norm(retention_chunkwise(q_r, k_r, v_r)).reshape(B, S, RET_H * RET_DV)
        y_a = (o_r.astype(x.dtype) * jax.nn.silu(rg)) @ w_ret_o[l]

        lb = lb_all[l]
        f_g = lb + (1.0 - lb) * jax.nn.sigmoid(hf.astype(f32))
        q_h = jax.nn.silu(hq.astype(f32)).reshape(B, S, HG_H, HG_DK)
        o_h = hgrn2_chunkwise(q_h, f_g.reshape(B, S, HG_H, HG_DK),
                              hi.astype(f32).reshape(B, S, HG_H, HG_DV))
        o_h = head_rms_norm(o_h).reshape(B, S, HG_H * HG_DV)
        y_b = (o_h.astype(x.dtype) * jax.nn.silu(hg)) @ w_hg_o[l]

        merged = jax.nn.sigmoid(ga) * y_a + jax.nn.sigmoid(gb) * y_b
        x = x + gate1[:, None, :] * (merged @ w_out[l])

        h2 = modulate(rms_norm(x, g_norm2[l]), shift2, scale2)
        u = causal_depthwise_conv(h2 @ w_up[l], conv_w[l], conv_b[l])
        a, bgl = jnp.split(u, 2, axis=-1)
        x = x + gate2[:, None, :] * ((jax.nn.silu(a) * bgl) @ w_down[l])

    return rms_norm(x, g_final)
```

```python
import contextlib
import numpy as np
import concourse.bass as bass
import concourse.mybir as mybir
from concourse.bass_utils import run_bass_kernel_spmd

F32 = mybir.dt.float32
BF16 = mybir.dt.bfloat16
I32 = mybir.dt.int32
ALU = mybir.AluOpType
AF = mybir.ActivationFunctionType

D = 2048
KC = 16
SEQ = 16384
NCORES = 8
CPTMAX = 2
TMAX = 128 * CPTMAX
DFF = 5632
NFB = DFF // 128
RET_H, HG_H = 4, 8
IN_COLS = 12288
EPS = 1e-6
HEPS = 1e-5


class Sched:
    ENGS = ("pe", "act", "dve", "pool", "sp")

    def __init__(self, nc, stack):
        self.nc = nc
        self.stack = stack
        self.q = {e: [] for e in self.ENGS}
        self.cnt = {e: 0 for e in self.ENGS}
        self.known = {e: {} for e in self.ENGS}
        self.last_w = {}
        self.readers = {}
        self.sems = {}
        self.dma_cnt = {}
        for e in ("pe", "act", "dve", "pool"):
            self.sems[e] = stack.enter_context(nc.semaphore("c_" + e))

    def _sem(self, key):
        if key not in self.sems:
            self.sems[key] = self.stack.enter_context(self.nc.semaphore("d_" + str(key)))
            self.dma_cnt[key] = 0
        return self.sems[key]

    def _deps(self, eng, reads, writes, extra=()):
        need = {}

        def add(t):
            if t is None:
                return
            sk, v = t
            if need.get(sk, 0) < v:
                need[sk] = v
        for k in reads:
            add(self.last_w.get(k))
        for k in writes:
            add(self.last_w.get(k))
            for t in self.readers.get(k, ()):
                add(t)
        for t in extra:
            add(t)
        waits = []
        for sk, v in need.items():
            if sk == eng and eng == "pe":
                continue
            if self.known[eng].get(sk, 0) >= v:
                continue
            self.known[eng][sk] = v
            waits.append((sk, v))
        return waits

    def _commit(self, tok, reads, writes):
        for k in reads:
            self.readers.setdefault(k, []).append(tok)
        for k in writes:
            self.last_w[k] = tok
            self.readers[k] = []

    def op(self, eng, fn, reads=(), writes=()):
        waits = self._deps(eng, reads, writes)
        self.cnt[eng] += 1
        tok = (eng, self.cnt[eng])
        self.q[eng].append((fn, waits, (eng, 1)))
        self._commit(tok, reads, writes)
        return tok

    def dma(self, eng, slot, fn, reads=(), writes=()):
        self._sem(slot)
        waits = self._deps(eng, reads, writes)
        self.dma_cnt[slot] += 16
        tok = (slot, self.dma_cnt[slot])
        self.q[eng].append((fn, waits, (slot, 16)))
        self._commit(tok, reads, writes)
        return tok

    def barrier(self):
        toks = [(e, self.cnt[e]) for e in ("pe", "act", "dve", "pool") if self.cnt[e] > 0]
        toks += [(k, v) for k, v in self.dma_cnt.items() if v > 0]
        for e in self.ENGS:
            self.wait_all(e, toks)

    def wait_all(self, eng, toks):
        waits = self._deps(eng, (), (), toks)
        self.q[eng].append((None, waits, None))

    def replay(self, block):
        engmap = {"pe": block.tensor, "act": block.scalar, "dve": block.vector,
                  "pool": block.gpsimd, "sp": block.sync}
        sems = self.sems
        for e in self.ENGS:
            def body(engine, items=self.q[e]):
                for fn, waits, inc in items:
                    for sk, v in waits:
                        engine.wait_ge(sems[sk], v)
                    if fn is None:
                        continue
                    ins = fn(engine)
                    ins.then_inc(sems[inc[0]], inc[1])
            engmap[e](body)


def host_consts():
    c = {}
    c["ident"] = np.eye(128, dtype=np.float32)
    idx = np.arange(128, dtype=np.float64)
    gam = 1.0 - np.exp2(-5.0 - np.arange(RET_H, dtype=np.float64))
    lg = np.log(gam)
    diff = idx[None, :] - idx[:, None]
    dec = np.where(diff >= 0, np.exp(np.maximum(diff, 0)[None] * lg[:, None, None]), 0.0)
    c["decT"] = np.ascontiguousarray((dec * 256 ** -0.5).transpose(1, 0, 2)).astype(np.float32)
    xi = np.exp((idx[None, :] + 1.0) * lg[:, None])
    c["xib"] = np.ascontiguousarray(np.broadcast_to(xi[None], (128, RET_H, 128))).astype(np.float32)
    zeta = np.exp((127.0 - idx[None, :]) * lg[:, None]) * 256 ** -0.5
    c["zeta"] = np.ascontiguousarray(zeta.T).astype(np.float32)
    c["gchunk"] = np.exp(128 * lg)
    sub = (np.arange(128) // 32)
    same = sub[:, None] == sub[None, :]
    s = np.arange(128)
    c["triinc"] = (same & (s[:, None] <= s[None, :])).astype(np.float32)
    c["trirev"] = (same & (s[:, None] > s[None, :])).astype(np.float32)
    c["rowmask"] = (sub[:, None] == np.arange(4)[None, :]).astype(np.float32)
    c["colmask"] = np.ascontiguousarray(np.broadcast_to((sub[None, :] == np.arange(4)[:, None])[None], (128, 4, 128))).astype(np.float32)
    c["invf"] = (10000.0 ** (-np.arange(0, 256, 2, dtype=np.float32) / 256)).astype(np.float32)[:, None]
    c["neghalf"] = np.full((128, 8), -0.5, np.float32)
    return c


CONST_SHAPES = {"ident": [128, 128], "decT": [128, 4, 128], "xib": [128, 4, 128], "zeta": [128, 4],
                "triinc": [128, 128], "trirev": [128, 128], "rowmask": [128, 4], "colmask": [128, 4, 128],
                "invf": [128, 1], "neghalf": [128, 8]}


def bc_mid(a, n):
    return bass.AP(a.tensor, a.offset, [list(a.ap[0]), [0, n], list(a.ap[-1])])


def build(tiles, dbg=None):
    NCH = sum(t[0] for t in tiles)
    NTOK = NCH * 128
    NT = len(tiles)
    NOUT = sum(t[0] * 128 for t in tiles if t[1] == "f" and t[2] is not None)
    HC = host_consts()
    gchunk = [float(v) for v in HC["gchunk"]]
    nc = bass.Bass("TRN2", target_bir_lowering=False)
    din = lambda n, s, dt=F32: nc.dram_tensor(n, s, dt, kind="ExternalInput").ap()
    x_d = din("x", [NTOK, D])
    pos_d = din("positions", [1, NTOK], I32)
    vmask_d = din("vmask", [128, NCH])
    pmask_d = din("pmask", [128, NT])
    c_d = din("c", [1, D])
    w_ada = din("w_ada", [D, 6 * D])
    b_ada = din("b_ada", [1, 6 * D])
    g1_d = din("g_norm1", [1, D])
    w_in = din("w_in", [D, IN_COLS])
    w_ro = din("w_ret_o", [1024, D])
    w_ho = din("w_hg_o", [1024, D])
    w_out = din("w_out", [D, D])
    lb_d = din("hg_lb", [2, 1024])
    g2_d = din("g_norm2", [1, D])
    w_up = din("w_up", [D, 2 * DFF])
    cw_d = din("conv_w", [3, 2 * DFF])
    cb_d = din("conv_b", [1, 2 * DFF])
    w_dn = din("w_down", [DFF, D])
    gf_d = din("g_final", [1, D])
    cd = {k: din("k_" + k, s) for k, s in CONST_SHAPES.items()}
    out_d = nc.dram_tensor("out", [max(NOUT, 128), D], F32, kind="ExternalOutput").ap()
    mod_d = nc.dram_tensor("mod_scratch", [1, 6 * D], F32).ap()
    WSRC = {"w_in": w_in, "w_ret_o": w_ro, "w_hg_o": w_ho, "w_out": w_out, "w_up": w_up, "w_down": w_dn}
    WBF = {k: nc.dram_tensor("bf_" + k, list(v.shape), BF16).ap() for k, v in WSRC.items()}
    WNAME = {id(v): k for k, v in WSRC.items()}
    PCB = 1024
    dbg_d = {}
    if dbg:
        for k, (s, dt_) in dbg.items():
            dbg_d[k] = nc.dram_tensor("dbg_" + k, s, dt_, kind="ExternalOutput").ap()

    with contextlib.ExitStack() as st:
        S = Sched(nc, st)
        sbytes = [0]

        def sb(name, shape, dt=F32):
            n = int(np.prod(shape[1:])) * (4 if dt in (F32, I32) else 2)
            sbytes[0] += n
            return st.enter_context(nc.sbuf_tensor(name, shape, dt))
        K = {k: sb("c_" + k, s) for k, s in CONST_SHAPES.items()}
        WS = [sb("ws%d" % i, [128, KC, 512], BF16) for i in range(2)]
        AB = sb("AB", [128, 4, KC])
        LBb = sb("LBb", [128, 1024])
        OMLb = sb("OMLb", [128, 1024])
        omlf = sb("omlf", [128, 8])
        cwf = sb("cwf", [128, 2 * NFB, 4])
        halo = sb("halo", [128, 2 * NFB, 2])
        vmask = sb("vmask_s", [128, NCH])
        pmask = sb("pmask_s", [128, NT])
        Rst = sb("Rst", [128, RET_H, 512])
        Rbf = sb("Rbf", [128, RET_H, 512], BF16)
        Sst = sb("Sst", [128, HG_H, 128])
        Sbf = sb("Sbf", [128, 4, 4, 128], BF16)
        small = sb("small", [128, 96])

        def spec(cp, full):
            T_ = cp * 128
            L = [("xt", [128, cp, D], F32), ("hT", [128, KC, T_], BF16), ("cosT", [128, T_], F32), ("sinT", [128, T_], F32),
                 ("posi", [128, min(T_, 256)], I32), ("sqj", [128, 1024], F32), ("dgs", [128, cp, 128], F32),
                 ("t1", [128, T_], F32), ("t2", [128, T_], F32), ("kf32", [128, 2, T_], F32),
                 ("logf", [128, 512], F32), ("ktm", [128, 512], F32), ("sgm", [128, 512], F32), ("ecb", [128, 512], F32),
                 ("eb", [128, cp, 512], F32), ("kbf", [128, 2, T_], BF16), ("kzt", [128, cp, 256], BF16), ("vret", [128, cp, 256], BF16),
                 ("khat", [128, cp, 512], BF16), ("vhg", [128, cp, 512], BF16), ("kz", [128, 512], BF16)]
            if full:
                L += [("ogT", [128, KC, T_], BF16), ("mgT", [128, KC, T_], BF16), ("gT", [128, NFB, T_], BF16), ("Gb", [128, D], F32),
                      ("sgr", [128, cp, 256], F32), ("osb", [128, 512], F32), ("osq", [128, 512], F32), ("enb", [128, cp, 512], F32),
                      ("sgate", [128, cp, 512], F32), ("ybuf", [128, 2, T_ + 2], F32), ("ua", [128, T_], F32), ("ub", [128, T_], F32),
                      ("qrot", [128, 2, T_], BF16), ("qhat", [128, 2, T_], BF16), ("inn", [128, 128], BF16), ("qt", [128, 4, T_], BF16),
                      ("kt", [128, 4, T_], BF16), ("amt", [128, 4, 128], BF16), ("qz", [128, 4, 4, 128], BF16)]
            return L

        def nf32(shape, dt_):
            n = int(np.prod(shape[1:]))
            return (n + 1) // 2 if dt_ == BF16 else n

        CPS_ = max([t[0] for t in tiles if t[1] == "s"] + [1])
        CPF_ = max([t[0] for t in tiles if t[1] == "f"] + [1])
        ls_tot = sum(nf32(sh, d_) for _, sh, d_ in spec(CPS_, False))
        lf_tot = sum(nf32(sh, d_) for _, sh, d_ in spec(CPF_, True))
        need = max(ls_tot + 6656, lf_tot)
        AR = sb("arena", [128, need])

        def layout(cp, full):
            L = {}
            off = 0
            for name, shape, dt_ in spec(cp, full):
                n = int(np.prod(shape[1:]))
                w = nf32(shape, dt_)
                v = AR[:, off:off + w]
                if dt_ == BF16:
                    v = v.bitcast(BF16)[:, 0:n]
                elif dt_ == I32:
                    v = v.bitcast(I32)
                if len(shape) == 3:
                    v = v.rearrange("p (a b) -> p a b", a=shape[1])
                elif len(shape) == 4:
                    v = v.rearrange("p (a b c) -> p a b c", a=shape[1], b=shape[2])
                L[name] = v
                off += w
            return L
        LF = layout(CPF_, True)
        LS = layout(CPS_, False)
        xt, Gb, sqj, gT = LF["xt"], LF["Gb"], LF["sqj"], LF["gT"]
        tgA = AR[:, ls_tot:ls_tot + 2048]
        tgB = AR[:, ls_tot + 2048:ls_tot + 4096]
        mrow = AR[0:1, ls_tot + 4096:ls_tot + 6144]
        brow2 = AR[0:1, ls_tot + 6144:ls_tot + 6656]
        sqjb = sqj.bitcast(BF16)
        rowb = Gb
        print("SBUF bytes/partition:", sbytes[0])
        PS = [st.enter_context(nc.psum_tensor("ps%d" % i, [128, 512], F32)) for i in range(8)]
        psn = [0]

        def bank():
            i = psn[0] % 8
            psn[0] += 1
            return i

        def load(dst_ap, src_ap, key, eng="sp", reads=()):
            return S.dma(eng, "ld_" + key, lambda e: e.dma_start(out=dst_ap, in_=src_ap, allow_slow_non_contiguous=True), writes=[key], reads=reads)

        wsn = [0]

        def wload(segs, krows=D):
            i = wsn[0] % 2
            wsn[0] += 1
            ws = WS[i]
            off = 0
            kc = krows // 128
            for si, (w, row0, c0, n) in enumerate(segs):
                name = WNAME[id(w)]
                src = WBF[name][row0:row0 + krows, c0:c0 + n].rearrange("(kc p) n -> p kc n", p=128)
                dst = ws[:, 0:kc, off:off + n]
                rk = sorted(set(["wb_%s_%d" % (name, c // PCB) for c in (c0, c0 + n - 1)]))
                S.dma("sp", "ws%d_%d" % (i, si), (lambda e, dst=dst, src=src: e.dma_start(out=dst, in_=src)), writes=["ws%d_s%d" % (i, si)], reads=rk)
                off += n
            return i

        def wk(i):
            return ["ws%d_s0" % i, "ws%d_s1" % i]

        def mm_group(out_ap, pairs, reads, wkey):
            def f(e):
                r = None
                n = len(pairs)
                for i, (l, rh) in enumerate(pairs):
                    r = e.matmul(out_ap, lhsT=l, rhs=rh, start=(i == 0), stop=(i == n - 1))
                return r
            return S.op("pe", f, reads=reads, writes=[wkey])

        def mm_multi(items, reads, wkey):
            def f(e):
                r = None
                for (o, l, rh) in items:
                    r = e.matmul(o, lhsT=l, rhs=rh, start=True, stop=True)
                return r
            return S.op("pe", f, reads=reads, writes=[wkey])

        def tr_multi(items, reads, wkey):
            def f(e):
                r = None
                for (o, i_) in items:
                    r = e.transpose(o, i_, K["ident"][:])
                return r
            return S.op("pe", f, reads=list(reads) + ["c_ident"], writes=[wkey])

        def act(out, in_, func, reads, writes, **kw):
            return S.op("act", lambda e: e.activation(out, in_, func, **kw), reads=reads, writes=writes)

        def tt(out, a, b, op, reads, writes, eng="dve"):
            return S.op(eng, lambda e: e.tensor_tensor(out, a, b, op), reads=reads, writes=writes)

        def ts(out, a, s1, s2, op0, op1, reads, writes, eng="dve"):
            if op1 is None:
                return S.op(eng, lambda e: e.tensor_scalar(out, a, s1, s2, op0), reads=reads, writes=writes)
            return S.op(eng, lambda e: e.tensor_scalar(out, a, s1, s2, op0, op1), reads=reads, writes=writes)

        def stt(out, a, s, b, op0, op1, reads, writes, eng="dve"):
            return S.op(eng, lambda e: e.scalar_tensor_tensor(out, a, s, b, op0, op1), reads=reads, writes=writes)

        def rsqrt_small(dst, src, scale, eps, reads, wkey):
            n = src.shape[1]
            ts(dst, src, scale, eps, ALU.mult, ALU.add, reads, [wkey])
            S.op("pool", lambda e: e.tensor_tensor(dst, dst, K["neghalf"][:, 0:n], ALU.pow), reads=[wkey, "c_neghalf"], writes=[wkey])

        def store_dbg(name, ap, key):
            if name in dbg_d:
                S.dma("sp", "dbg_" + name, lambda e: e.dma_start(out=dbg_d[name], in_=ap), reads=[key])

        for k in CONST_SHAPES:
            load(K[k][:], cd[k], "c_" + k)
        def precast(name, cb):
            src = WSRC[name]
            ncols = src.shape[1]
            c0 = cb * PCB
            n = min(PCB, ncols - c0)
            key = "wb_%s_%d" % (name, cb)
            S.dma("pool", "pc_" + key, (lambda e, src=src, name=name, c0=c0, n=n: e.dma_start(out=WBF[name][:, c0:c0 + n], in_=src[:, c0:c0 + n])), writes=[key])

        for cb in (1, 2, 5, 6, 0, 3, 4, 7, 8, 9, 10, 11):
            precast("w_in", cb)
        for name in ("w_ret_o", "w_hg_o", "w_out"):
            for cb in range(2):
                precast(name, cb)
        for cb in range(11):
            precast("w_up", cb)
        for cb in range(2):
            precast("w_down", cb)
        load(vmask[:], vmask_d, "vmask")
        load(pmask[:], pmask_d, "pmask")
        S.op("pool", lambda e: e.memset(halo[:], 0.0), writes=["halo"])
        S.op("pool", lambda e: e.memset(Rst[:], 0.0), writes=["Rst"])
        S.op("pool", lambda e: e.memset(Rbf[:], 0.0), writes=["Rbf"])
        S.op("pool", lambda e: e.memset(Sst[:], 0.0), writes=["Sst"])
        S.op("pool", lambda e: e.memset(Sbf[:], 0.0), writes=["Sbf0", "Sbf1", "Sbf2", "Sbf3"])
        cf = small[:, 0:16]
        g1f = small[:, 16:32]
        g2f = small[:, 32:48]
        scf = small[:, 48:64]
        lbf = small[:, 64:80]
        with nc.allow_non_contiguous_dma(reason="tiny one-time strided loads"):
            for j in range(3):
                load(cwf[:, :, j:j + 1], cw_d[j:j + 1, :].rearrange("o (b p) -> p b o", p=128), "cwf")
            load(cwf[:, :, 3:4], cb_d.rearrange("o (b p) -> p b o", p=128), "cwf")
            load(cf, c_d.rearrange("o (k p) -> p (o k)", p=128), "small_c")
            load(g1f, g1_d.rearrange("o (k p) -> p (o k)", p=128), "small_g1")
            load(g2f, g2_d.rearrange("o (k p) -> p (o k)", p=128), "small_g2")
            load(lbf.rearrange("p (l h) -> p l h", l=2), lb_d.rearrange("l (h p) -> p l h", p=128), "small_lb")
        act(scf, cf, AF.Silu, ["small_c"], ["small_sc"])
        sn_ = [0]

        def gemv_block(g, nb):
            col = g * 2048 + nb * 512
            load(brow2, b_ada[0:1, col:col + 512], "brow2")
            pb = bank()
            for kq in range(4):
                buf = (tgA, tgB)[sn_[0] % 2]
                key = "tg%d" % (sn_[0] % 2)
                sn_[0] += 1
                src = w_ada[kq * 512:(kq + 1) * 512, col:col + 512].rearrange("(k p) n -> p k n", p=128)
                load(buf.rearrange("p (k n) -> p k n", k=4), src, key)

                def f(e, kq=kq, pb=pb, buf=buf):
                    r = None
                    for k4 in range(4):
                        kc = kq * 4 + k4
                        r = e.matmul(PS[pb][0:1, :], lhsT=scf[:, kc:kc + 1], rhs=buf[:, k4 * 512:(k4 + 1) * 512],
                                     start=(kc == 0), stop=(kc == 15))
                    return r
                S.op("pe", f, reads=[key, "small_sc"], writes=["ps%d" % pb])
            tt(mrow[:, nb * 512:(nb + 1) * 512], PS[pb][0:1, :], brow2, ALU.add, ["ps%d" % pb, "brow2"], ["mrow"])
            if nb == 3:
                S.dma("sp", "st_mod", lambda e, g=g: e.dma_start(out=mod_d[0:1, g * 2048:(g + 1) * 2048], in_=mrow), reads=["mrow"], writes=["mod_d%d" % g])

        for g in range(2):
            for nb in range(4):
                gemv_block(g, nb)
        gemv_todo = [(g, nb) for g in range(2, 6) for nb in range(4)]
        s1f = small[:, 80:96]
        fmv = lambda g: mod_d[0:1, g * 2048:(g + 1) * 2048].rearrange("o (k p) -> p (o k)", p=128)
        load(AB[:, 1, :], fmv(0), "AB1", reads=["mod_d0"])
        load(s1f, fmv(1), "small_s1", reads=["mod_d1"])
        stt(AB[:, 0, :], s1f, 1.0, g1f, ALU.add, ALU.mult, ["small_s1", "small_g1"], ["AB0"])

        def late_mod():
            while gemv_todo:
                gemv_block(*gemv_todo.pop(0))
            load(AB[:, 3, :], fmv(3), "AB3", reads=["mod_d3"])
            load(s1f, fmv(4), "small_s1", reads=["mod_d4"])
            stt(AB[:, 2, :], s1f, 1.0, g2f, ALU.add, ALU.mult, ["small_s1", "small_g2"], ["AB2"])
        load(tgA[:, 0:1024], lb_d[0:1, :].partition_broadcast(128), "tg0")
        load(tgA[:, 1024:2048], lb_d[1:2, :].partition_broadcast(128), "tg0")
        act(tgA, tgA, AF.Exp, ["tg0"], ["tg0"])
        tt(tgB[:, 0:1024], tgA[:, 0:1024], tgA[:, 1024:2048], ALU.add, ["tg0"], ["tg1"])
        S.op("dve", lambda e: e.reciprocal(tgB[:, 0:1024], tgB[:, 0:1024]), reads=["tg1"], writes=["tg1"])
        tt(LBb[:], tgA[:, 0:1024], tgB[:, 0:1024], ALU.mult, ["tg0", "tg1"], ["LBb"])
        ts(OMLb[:], LBb[:], -1.0, 1.0, ALU.mult, ALU.add, ["LBb"], ["OMLb"])
        act(lbf, lbf, AF.Exp, ["small_lb"], ["small_lb"])
        tt(omlf[:], lbf[:, 0:8], lbf[:, 8:16], ALU.add, ["small_lb"], ["omlf"])
        S.op("dve", lambda e: e.reciprocal(omlf[:], omlf[:]), reads=["omlf"], writes=["omlf"])
        tt(omlf[:], omlf[:], lbf[:, 8:16], ALU.mult, ["omlf", "small_lb"], ["omlf"])

        def make_ops(L):
            xt = L.get('xt')
            hT = L.get('hT')
            cosT = L.get('cosT')
            sinT = L.get('sinT')
            posi = L.get('posi')
            sqj = L.get('sqj')
            dgs = L.get('dgs')
            t1 = L.get('t1')
            t2 = L.get('t2')
            kf32 = L.get('kf32')
            logf = L.get('logf')
            ktm = L.get('ktm')
            sgm = L.get('sgm')
            ecb = L.get('ecb')
            eb = L.get('eb')
            kbf = L.get('kbf')
            kzt = L.get('kzt')
            vret = L.get('vret')
            khat = L.get('khat')
            vhg = L.get('vhg')
            kz = L.get('kz')
            ogT = L.get('ogT')
            mgT = L.get('mgT')
            gT = L.get('gT')
            Gb = L.get('Gb')
            sgr = L.get('sgr')
            osb = L.get('osb')
            osq = L.get('osq')
            enb = L.get('enb')
            sgate = L.get('sgate')
            ybuf = L.get('ybuf')
            ua = L.get('ua')
            ub = L.get('ub')
            qrot = L.get('qrot')
            qhat = L.get('qhat')
            inn = L.get('inn')
            qt = L.get('qt')
            kt = L.get('kt')
            amt = L.get('amt')
            qz = L.get('qz')
            sqjb = sqj.bitcast(BF16)
            sgA = sgB = None
            if enb is not None:
                tq = eb.shape[1] * 512 // 4
                sgA = eb.rearrange("p c n -> p (c n)").rearrange("p (m t) -> p m t", m=4)
                sgB = enb.rearrange("p c n -> p (c n)").rearrange("p (m t) -> p m t", m=4)
            def norm_to_hT(ai, bi, cpt):
                ss = small[:, 0:cpt]
                rs = small[:, 8:8 + cpt]
                for c in range(cpt):
                    act(sqjb[:, 0:2048], xt[:, c, :], AF.Square, ["xt"], ["sqj", "ss"], accum_out=ss[:, c:c + 1])
                rsqrt_small(rs, ss, 1.0 / D, EPS, ["ss"], "rs")
                for c in range(cpt):
                    ts(dgs[:, c, :], K["ident"][:], rs[:, c:c + 1], None, ALU.mult, None, ["c_ident", "rs"], ["dgs%d" % c])
                    for q in range(4):
                        pb = bank()
                        mm_multi([(PS[pb][:, k4 * 128:(k4 + 1) * 128], xt[:, c, (q * 4 + k4) * 128:(q * 4 + k4 + 1) * 128], dgs[:, c, :]) for k4 in range(4)],
                                 ["xt", "dgs%d" % c], "ps%d" % pb)
                        for k4 in range(4):
                            kc = q * 4 + k4
                            act(hT[:, kc, c * 128:(c + 1) * 128], PS[pb][:, k4 * 128:(k4 + 1) * 128], AF.Identity,
                                ["ps%d" % pb, "AB%d" % ai, "AB%d" % bi], ["hT"], scale=AB[:, ai, kc:kc + 1], bias=AB[:, bi, kc:kc + 1])

            def gemm_fm(wi, col0, nmb, actT, akey, kcs, tw, cb):
                for mb in range(nmb):
                    pb = bank()
                    pairs = [(WS[wi][:, kci, col0 + mb * 128: col0 + (mb + 1) * 128], actT[:, kc, 0:tw]) for kci, kc in enumerate(kcs)]
                    mm_group(PS[pb][:, 0:tw], pairs, wk(wi) + [akey], "ps%d" % pb)
                    cb(mb, pb)

            def gemm_tm(wi, col0, ncols, actT, akey, kcs, cpt, cb):
                for c in range(cpt):
                    pb = bank()
                    pairs = [(actT[:, kc, c * 128:(c + 1) * 128], WS[wi][:, kci, col0:col0 + ncols]) for kci, kc in enumerate(kcs)]
                    mm_group(PS[pb][:, 0:ncols], pairs, wk(wi) + [akey], "ps%d" % pb)
                    cb(c, pb)

            def rope_tables(tok0, tw_all):
                for off in range(0, tw_all, 256):
                    rope_piece(tok0 + off, off, min(256, tw_all - off))

            def rope_piece(tok0, off, tw):
                load(posi[:, 0:tw], pos_d[0:1, tok0:tok0 + tw].partition_broadcast(128), "posi")
                ang = sqj[:, 0:tw]
                kf = sqj[:, 256:256 + tw]
                ki = sqj[:, 512:512 + tw].bitcast(I32)
                a2 = sqj[:, 768:768 + tw]
                kk = ["sqj"]
                S.op("dve", lambda e: e.tensor_copy(ang, posi[:, 0:tw]), reads=["posi"], writes=kk)
                ts(ang, ang, K["invf"][:, 0:1], None, ALU.mult, None, kk + ["c_invf"], kk)
                for shift, dst, key in ((0.0, sinT, "sinT"), (float(np.pi / 2), cosT, "cosT")):
                    ts(a2, ang, shift, None, ALU.add, None, kk, kk)
                    ts(kf, a2, float(1.0 / (2 * np.pi)), None, ALU.mult, None, kk, kk)
                    S.op("dve", lambda e: e.tensor_copy(ki, kf), reads=kk, writes=kk)
                    S.op("dve", lambda e: e.tensor_copy(kf, ki), reads=kk, writes=kk)
                    stt(a2, kf, -6.28125, a2, ALU.mult, ALU.add, kk, kk)
                    stt(a2, kf, -float(2 * np.pi - 6.28125), a2, ALU.mult, ALU.add, kk, kk)
                    ts(kf, a2, float(np.pi), None, ALU.is_gt, None, kk, kk)
                    stt(a2, kf, -float(2 * np.pi), a2, ALU.mult, ALU.add, kk, kk)
                    ts(kf, a2, -float(np.pi), None, ALU.is_lt, None, kk, kk)
                    stt(a2, kf, float(2 * np.pi), a2, ALU.mult, ALU.add, kk, kk)
                    ts(a2, a2, 3.1415925, -3.1415925, ALU.min, ALU.max, kk, kk)
                    act(dst[:, off:off + tw], a2, AF.Sin, kk, [key])

            def ret_head(h, ch0, cpt, so):
                tw = cpt * 128
                if not so:
                    act(Rbf[:, h, :], Rst[:, h, :], AF.Copy, ["Rst"], ["Rbf"])
                segs = [(w_in, 0, 1024 + h * 256, 256)]
                if not so:
                    segs = [(w_in, 0, h * 256, 256)] + segs
                wi = wload(segs)
                kcol = 0 if so else 256
                held = {}

                def rope_cb(is_k):
                    def cb(mb, pb):
                        held[mb] = pb
                        if mb != 1:
                            return
                        p1, p2 = PS[held[0]][:, 0:tw], PS[held[1]][:, 0:tw]
                        k12 = ["ps%d" % held[0], "ps%d" % held[1]]
                        for half, (pa, pbb, op) in enumerate(((p1, p2, ALU.subtract), (p2, p1, ALU.add))):
                            tt(t1[:, 0:tw], pa, cosT[:, 0:tw], ALU.mult, k12 + ["cosT"], ["t1"])
                            tt(t2[:, 0:tw], pbb, sinT[:, 0:tw], ALU.mult, k12 + ["sinT"], ["t2"])
                            if is_k:
                                tt(kf32[:, half, 0:tw], t1[:, 0:tw], t2[:, 0:tw], op, ["t1", "t2"], ["kf32"])
                                act(kbf[:, half, 0:tw], kf32[:, half, 0:tw], AF.Copy, ["kf32"], ["kbf"])
                            else:
                                tt(t1[:, 0:tw], t1[:, 0:tw], t2[:, 0:tw], op, ["t1", "t2"], ["t1"])
                                act(qrot[:, half, 0:tw], t1[:, 0:tw], AF.Copy, ["t1"], ["qrot"])
                                tt(qhat[:, half, 0:tw].rearrange("p (c t) -> p c t", c=cpt), t1[:, 0:tw].rearrange("p (c t) -> p c t", c=cpt),
                                   bc_mid(K["xib"][:, h, :], cpt), ALU.mult, ["t1", "c_xib"], ["qhat"])
                    return cb
                if not so:
                    gemm_fm(wi, 0, 2, hT, "hT", range(KC), tw, rope_cb(False))
                gemm_fm(wi, kcol, 2, hT, "hT", range(KC), tw, rope_cb(True))
                for c in range(cpt):
                    pb = bank()
                    tr_multi([(PS[pb][:, k2 * 128:(k2 + 1) * 128], kf32[:, k2, c * 128:(c + 1) * 128]) for k2 in range(2)], ["kf32"], "ps%d" % pb)
                    act(kzt[:, c, :], PS[pb][:, 0:256], AF.Identity, ["ps%d" % pb, "c_zeta"], ["kzt"], scale=K["zeta"][:, h:h + 1])
                wi2 = wload([(w_in, 0, 2048 + h * 256, 256)] + ([] if so else [(w_in, 0, 3072 + h * 256, 256)]))

                def vg_cb(c, pb):
                    gi = ch0 + c
                    act(vret[:, c, :], PS[pb][:, 0:256], AF.Identity, ["ps%d" % pb, "vmask"], ["vret"], scale=vmask[:, gi:gi + 1])
                    if not so:
                        act(sgr[:, c, :], PS[pb][:, 256:512], AF.Silu, ["ps%d" % pb], ["sgr"])
                gemm_tm(wi2, 0, 256 if so else 512, hT, "hT", range(KC), cpt, vg_cb)
                for c in range(cpt):
                    cs = slice(c * 128, (c + 1) * 128)
                    if not so:
                        pbs = bank()
                        mm_group(PS[pbs][:, 0:128], [(kbf[:, k2, cs], qrot[:, k2, cs]) for k2 in range(2)], ["kbf", "qrot"], "ps%d" % pbs)
                        tt(inn[:], PS[pbs][:, 0:128], K["decT"][:, h, :], ALU.mult, ["ps%d" % pbs, "c_decT"], ["inn"])
                        pbo = bank()
                        pairs = [(inn[:], vret[:, c, :])] + [(qhat[:, k2, cs], Rbf[:, h, k2 * 256:(k2 + 1) * 256]) for k2 in range(2)]
                        mm_group(PS[pbo][:, 0:256], pairs, ["inn", "vret", "qhat", "Rbf"], "ps%d" % pbo)
                        st_ = small[:, 16:24]
                        o_ = osb[:, 0:256]
                        act(o_, PS[pbo][:, 0:256], AF.Copy, ["ps%d" % pbo], ["osb", "st0"], accum_out=st_[:, 0:1])
                        act(osq[:, 0:256], o_, AF.Square, ["osb"], ["osq", "st1"], accum_out=st_[:, 1:2])
                        ts(st_[:, 2:3], st_[:, 0:1], 1.0 / 256, None, ALU.mult, None, ["st0"], ["st2"])
                        tt(st_[:, 3:4], st_[:, 2:3], st_[:, 2:3], ALU.mult, ["st2"], ["st3"])
                        stt(st_[:, 4:5], st_[:, 1:2], 1.0 / 256, st_[:, 3:4], ALU.mult, ALU.subtract, ["st1", "st3"], ["st4"])
                        rsqrt_small(st_[:, 5:6], st_[:, 4:5], 1.0, HEPS, ["st4"], "st5")
                        ts(o_, o_, st_[:, 2:3], st_[:, 5:6], ALU.subtract, ALU.mult, ["osb", "st2", "st5"], ["osb"])
                        tt(o_, o_, sgr[:, c, :], ALU.mult, ["osb", "sgr"], ["osb"])
                        pbt = bank()
                        tr_multi([(PS[pbt][:, k2 * 128:(k2 + 1) * 128], osb[:, k2 * 128:(k2 + 1) * 128]) for k2 in range(2)], ["osb"], "ps%d" % pbt)
                        for k2 in range(2):
                            act(ogT[:, h * 2 + k2, cs], PS[pbt][:, k2 * 128:(k2 + 1) * 128], AF.Copy, ["ps%d" % pbt], ["ogT"])
                    pbr = bank()
                    mm_multi([(PS[pbr][:, k2 * 256:(k2 + 1) * 256], kzt[:, c, k2 * 128:(k2 + 1) * 128], vret[:, c, :]) for k2 in range(2)],
                             ["kzt", "vret"], "ps%d" % pbr)
                    stt(Rst[:, h, :], Rst[:, h, :], gchunk[h], PS[pbr][:, 0:512], ALU.mult, ALU.add, ["Rst", "ps%d" % pbr], ["Rst"])
                    if not so:
                        act(Rbf[:, h, :], Rst[:, h, :], AF.Copy, ["Rst"], ["Rbf"])

            def hg_group(g, ch0, cpt, so):
                tw = cpt * 128
                c0 = 4096 + g * 512
                wif = wload([(w_in, 0, c0 + 1024, 512)])

                def hf_cb(c, pb):
                    act(sgm[:], PS[pb][:, 0:512], AF.Sigmoid, ["ps%d" % pb], ["sgm"])
                    tt(sgm[:], sgm[:], OMLb[:, g * 512:(g + 1) * 512], ALU.mult, ["sgm", "OMLb"], ["sgm"])
                    tt(sgm[:], sgm[:], LBb[:, g * 512:(g + 1) * 512], ALU.add, ["sgm", "LBb"], ["sgm"])
                    ts(ktm[:], sgm[:], -1.0, 1.0, ALU.mult, ALU.add, ["sgm"], ["ktm"])
                    act(logf[:], sgm[:], AF.Ln, ["sgm"], ["logf"])
                    pbc = bank()
                    mm_multi([(PS[pbc][:, 0:512], K["trirev"][:], logf[:])], ["c_trirev", "logf"], "ps%d" % pbc)
                    act(ecb[:], PS[pbc][:, 0:512], AF.Exp, ["ps%d" % pbc], ["ecb"])
                    tt(khat[:, c, :], ktm[:], ecb[:], ALU.mult, ["ktm", "ecb"], ["khat"])
                    pbb = bank()
                    mm_multi([(PS[pbb][:, hh * 128:(hh + 1) * 128], logf[:, hh * 128:(hh + 1) * 128], K["triinc"][:]) for hh in range(4)],
                             ["logf", "c_triinc"], "ps%d" % pbb)
                    act(eb[:, c, :], PS[pbb][:, 0:512], AF.Exp, ["ps%d" % pbb], ["eb"])
                    if not so:
                        act(enb[:, c, :], PS[pbb][:, 0:512], AF.Exp, ["ps%d" % pbb], ["enb"], scale=-1.0)
                gemm_tm(wif, 0, 512, hT, "hT", range(KC), cpt, hf_cb)
                if not so:
                    def hfT_cb(mb, pb):
                        act(t1[:, 0:tw], PS[pb][:, 0:tw], AF.Sigmoid, ["ps%d" % pb], ["t1"], scale=-1.0)
                        stt(kt[:, mb, 0:tw].rearrange("p (c t) -> p c t", c=cpt), t1[:, 0:tw].rearrange("p (c t) -> p c t", c=cpt),
                            omlf[:, g * 4 + mb: g * 4 + mb + 1], enb[:, 0:cpt, mb * 128:(mb + 1) * 128], ALU.mult, ALU.mult, ["t1", "omlf", "enb"], ["kt"])
                    gemm_fm(wif, 0, 4, hT, "hT", range(KC), tw, hfT_cb)
                    wiq = wload([(w_in, 0, c0, 512)])

                    def hqT_cb(mb, pb):
                        act(t2[:, 0:tw], PS[pb][:, 0:tw], AF.Silu, ["ps%d" % pb], ["t2"])
                        tt(qt[:, mb, 0:tw].rearrange("p (c t) -> p c t", c=cpt), t2[:, 0:tw].rearrange("p (c t) -> p c t", c=cpt),
                           eb[:, 0:cpt, mb * 128:(mb + 1) * 128], ALU.mult, ["t2", "eb"], ["qt"])
                    gemm_fm(wiq, 0, 4, hT, "hT", range(KC), tw, hqT_cb)
                wiv = wload([(w_in, 0, c0 + 2048, 512)])

                def v_cb(c, pb):
                    gi = ch0 + c
                    act(vhg[:, c, :], PS[pb][:, 0:512], AF.Identity, ["ps%d" % pb, "vmask"], ["vhg"], scale=vmask[:, gi:gi + 1])
                gemm_tm(wiv, 0, 512, hT, "hT", range(KC), cpt, v_cb)
                if not so:
                    wig = wload([(w_in, 0, c0 + 3072, 512)])

                    def g_cb(c, pb):
                        act(sgate[:, c, :], PS[pb][:, 0:512], AF.Silu, ["ps%d" % pb], ["sgate"])
                    gemm_tm(wig, 0, 512, hT, "hT", range(KC), cpt, g_cb)
                if not so:
                    act(Sbf[:, 0, :, :], Sst[:, g * 4:g * 4 + 4, :], AF.Copy, ["Sst"], ["Sbf0"])
                for c in range(cpt):
                    cs = slice(c * 128, (c + 1) * 128)
                    if not so:
                        pba = bank()
                        mm_multi([(PS[pba][:, hh * 128:(hh + 1) * 128], kt[:, hh, cs], qt[:, hh, cs]) for hh in range(4)], ["kt", "qt"], "ps%d" % pba)
                        tt(amt[:], PS[pba][:, 0:512].rearrange("p (h t) -> p h t", h=4), bc_mid(K["triinc"][:], 4), ALU.mult,
                           ["ps%d" % pba, "c_triinc"], ["amt"])
                        for j in range(4):
                            tt(qz[:, j, :, :], qt[:, :, cs], bc_mid(K["colmask"][:, j, :], 4), ALU.mult, ["qt", "c_colmask"], ["qz"], eng="pool")

                    def sub_update(j):
                        act(kz[:], khat[:, c, :], AF.Identity, ["khat", "c_rowmask"], ["kz"], scale=K["rowmask"][:, j:j + 1])
                        pbd = bank()
                        mm_multi([(PS[pbd][:, hh * 128:(hh + 1) * 128], kz[:, hh * 128:(hh + 1) * 128], vhg[:, c, hh * 128:(hh + 1) * 128]) for hh in range(4)],
                                 ["kz", "vhg"], "ps%d" % pbd)
                        for hh in range(4):
                            hd = g * 4 + hh
                            col = hh * 128 + j * 32 + 31
                            stt(Sst[:, hd, :], Sst[:, hd, :], eb[:, c, col:col + 1], PS[pbd][:, hh * 128:(hh + 1) * 128], ALU.mult, ALU.add,
                                ["Sst", "eb", "ps%d" % pbd], ["Sst"])
                        jn = (j + 1) % 4
                        if not so:
                            act(Sbf[:, jn, :, :], Sst[:, g * 4:g * 4 + 4, :], AF.Copy, ["Sst"], ["Sbf%d" % jn])
                    for j in range(3):
                        sub_update(j)
                    if not so:
                        pbo = bank()

                        def f(e, pbo=pbo, c=c):
                            r = None
                            for hh in range(4):
                                o = PS[pbo][:, hh * 128:(hh + 1) * 128]
                                e.matmul(o, lhsT=amt[:, hh, :], rhs=vhg[:, c, hh * 128:(hh + 1) * 128], start=True, stop=False)
                                for j in range(4):
                                    r = e.matmul(o, lhsT=qz[:, j, hh, :], rhs=Sbf[:, j, hh, :], start=False, stop=(j == 3))
                            return r
                        S.op("pe", f, reads=["amt", "vhg", "qz", "Sbf0", "Sbf1", "Sbf2", "Sbf3"], writes=["ps%d" % pbo])
                        ss_ = small[:, 24:28]
                        rs_ = small[:, 28:32]
                        for hh in range(4):
                            act(osq[:, hh * 128:(hh + 1) * 128], PS[pbo][:, hh * 128:(hh + 1) * 128], AF.Square, ["ps%d" % pbo], ["osq", "hss"], accum_out=ss_[:, hh:hh + 1])
                        rsqrt_small(rs_, ss_, 1.0 / 128, HEPS, ["hss"], "hrs")
                        for hh in range(4):
                            stt(osb[:, hh * 128:(hh + 1) * 128], PS[pbo][:, hh * 128:(hh + 1) * 128], rs_[:, hh:hh + 1], sgate[:, c, hh * 128:(hh + 1) * 128],
                                ALU.mult, ALU.mult, ["ps%d" % pbo, "hrs", "sgate"], ["osb"])
                        pbt = bank()
                        tr_multi([(PS[pbt][:, hh * 128:(hh + 1) * 128], osb[:, hh * 128:(hh + 1) * 128]) for hh in range(4)], ["osb"], "ps%d" % pbt)
                        for hh in range(4):
                            act(ogT[:, 8 + g * 4 + hh, cs], PS[pbt][:, hh * 128:(hh + 1) * 128], AF.Copy, ["ps%d" % pbt], ["ogT"])
                    sub_update(3)


            def merge_and_out2(cpt):
                tw = cpt * 128
                for jb in range(4):
                    wa = wload([(w_in, 0, 8192 + jb * 512, 512)])
                    wb = wload([(w_in, 0, 10240 + jb * 512, 512)])
                    sig = []
                    for mb in range(4):
                        kcb = jb * 4 + mb
                        pga, pgb = bank(), bank()
                        mm_group(PS[pga][:, 0:tw], [(WS[wa][:, kc, mb * 128:(mb + 1) * 128], hT[:, kc, 0:tw]) for kc in range(KC)], wk(wa) + ["hT"], "ps%d" % pga)
                        mm_group(PS[pgb][:, 0:tw], [(WS[wb][:, kc, mb * 128:(mb + 1) * 128], hT[:, kc, 0:tw]) for kc in range(KC)], wk(wb) + ["hT"], "ps%d" % pgb)
                        act(sgA[:, mb, 0:tw], PS[pga][:, 0:tw], AF.Sigmoid, ["ps%d" % pga], ["eb"])
                        act(sgB[:, mb, 0:tw], PS[pgb][:, 0:tw], AF.Sigmoid, ["ps%d" % pgb], ["enb"])
                    wo = wload([(w_ro, 0, jb * 512, 512)], krows=1024)
                    srcb = WBF["w_hg_o"][:, jb * 512:(jb + 1) * 512].rearrange("(kc p) n -> p kc n", p=128)
                    S.dma("sp", "ws%d_1" % wo, (lambda e, wo=wo, srcb=srcb: e.dma_start(out=WS[wo][:, 8:16, :], in_=srcb)), writes=["ws%d_s1" % wo],
                          reads=["wb_w_hg_o_%d" % ((jb * 512) // PCB)])
                    for mb in range(4):
                        kcb = jb * 4 + mb
                        pya, pyb = bank(), bank()
                        mm_group(PS[pya][:, 0:tw], [(WS[wo][:, kc, mb * 128:(mb + 1) * 128], ogT[:, kc, 0:tw]) for kc in range(8)], wk(wo) + ["ogT"], "ps%d" % pya)
                        mm_group(PS[pyb][:, 0:tw], [(WS[wo][:, 8 + kc, mb * 128:(mb + 1) * 128], ogT[:, 8 + kc, 0:tw]) for kc in range(8)], wk(wo) + ["ogT"], "ps%d" % pyb)
                        tt(ua[:, 0:tw], sgA[:, mb, 0:tw], PS[pya][:, 0:tw], ALU.mult, ["eb", "ps%d" % pya], ["ua"])
                        tt(ub[:, 0:tw], sgB[:, mb, 0:tw], PS[pyb][:, 0:tw], ALU.mult, ["enb", "ps%d" % pyb], ["ub"])
                        tt(mgT[:, kcb, 0:tw], ua[:, 0:tw], ub[:, 0:tw], ALU.add, ["ua", "ub"], ["mgT"], eng="pool")
                load(Gb[:], mod_d[0:1, 2 * D:3 * D].partition_broadcast(128), "Gb", reads=["mod_d2"])
                for jb in range(4):
                    wo = wload([(w_out, 0, jb * 512, 512)])

                    gemm_tm(wo, 0, 512, mgT, "mgT", range(KC), cpt, lambda c, pb, jb=jb: resid_cb(c, pb, jb))

            def resid_cb(c, pb, jb):
                tt(osq[:, 0:512], PS[pb][:, 0:512], Gb[:, jb * 512:(jb + 1) * 512], ALU.mult, ["ps%d" % pb, "Gb"], ["osq"])
                tt(xt[:, c, jb * 512:(jb + 1) * 512], xt[:, c, jb * 512:(jb + 1) * 512], osq[:, 0:512], ALU.add, ["xt", "osq"], ["xt"], eng="pool")

            def ffn(ti, cpt, store=True):
                tw = cpt * 128
                for fb in range(11):
                    wa = wload([(w_up, 0, fb * 512, 512)])
                    wb = wload([(w_up, 0, DFF + fb * 512, 512)])
                    for mb in range(4):
                        blk = fb * 4 + mb
                        res = {}
                        for which, wi in ((0, wa), (1, wb)):
                            ch = which * NFB + blk
                            pb = bank()
                            mm_group(PS[pb][:, 0:tw], [(WS[wi][:, kc, mb * 128:(mb + 1) * 128], hT[:, kc, 0:tw]) for kc in range(KC)], wk(wi) + ["hT"], "ps%d" % pb)
                            if not store:
                                act(halo[:, ch, :], PS[pb][:, tw - 2:tw], AF.Copy, ["ps%d" % pb], ["halo"])
                                continue
                            yb = ybuf[:, which, :]
                            yk = "ybuf%d" % which
                            act(yb[:, 2:2 + tw], PS[pb][:, 0:tw], AF.Copy, ["ps%d" % pb], [yk])
                            ts(yb[:, 0:2], halo[:, ch, :], pmask[:, ti:ti + 1], None, ALU.mult, None, ["halo", "pmask"], [yk], eng="pool")
                            u = ua if which == 0 else ub
                            uk = "ua" if which == 0 else "ub"
                            act(u[:, 0:tw], yb[:, 2:2 + tw], AF.Identity, [yk, "cwf"], [uk], scale=cwf[:, ch, 2:3], bias=cwf[:, ch, 3:4])
                            stt(u[:, 0:tw], yb[:, 1:1 + tw], cwf[:, ch, 1:2], u[:, 0:tw], ALU.mult, ALU.add, [yk, "cwf", uk], [uk])
                            stt(u[:, 0:tw], yb[:, 0:tw], cwf[:, ch, 0:1], u[:, 0:tw], ALU.mult, ALU.add, [yk, "cwf", uk], [uk])
                            S.op("pool", lambda e, yb=yb, ch=ch: e.tensor_copy(halo[:, ch, :], yb[:, tw:tw + 2]), reads=[yk], writes=["halo"])
                        if store:
                            act(t1[:, 0:tw], ua[:, 0:tw], AF.Silu, ["ua"], ["t1"])
                            tt(gT[:, blk, 0:tw], t1[:, 0:tw], ub[:, 0:tw], ALU.mult, ["t1", "ub"], ["gT"])
                if ti == 0:
                    store_dbg("gT", gT[:], "gT")
                if not store:
                    return
                load(Gb[:], mod_d[0:1, 5 * D:6 * D].partition_broadcast(128), "Gb", reads=["mod_d5"])
                for jb in range(4):
                    pbs = [bank() for _ in range(cpt)]
                    parts = [(0, 16), (16, 16), (32, 12)]
                    for pi, (f0, nf) in enumerate(parts):
                        wd = wload([(w_dn, f0 * 128, jb * 512, 512)], krows=nf * 128)
                        for c in range(cpt):
                            def f(e, c=c, wd=wd, f0=f0, nf=nf, pi=pi, pbs=pbs):
                                r = None
                                for k in range(nf):
                                    r = e.matmul(PS[pbs[c]][:, 0:512], lhsT=gT[:, f0 + k, c * 128:(c + 1) * 128], rhs=WS[wd][:, k, 0:512],
                                                 start=(pi == 0 and k == 0), stop=(pi == 2 and k == nf - 1))
                                return r
                            S.op("pe", f, reads=wk(wd) + ["gT"], writes=["ps%d" % pbs[c]])
                    for c in range(cpt):
                        resid_cb(c, pbs[c], jb)

            def final_store(cpt, row0):
                if row0 == 0:
                    store_dbg("x2", xt[:], "xt")
                load(Gb[:], gf_d.partition_broadcast(128), "Gb")
                ss = small[:, 0:cpt]
                rs = small[:, 8:8 + cpt]
                for c in range(cpt):
                    act(sqjb[:, 0:2048], xt[:, c, :], AF.Square, ["xt"], ["sqj", "ss"], accum_out=ss[:, c:c + 1])
                rsqrt_small(rs, ss, 1.0 / D, EPS, ["ss"], "rs")
                for c in range(cpt):
                    stt(xt[:, c, :], xt[:, c, :], rs[:, c:c + 1], Gb[:], ALU.mult, ALU.mult, ["xt", "rs", "Gb"], ["xt"])
                    S.dma("sp", "st_out", lambda e, c=c: e.dma_start(out=out_d[row0 + c * 128: row0 + (c + 1) * 128, :], in_=xt[:, c, :]), reads=["xt"])


            return dict(norm_to_hT=norm_to_hT, rope_tables=rope_tables, ret_head=ret_head, hg_group=hg_group,
                        merge_and_out2=merge_and_out2, ffn=ffn, final_store=final_store, xt=xt)

        OPS = {"s": make_ops(LS), "f": make_ops(LF)}
        print("SBUF bytes/partition (final):", sbytes[0])
        ch0 = 0
        prev_mode = None
        S.barrier()
        for ti, (cpt, mode, row0) in enumerate(tiles):
            so = mode == "s"
            if mode == "f" and prev_mode != "f":
                late_mod()
            if prev_mode is not None and prev_mode != mode:
                S.barrier()
            prev_mode = mode
            O = OPS[mode]
            for c in range(cpt):
                load(O["xt"][:, c, :], x_d[(ch0 + c) * 128:(ch0 + c + 1) * 128, :], "xt")
            O["norm_to_hT"](0, 1, cpt)
            O["rope_tables"](ch0 * 128, cpt * 128)
            if ti == 0 and "cos" in dbg_d:
                store_dbg("cos", (LS if so else LF)["cosT"], "cosT")
                store_dbg("sin", (LS if so else LF)["sinT"], "sinT")
                store_dbg("sqj", (LS if so else LF)["sqj"], "sqj")
            for h in range(RET_H):
                O["ret_head"](h, ch0, cpt, so)
                if ti == 0 and h == 0 and "kf32" in dbg_d:
                    store_dbg("kf32", (LS if so else LF)["kf32"], "kf32")
                    store_dbg("R0", Rst[:, 0, :], "Rst")
            for g in range(2):
                O["hg_group"](g, ch0, cpt, so)
            if not so and row0 == 0 and "ogT" in dbg_d:
                store_dbg("ogT", LF["ogT"], "ogT")
            if not so:
                O["merge_and_out2"](cpt)
                O["norm_to_hT"](2, 3, cpt)
                O["ffn"](ti, cpt, row0 is not None)
                if row0 is not None:
                    O["final_store"](cpt, row0)
            if so and gemv_todo:
                gemv_block(*gemv_todo.pop(0))
            ch0 += cpt

        fin = [(k, v) for k, v in S.dma_cnt.items() if k.startswith("st_") or k.startswith("dbg_")]
        S.wait_all("sp", fin)
        with nc.Block() as block:
            S.replay(block)
    return nc


CPS = 4


def make_tiles(nstate_chunks, pre_chunks, own_chunks):
    tiles = []
    n = nstate_chunks
    while n > 0:
        c = min(CPS, n)
        tiles.append((c, "s", None))
        n -= c
    n = pre_chunks
    while n > 0:
        c = min(CPTMAX, n)
        tiles.append((c, "f", None))
        n -= c
    row = 0
    n = own_chunks
    while n > 0:
        c = min(CPTMAX, n)
        tiles.append((c, "f", row))
        row += c * 128
        n -= c
    return tiles


_CACHE = {}


def kernel(x, c, positions, w_ada, b_ada, g_norm1, w_in, w_ret_o, w_hg_o, w_out,
           hg_lb, g_norm2, w_up, conv_w, conv_b, w_down, g_final):
    own = SEQ // NCORES // 128
    pre = 1
    nstate = (NCORES - 1) * own - pre
    tiles = make_tiles(nstate, pre, own)
    if "nc" not in _CACHE:
        _CACHE["nc"] = build(tiles)
    nc = _CACHE["nc"]
    nch = nstate + pre + own
    ntok = nch * 128
    xs = np.asarray(x, np.float32).reshape(SEQ, D)
    ps = np.asarray(positions, np.int32).reshape(SEQ)
    HC = host_consts()
    shared = {
        "c": np.asarray(c, np.float32).reshape(1, D),
        "w_ada": np.ascontiguousarray(np.asarray(w_ada, np.float32).reshape(D, 6 * D)),
        "b_ada": np.asarray(b_ada, np.float32).reshape(1, 6 * D),
        "g_norm1": np.asarray(g_norm1, np.float32).reshape(1, D),
        "w_in": np.ascontiguousarray(np.asarray(w_in, np.float32).reshape(D, IN_COLS)),
        "w_ret_o": np.ascontiguousarray(np.asarray(w_ret_o, np.float32).reshape(1024, D)),
        "w_hg_o": np.ascontiguousarray(np.asarray(w_hg_o, np.float32).reshape(1024, D)),
        "w_out": np.ascontiguousarray(np.asarray(w_out, np.float32).reshape(D, D)),
        "hg_lb": np.asarray(hg_lb, np.float32).reshape(2, 1024),
        "g_norm2": np.asarray(g_norm2, np.float32).reshape(1, D),
        "w_up": np.ascontiguousarray(np.asarray(w_up, np.float32).reshape(D, 2 * DFF)),
        "conv_w": np.asarray(conv_w, np.float32).reshape(3, 2 * DFF),
        "conv_b": np.asarray(conv_b, np.float32).reshape(1, 2 * DFF),
        "w_down": np.ascontiguousarray(np.asarray(w_down, np.float32).reshape(DFF, D)),
        "g_final": np.asarray(g_final, np.float32).reshape(1, D),
    }
    for k in CONST_SHAPES:
        shared["k_" + k] = np.ascontiguousarray(HC[k]).reshape(CONST_SHAPES[k])
    in_maps = []
    for core in range(NCORES):
        nreal = (core + 1) * own * 128
        xc = np.zeros((ntok, D), np.float32)
        xc[ntok - nreal:] = xs[:nreal]
        pc = np.zeros((1, ntok), np.int32)
        pc[0, ntok - nreal:] = ps[:nreal]
        tokmask = np.zeros(ntok, np.float32)
        tokmask[ntok - nreal:] = 1.0
        vm = np.ascontiguousarray(tokmask.reshape(nch, 128).T)
        pm = np.zeros((128, len(tiles)), np.float32)
        chs = 0
        for ti, (cpt, mode, row0) in enumerate(tiles):
            prev_real = 1.0 if (chs * 128 - 1) >= (ntok - nreal) else 0.0
            pm[:, ti] = prev_real
            chs += cpt
        m = dict(shared)
        m.update({"x": xc, "positions": pc, "vmask": vm, "pmask": pm})
        in_maps.append(m)
    res = run_bass_kernel_spmd(nc, in_maps, core_ids=list(range(NCORES)))
    outs = [np.asarray(r["out"])[: own * 128] for r in res.results]
    return np.concatenate(outs, axis=0).reshape(1, SEQ, D).astype(np.float32)
```

```python
import contextlib
import numpy as np
import concourse.bass as bass
import concourse.mybir as mybir
from concourse.bass_utils import run_bass_kernel_spmd

F32 = mybir.dt.float32
BF16 = mybir.dt.bfloat16
I32 = mybir.dt.int32
ALU = mybir.AluOpType
AF = mybir.ActivationFunctionType

D = 2048
KC = 16
SEQ = 16384
NCORES = 8
CPTMAX = 2
TMAX = 128 * CPTMAX
DFF = 5632
NFB = DFF // 128
RET_H, HG_H = 4, 8
IN_COLS = 12288
EPS = 1e-6
HEPS = 1e-5


class Sched:
    ENGS = ("pe", "act", "dve", "pool", "sp")

    def __init__(self, nc, stack):
        self.nc = nc
        self.stack = stack
        self.q = {e: [] for e in self.ENGS}
        self.cnt = {e: 0 for e in self.ENGS}
        self.known = {e: {} for e in self.ENGS}
        self.last_w = {}
        self.readers = {}
        self.sems = {}
        self.dma_cnt = {}
        for e in ("pe", "act", "dve", "pool"):
            self.sems[e] = stack.enter_context(nc.semaphore("c_" + e))

    def _sem(self, key):
        if key not in self.sems:
            self.sems[key] = self.stack.enter_context(self.nc.semaphore("d_" + str(key)))
            self.dma_cnt[key] = 0
        return self.sems[key]

    def _deps(self, eng, reads, writes, extra=()):
        need = {}

        def add(t):
            if t is None:
                return
            sk, v = t
            if need.get(sk, 0) < v:
                need[sk] = v
        for k in reads:
            add(self.last_w.get(k))
        for k in writes:
            add(self.last_w.get(k))
            for t in self.readers.get(k, ()):
                add(t)
        for t in extra:
            add(t)
        waits = []
        for sk, v in need.items():
            if sk == eng and eng == "pe":
                continue
            if self.known[eng].get(sk, 0) >= v:
                continue
            self.known[eng][sk] = v
            waits.append((sk, v))
        return waits

    def _commit(self, tok, reads, writes):
        for k in reads:
            self.readers.setdefault(k, []).append(tok)
        for k in writes:
            self.last_w[k] = tok
            self.readers[k] = []

    def op(self, eng, fn, reads=(), writes=()):
        waits = self._deps(eng, reads, writes)
        self.cnt[eng] += 1
        tok = (eng, self.cnt[eng])
        self.q[eng].append((fn, waits, (eng, 1)))
        self._commit(tok, reads, writes)
        return tok

    def dma(self, eng, slot, fn, reads=(), writes=()):
        self._sem(slot)
        waits = self._deps(eng, reads, writes)
        self.dma_cnt[slot] += 16
        tok = (slot, self.dma_cnt[slot])
        self.q[eng].append((fn, waits, (slot, 16)))
        self._commit(tok, reads, writes)
        return tok

    def barrier(self):
        toks = [(e, self.cnt[e]) for e in ("pe", "act", "dve", "pool") if self.cnt[e] > 0]
        toks += [(k, v) for k, v in self.dma_cnt.items() if v > 0]
        for e in self.ENGS:
            self.wait_all(e, toks)

    def wait_all(self, eng, toks):
        waits = self._deps(eng, (), (), toks)
        self.q[eng].append((None, waits, None))

    def replay(self, block):
        engmap = {"pe": block.tensor, "act": block.scalar, "dve": block.vector,
                  "pool": block.gpsimd, "sp": block.sync}
        sems = self.sems
        for e in self.ENGS:
            def body(engine, items=self.q[e]):
                for fn, waits, inc in items:
                    for sk, v in waits:
                        engine.wait_ge(sems[sk], v)
                    if fn is None:
                        continue
                    ins = fn(engine)
                    ins.then_inc(sems[inc[0]], inc[1])
            engmap[e](body)


def host_consts():
    c = {}
    c["ident"] = np.eye(128, dtype=np.float32)
    idx = np.arange(128, dtype=np.float64)
    gam = 1.0 - np.exp2(-5.0 - np.arange(RET_H, dtype=np.float64))
    lg = np.log(gam)
    diff = idx[None, :] - idx[:, None]
    dec = np.where(diff >= 0, np.exp(np.maximum(diff, 0)[None] * lg[:, None, None]), 0.0)
    c["decT"] = np.ascontiguousarray((dec * 256 ** -0.5).transpose(1, 0, 2)).astype(np.float32)
    xi = np.exp((idx[None, :] + 1.0) * lg[:, None])
    c["xib"] = np.ascontiguousarray(np.broadcast_to(xi[None], (128, RET_H, 128))).astype(np.float32)
    zeta = np.exp((127.0 - idx[None, :]) * lg[:, None]) * 256 ** -0.5
    c["zeta"] = np.ascontiguousarray(zeta.T).astype(np.float32)
    c["gchunk"] = np.exp(128 * lg)
    sub = (np.arange(128) // 32)
    same = sub[:, None] == sub[None, :]
    s = np.arange(128)
    c["triinc"] = (same & (s[:, None] <= s[None, :])).astype(np.float32)
    c["trirev"] = (same & (s[:, None] > s[None, :])).astype(np.float32)
    c["rowmask"] = (sub[:, None] == np.arange(4)[None, :]).astype(np.float32)
    c["colmask"] = np.ascontiguousarray(np.broadcast_to((sub[None, :] == np.arange(4)[:, None])[None], (128, 4, 128))).astype(np.float32)
    c["invf"] = (10000.0 ** (-np.arange(0, 256, 2, dtype=np.float32) / 256)).astype(np.float32)[:, None]
    c["neghalf"] = np.full((128, 8), -0.5, np.float32)
    return c


CONST_SHAPES = {"ident": [128, 128], "decT": [128, 4, 128], "xib": [128, 4, 128], "zeta": [128, 4],
                "triinc": [128, 128], "trirev": [128, 128], "rowmask": [128, 4], "colmask": [128, 4, 128],
                "invf": [128, 1], "neghalf": [128, 8]}


def bc_mid(a, n):
    return bass.AP(a.tensor, a.offset, [list(a.ap[0]), [0, n], list(a.ap[-1])])


def build(tiles, dbg=None):
    NCH = sum(t[0] for t in tiles)
    NTOK = NCH * 128
    NT = len(tiles)
    NOUT = sum(t[0] * 128 for t in tiles if t[1] == "f" and t[2] is not None)
    HC = host_consts()
    gchunk = [float(v) for v in HC["gchunk"]]
    nc = bass.Bass("TRN2", target_bir_lowering=False)
    din = lambda n, s, dt=F32: nc.dram_tensor(n, s, dt, kind="ExternalInput").ap()
    x_d = din("x", [NTOK, D])
    pos_d = din("positions", [1, NTOK], I32)
    vmask_d = din("vmask", [128, NCH])
    pmask_d = din("pmask", [128, NT])
    c_d = din("c", [1, D])
    w_ada = din("w_ada", [D, 6 * D])
    b_ada = din("b_ada", [1, 6 * D])
    g1_d = din("g_norm1", [1, D])
    w_in = din("w_in", [D, IN_COLS])
    w_ro = din("w_ret_o", [1024, D])
    w_ho = din("w_hg_o", [1024, D])
    w_out = din("w_out", [D, D])
    lb_d = din("hg_lb", [2, 1024])
    g2_d = din("g_norm2", [1, D])
    w_up = din("w_up", [D, 2 * DFF])
    cw_d = din("conv_w", [3, 2 * DFF])
    cb_d = din("conv_b", [1, 2 * DFF])
    w_dn = din("w_down", [DFF, D])
    gf_d = din("g_final", [1, D])
    cd = {k: din("k_" + k, s) for k, s in CONST_SHAPES.items()}
    out_d = nc.dram_tensor("out", [max(NOUT, 128), D], F32, kind="ExternalOutput").ap()
    mod_d = nc.dram_tensor("mod_scratch", [1, 6 * D], F32).ap()
    WSRC = {"w_in": w_in, "w_ret_o": w_ro, "w_hg_o": w_ho, "w_out": w_out, "w_up": w_up, "w_down": w_dn}
    WBF = {k: nc.dram_tensor("bf_" + k, list(v.shape), BF16).ap() for k, v in WSRC.items()}
    WNAME = {id(v): k for k, v in WSRC.items()}
    PCB = 1024
    dbg_d = {}
    if dbg:
        for k, (s, dt_) in dbg.items():
            dbg_d[k] = nc.dram_tensor("dbg_" + k, s, dt_, kind="ExternalOutput").ap()

    with contextlib.ExitStack() as st:
        S = Sched(nc, st)
        sbytes = [0]

        def sb(name, shape, dt=F32):
            n = int(np.prod(shape[1:])) * (4 if dt in (F32, I32) else 2)
            sbytes[0] += n
            return st.enter_context(nc.sbuf_tensor(name, shape, dt))
        K = {k: sb("c_" + k, s) for k, s in CONST_SHAPES.items()}
        WS = [sb("ws%d" % i, [128, KC, 512], BF16) for i in range(2)]
        AB = sb("AB", [128, 4, KC])
        LBb = sb("LBb", [128, 1024])
        OMLb = sb("OMLb", [128, 1024])
        omlf = sb("omlf", [128, 8])
        cwf = sb("cwf", [128, 2 * NFB, 4])
        halo = sb("halo", [128, 2 * NFB, 2])
        vmask = sb("vmask_s", [128, NCH])
        pmask = sb("pmask_s", [128, NT])
        Rst = sb("Rst", [128, RET_H, 512])
        Rbf = sb("Rbf", [128, RET_H, 512], BF16)
        Sst = sb("Sst", [128, HG_H, 128])
        Sbf = sb("Sbf", [128, 4, 4, 128], BF16)
        small = sb("small", [128, 96])

        def spec(cp, full):
            T_ = cp * 128
            L = [("xt", [128, cp, D], F32), ("hT", [128, KC, T_], BF16), ("cosT", [128, T_], F32), ("sinT", [128, T_], F32),
                 ("posi", [128, min(T_, 256)], I32), ("sqj", [128, 1024], F32), ("dgs", [128, cp, 128], F32),
                 ("t1", [128, T_], F32), ("t2", [128, T_], F32), ("kf32", [128, 2, T_], F32),
                 ("logf", [128, 512], F32), ("ktm", [128, 512], F32), ("sgm", [128, 512], F32), ("ecb", [128, 512], F32),
                 ("eb", [128, cp, 512], F32), ("kbf", [128, 2, T_], BF16), ("kzt", [128, cp, 256], BF16), ("vret", [128, cp, 256], BF16),
                 ("khat", [128, cp, 512], BF16), ("vhg", [128, cp, 512], BF16), ("kz", [128, 512], BF16)]
            if full:
                L += [("ogT", [128, KC, T_], BF16), ("mgT", [128, KC, T_], BF16), ("gT", [128, NFB, T_], BF16), ("Gb", [128, D], F32),
                      ("sgr", [128, cp, 256], F32), ("osb", [128, 512], F32), ("osq", [128, 512], F32), ("enb", [128, cp, 512], F32),
                      ("sgate", [128, cp, 512], F32), ("ybuf", [128, 2, T_ + 2], F32), ("ua", [128, T_], F32), ("ub", [128, T_], F32),
                      ("qrot", [128, 2, T_], BF16), ("qhat", [128, 2, T_], BF16), ("inn", [128, 128], BF16), ("qt", [128, 4, T_], BF16),
                      ("kt", [128, 4, T_], BF16), ("amt", [128, 4, 128], BF16), ("qz", [128, 4, 4, 128], BF16)]
            return L

        def nf32(shape, dt_):
            n = int(np.prod(shape[1:]))
            return (n + 1) // 2 if dt_ == BF16 else n

        CPS_ = max([t[0] for t in tiles if t[1] == "s"] + [1])
        CPF_ = max([t[0] for t in tiles if t[1] == "f"] + [1])
        ls_tot = sum(nf32(sh, d_) for _, sh, d_ in spec(CPS_, False))
        lf_tot = sum(nf32(sh, d_) for _, sh, d_ in spec(CPF_, True))
        need = max(ls_tot + 6656, lf_tot)
        AR = sb("arena", [128, need])

        def layout(cp, full):
            L = {}
            off = 0
            for name, shape, dt_ in spec(cp, full):
                n = int(np.prod(shape[1:]))
                w = nf32(shape, dt_)
                v = AR[:, off:off + w]
                if dt_ == BF16:
                    v = v.bitcast(BF16)[:, 0:n]
                elif dt_ == I32:
                    v = v.bitcast(I32)
                if len(shape) == 3:
                    v = v.rearrange("p (a b) -> p a b", a=shape[1])
                elif len(shape) == 4:
                    v = v.rearrange("p (a b c) -> p a b c", a=shape[1], b=shape[2])
                L[name] = v
                off += w
            return L
        LF = layout(CPF_, True)
        LS = layout(CPS_, False)
        xt, Gb, sqj, gT = LF["xt"], LF["Gb"], LF["sqj"], LF["gT"]
        tgA = AR[:, ls_tot:ls_tot + 2048]
        tgB = AR[:, ls_tot + 2048:ls_tot + 4096]
        mrow = AR[0:1, ls_tot + 4096:ls_tot + 6144]
        brow2 = AR[0:1, ls_tot + 6144:ls_tot + 6656]
        sqjb = sqj.bitcast(BF16)
        rowb = Gb
        print("SBUF bytes/partition:", sbytes[0])
        PS = [st.enter_context(nc.psum_tensor("ps%d" % i, [128, 512], F32)) for i in range(8)]
        psn = [0]

        def bank():
            i = psn[0] % 8
            psn[0] += 1
            return i

        def load(dst_ap, src_ap, key, eng="sp", reads=()):
            return S.dma(eng, "ld_" + key, lambda e: e.dma_start(out=dst_ap, in_=src_ap, allow_slow_non_contiguous=True), writes=[key], reads=reads)

        wsn = [0]

        def wload(segs, krows=D):
            i = wsn[0] % 2
            wsn[0] += 1
            ws = WS[i]
            off = 0
            kc = krows // 128
            for si, (w, row0, c0, n) in enumerate(segs):
                name = WNAME[id(w)]
                src = WBF[name][row0:row0 + krows, c0:c0 + n].rearrange("(kc p) n -> p kc n", p=128)
                dst = ws[:, 0:kc, off:off + n]
                rk = sorted(set(["wb_%s_%d" % (name, c // PCB) for c in (c0, c0 + n - 1)]))
                S.dma("sp", "ws%d_%d" % (i, si), (lambda e, dst=dst, src=src: e.dma_start(out=dst, in_=src)), writes=["ws%d_s%d" % (i, si)], reads=rk)
                off += n
            return i

        def wk(i):
            return ["ws%d_s0" % i, "ws%d_s1" % i]

        def mm_group(out_ap, pairs, reads, wkey):
            def f(e):
                r = None
                n = len(pairs)
                for i, (l, rh) in enumerate(pairs):
                    r = e.matmul(out_ap, lhsT=l, rhs=rh, start=(i == 0), stop=(i == n - 1))
                return r
            return S.op("pe", f, reads=reads, writes=[wkey])

        def mm_multi(items, reads, wkey):
            def f(e):
                r = None
                for (o, l, rh) in items:
                    r = e.matmul(o, lhsT=l, rhs=rh, start=True, stop=True)
                return r
            return S.op("pe", f, reads=reads, writes=[wkey])

        def tr_multi(items, reads, wkey):
            def f(e):
                r = None
                for (o, i_) in items:
                    r = e.transpose(o, i_, K["ident"][:])
                return r
            return S.op("pe", f, reads=list(reads) + ["c_ident"], writes=[wkey])

        def act(out, in_, func, reads, writes, **kw):
            return S.op("act", lambda e: e.activation(out, in_, func, **kw), reads=reads, writes=writes)

        def tt(out, a, b, op, reads, writes, eng="dve"):
            return S.op(eng, lambda e: e.tensor_tensor(out, a, b, op), reads=reads, writes=writes)

        def ts(out, a, s1, s2, op0, op1, reads, writes, eng="dve"):
            if op1 is None:
                return S.op(eng, lambda e: e.tensor_scalar(out, a, s1, s2, op0), reads=reads, writes=writes)
            return S.op(eng, lambda e: e.tensor_scalar(out, a, s1, s2, op0, op1), reads=reads, writes=writes)

        def stt(out, a, s, b, op0, op1, reads, writes, eng="dve"):
            return S.op(eng, lambda e: e.scalar_tensor_tensor(out, a, s, b, op0, op1), reads=reads, writes=writes)

        def rsqrt_small(dst, src, scale, eps, reads, wkey):
            n = src.shape[1]
            ts(dst, src, scale, eps, ALU.mult, ALU.add, reads, [wkey])
            S.op("pool", lambda e: e.tensor_tensor(dst, dst, K["neghalf"][:, 0:n], ALU.pow), reads=[wkey, "c_neghalf"], writes=[wkey])

        def store_dbg(name, ap, key):
            if name in dbg_d:
                S.dma("sp", "dbg_" + name, lambda e: e.dma_start(out=dbg_d[name], in_=ap), reads=[key])

        for k in CONST_SHAPES:
            load(K[k][:], cd[k], "c_" + k)
        def precast(name, cb):
            src = WSRC[name]
            ncols = src.shape[1]
            c0 = cb * PCB
            n = min(PCB, ncols - c0)
            key = "wb_%s_%d" % (name, cb)
            S.dma("pool", "pc_" + key, (lambda e, src=src, name=name, c0=c0, n=n: e.dma_start(out=WBF[name][:, c0:c0 + n], in_=src[:, c0:c0 + n])), writes=[key])

        precast_todo = [("w_in", cb) for cb in (0, 3, 4, 7, 8, 9, 10, 11)]
        for name in ("w_ret_o", "w_hg_o", "w_out"):
            precast_todo += [(name, cb) for cb in range(2)]
        precast_todo += [("w_up", cb) for cb in range(11)] + [("w_down", cb) for cb in range(2)]
        for cb in (1, 2, 5, 6):
            precast("w_in", cb)
        if not any(t[1] == "s" for t in tiles):
            while precast_todo:
                precast(*precast_todo.pop(0))
        load(vmask[:], vmask_d, "vmask")
        load(pmask[:], pmask_d, "pmask")
        S.op("pool", lambda e: e.memset(halo[:], 0.0), writes=["halo"])
        S.op("pool", lambda e: e.memset(Rst[:], 0.0), writes=["Rst"])
        S.op("pool", lambda e: e.memset(Rbf[:], 0.0), writes=["Rbf"])
        S.op("pool", lambda e: e.memset(Sst[:], 0.0), writes=["Sst"])
        S.op("pool", lambda e: e.memset(Sbf[:], 0.0), writes=["Sbf0", "Sbf1", "Sbf2", "Sbf3"])
        cf = small[:, 0:16]
        g1f = small[:, 16:32]
        g2f = small[:, 32:48]
        scf = small[:, 48:64]
        lbf = small[:, 64:80]
        with nc.allow_non_contiguous_dma(reason="tiny one-time strided loads"):
            for j in range(3):
                load(cwf[:, :, j:j + 1], cw_d[j:j + 1, :].rearrange("o (b p) -> p b o", p=128), "cwf")
            load(cwf[:, :, 3:4], cb_d.rearrange("o (b p) -> p b o", p=128), "cwf")
            load(cf, c_d.rearrange("o (k p) -> p (o k)", p=128), "small_c")
            load(g1f, g1_d.rearrange("o (k p) -> p (o k)", p=128), "small_g1")
            load(g2f, g2_d.rearrange("o (k p) -> p (o k)", p=128), "small_g2")
            load(lbf.rearrange("p (l h) -> p l h", l=2), lb_d.rearrange("l (h p) -> p l h", p=128), "small_lb")
        act(scf, cf, AF.Silu, ["small_c"], ["small_sc"])
        sn_ = [0]

        def gemv_block(g, nb):
            col = g * 2048 + nb * 512
            load(brow2, b_ada[0:1, col:col + 512], "brow2")
            pb = bank()
            for kq in range(4):
                buf = (tgA, tgB)[sn_[0] % 2]
                key = "tg%d" % (sn_[0] % 2)
                sn_[0] += 1
                src = w_ada[kq * 512:(kq + 1) * 512, col:col + 512].rearrange("(k p) n -> p k n", p=128)
                load(buf.rearrange("p (k n) -> p k n", k=4), src, key)

                def f(e, kq=kq, pb=pb, buf=buf):
                    r = None
                    for k4 in range(4):
                        kc = kq * 4 + k4
                        r = e.matmul(PS[pb][0:1, :], lhsT=scf[:, kc:kc + 1], rhs=buf[:, k4 * 512:(k4 + 1) * 512],
                                     start=(kc == 0), stop=(kc == 15))
                    return r
                S.op("pe", f, reads=[key, "small_sc"], writes=["ps%d" % pb])
            tt(mrow[:, nb * 512:(nb + 1) * 512], PS[pb][0:1, :], brow2, ALU.add, ["ps%d" % pb, "brow2"], ["mrow"])
            if nb == 3:
                S.dma("sp", "st_mod", lambda e, g=g: e.dma_start(out=mod_d[0:1, g * 2048:(g + 1) * 2048], in_=mrow), reads=["mrow"], writes=["mod_d%d" % g])

        for g in range(2):
            for nb in range(4):
                gemv_block(g, nb)
        gemv_todo = [(g, nb) for g in range(2, 6) for nb in range(4)]
        s1f = small[:, 80:96]
        fmv = lambda g: mod_d[0:1, g * 2048:(g + 1) * 2048].rearrange("o (k p) -> p (o k)", p=128)
        load(AB[:, 1, :], fmv(0), "AB1", reads=["mod_d0"])
        load(s1f, fmv(1), "small_s1", reads=["mod_d1"])
        stt(AB[:, 0, :], s1f, 1.0, g1f, ALU.add, ALU.mult, ["small_s1", "small_g1"], ["AB0"])

        def late_mod():
            while gemv_todo:
                gemv_block(*gemv_todo.pop(0))
            load(AB[:, 3, :], fmv(3), "AB3", reads=["mod_d3"])
            load(s1f, fmv(4), "small_s1", reads=["mod_d4"])
            stt(AB[:, 2, :], s1f, 1.0, g2f, ALU.add, ALU.mult, ["small_s1", "small_g2"], ["AB2"])
        load(tgA[:, 0:1024], lb_d[0:1, :].partition_broadcast(128), "tg0")
        load(tgA[:, 1024:2048], lb_d[1:2, :].partition_broadcast(128), "tg0")
        act(tgA, tgA, AF.Exp, ["tg0"], ["tg0"])
        tt(tgB[:, 0:1024], tgA[:, 0:1024], tgA[:, 1024:2048], ALU.add, ["tg0"], ["tg1"])
        S.op("dve", lambda e: e.reciprocal(tgB[:, 0:1024], tgB[:, 0:1024]), reads=["tg1"], writes=["tg1"])
        tt(LBb[:], tgA[:, 0:1024], tgB[:, 0:1024], ALU.mult, ["tg0", "tg1"], ["LBb"])
        ts(OMLb[:], LBb[:], -1.0, 1.0, ALU.mult, ALU.add, ["LBb"], ["OMLb"])
        act(lbf, lbf, AF.Exp, ["small_lb"], ["small_lb"])
        tt(omlf[:], lbf[:, 0:8], lbf[:, 8:16], ALU.add, ["small_lb"], ["omlf"])
        S.op("dve", lambda e: e.reciprocal(omlf[:], omlf[:]), reads=["omlf"], writes=["omlf"])
        tt(omlf[:], omlf[:], lbf[:, 8:16], ALU.mult, ["omlf", "small_lb"], ["omlf"])

        def make_ops(L):
            xt = L.get('xt')
            hT = L.get('hT')
            cosT = L.get('cosT')
            sinT = L.get('sinT')
            posi = L.get('posi')
            sqj = L.get('sqj')
            dgs = L.get('dgs')
            t1 = L.get('t1')
            t2 = L.get('t2')
            kf32 = L.get('kf32')
            logf = L.get('logf')
            ktm = L.get('ktm')
            sgm = L.get('sgm')
            ecb = L.get('ecb')
            eb = L.get('eb')
            kbf = L.get('kbf')
            kzt = L.get('kzt')
            vret = L.get('vret')
            khat = L.get('khat')
            vhg = L.get('vhg')
            kz = L.get('kz')
            ogT = L.get('ogT')
            mgT = L.get('mgT')
            gT = L.get('gT')
            Gb = L.get('Gb')
            sgr = L.get('sgr')
            osb = L.get('osb')
            osq = L.get('osq')
            enb = L.get('enb')
            sgate = L.get('sgate')
            ybuf = L.get('ybuf')
            ua = L.get('ua')
            ub = L.get('ub')
            qrot = L.get('qrot')
            qhat = L.get('qhat')
            inn = L.get('inn')
            qt = L.get('qt')
            kt = L.get('kt')
            amt = L.get('amt')
            qz = L.get('qz')
            sqjb = sqj.bitcast(BF16)
            sgA = sgB = None
            if enb is not None:
                tq = eb.shape[1] * 512 // 4
                sgA = eb.rearrange("p c n -> p (c n)").rearrange("p (m t) -> p m t", m=4)
                sgB = enb.rearrange("p c n -> p (c n)").rearrange("p (m t) -> p m t", m=4)
            def norm_to_hT(ai, bi, cpt):
                ss = small[:, 0:cpt]
                rs = small[:, 8:8 + cpt]
                for c in range(cpt):
                    act(sqjb[:, 0:2048], xt[:, c, :], AF.Square, ["xt"], ["sqj", "ss"], accum_out=ss[:, c:c + 1])
                rsqrt_small(rs, ss, 1.0 / D, EPS, ["ss"], "rs")
                for c in range(cpt):
                    ts(dgs[:, c, :], K["ident"][:], rs[:, c:c + 1], None, ALU.mult, None, ["c_ident", "rs"], ["dgs%d" % c])
                    for q in range(4):
                        pb = bank()
                        mm_multi([(PS[pb][:, k4 * 128:(k4 + 1) * 128], xt[:, c, (q * 4 + k4) * 128:(q * 4 + k4 + 1) * 128], dgs[:, c, :]) for k4 in range(4)],
                                 ["xt", "dgs%d" % c], "ps%d" % pb)
                        for k4 in range(4):
                            kc = q * 4 + k4
                            act(hT[:, kc, c * 128:(c + 1) * 128], PS[pb][:, k4 * 128:(k4 + 1) * 128], AF.Identity,
                                ["ps%d" % pb, "AB%d" % ai, "AB%d" % bi], ["hT"], scale=AB[:, ai, kc:kc + 1], bias=AB[:, bi, kc:kc + 1])

            def gemm_fm(wi, col0, nmb, actT, akey, kcs, tw, cb):
                for mb in range(nmb):
                    pb = bank()
                    pairs = [(WS[wi][:, kci, col0 + mb * 128: col0 + (mb + 1) * 128], actT[:, kc, 0:tw]) for kci, kc in enumerate(kcs)]
                    mm_group(PS[pb][:, 0:tw], pairs, wk(wi) + [akey], "ps%d" % pb)
                    cb(mb, pb)

            def gemm_tm(wi, col0, ncols, actT, akey, kcs, cpt, cb):
                for c in range(cpt):
                    pb = bank()
                    pairs = [(actT[:, kc, c * 128:(c + 1) * 128], WS[wi][:, kci, col0:col0 + ncols]) for kci, kc in enumerate(kcs)]
                    mm_group(PS[pb][:, 0:ncols], pairs, wk(wi) + [akey], "ps%d" % pb)
                    cb(c, pb)

            def rope_tables(tok0, tw_all):
                for off in range(0, tw_all, 256):
                    rope_piece(tok0 + off, off, min(256, tw_all - off))

            def rope_piece(tok0, off, tw):
                load(posi[:, 0:tw], pos_d[0:1, tok0:tok0 + tw].partition_broadcast(128), "posi")
                ang = sqj[:, 0:tw]
                kf = sqj[:, 256:256 + tw]
                ki = sqj[:, 512:512 + tw].bitcast(I32)
                a2 = sqj[:, 768:768 + tw]
                kk = ["sqj"]
                S.op("dve", lambda e: e.tensor_copy(ang, posi[:, 0:tw]), reads=["posi"], writes=kk)
                ts(ang, ang, K["invf"][:, 0:1], None, ALU.mult, None, kk + ["c_invf"], kk)
                for shift, dst, key in ((0.0, sinT, "sinT"), (float(np.pi / 2), cosT, "cosT")):
                    ts(a2, ang, shift, None, ALU.add, None, kk, kk)
                    ts(kf, a2, float(1.0 / (2 * np.pi)), None, ALU.mult, None, kk, kk)
                    S.op("dve", lambda e: e.tensor_copy(ki, kf), reads=kk, writes=kk)
                    S.op("dve", lambda e: e.tensor_copy(kf, ki), reads=kk, writes=kk)
                    stt(a2, kf, -6.28125, a2, ALU.mult, ALU.add, kk, kk)
                    stt(a2, kf, -float(2 * np.pi - 6.28125), a2, ALU.mult, ALU.add, kk, kk)
                    ts(kf, a2, float(np.pi), None, ALU.is_gt, None, kk, kk)
                    stt(a2, kf, -float(2 * np.pi), a2, ALU.mult, ALU.add, kk, kk)
                    ts(kf, a2, -float(np.pi), None, ALU.is_lt, None, kk, kk)
                    stt(a2, kf, float(2 * np.pi), a2, ALU.mult, ALU.add, kk, kk)
                    ts(a2, a2, 3.1415925, -3.1415925, ALU.min, ALU.max, kk, kk)
                    act(dst[:, off:off + tw], a2, AF.Sin, kk, [key])

            def ret_head(h, ch0, cpt, so):
                tw = cpt * 128
                if not so:
                    act(Rbf[:, h, :], Rst[:, h, :], AF.Copy, ["Rst"], ["Rbf"])
                segs = [(w_in, 0, 1024 + h * 256, 256)]
                if not so:
                    segs = [(w_in, 0, h * 256, 256)] + segs
                wi = wload(segs)
                kcol = 0 if so else 256
                held = {}

                def rope_cb(is_k):
                    def cb(mb, pb):
                        held[mb] = pb
                        if mb != 1:
                            return
                        p1, p2 = PS[held[0]][:, 0:tw], PS[held[1]][:, 0:tw]
                        k12 = ["ps%d" % held[0], "ps%d" % held[1]]
                        for half, (pa, pbb, op) in enumerate(((p1, p2, ALU.subtract), (p2, p1, ALU.add))):
                            tt(t1[:, 0:tw], pa, cosT[:, 0:tw], ALU.mult, k12 + ["cosT"], ["t1"])
                            tt(t2[:, 0:tw], pbb, sinT[:, 0:tw], ALU.mult, k12 + ["sinT"], ["t2"])
                            if is_k:
                                tt(kf32[:, half, 0:tw], t1[:, 0:tw], t2[:, 0:tw], op, ["t1", "t2"], ["kf32"])
                                act(kbf[:, half, 0:tw], kf32[:, half, 0:tw], AF.Copy, ["kf32"], ["kbf"])
                            else:
                                tt(t1[:, 0:tw], t1[:, 0:tw], t2[:, 0:tw], op, ["t1", "t2"], ["t1"])
                                act(qrot[:, half, 0:tw], t1[:, 0:tw], AF.Copy, ["t1"], ["qrot"])
                                tt(qhat[:, half, 0:tw].rearrange("p (c t) -> p c t", c=cpt), t1[:, 0:tw].rearrange("p (c t) -> p c t", c=cpt),
                                   bc_mid(K["xib"][:, h, :], cpt), ALU.mult, ["t1", "c_xib"], ["qhat"])
                    return cb
                if not so:
                    gemm_fm(wi, 0, 2, hT, "hT", range(KC), tw, rope_cb(False))
                gemm_fm(wi, kcol, 2, hT, "hT", range(KC), tw, rope_cb(True))
                for c in range(cpt):
                    pb = bank()
                    tr_multi([(PS[pb][:, k2 * 128:(k2 + 1) * 128], kf32[:, k2, c * 128:(c + 1) * 128]) for k2 in range(2)], ["kf32"], "ps%d" % pb)
                    act(kzt[:, c, :], PS[pb][:, 0:256], AF.Identity, ["ps%d" % pb, "c_zeta"], ["kzt"], scale=K["zeta"][:, h:h + 1])
                wi2 = wload([(w_in, 0, 2048 + h * 256, 256)] + ([] if so else [(w_in, 0, 3072 + h * 256, 256)]))

                def vg_cb(c, pb):
                    gi = ch0 + c
                    act(vret[:, c, :], PS[pb][:, 0:256], AF.Identity, ["ps%d" % pb, "vmask"], ["vret"], scale=vmask[:, gi:gi + 1])
                    if not so:
                        act(sgr[:, c, :], PS[pb][:, 256:512], AF.Silu, ["ps%d" % pb], ["sgr"])
                gemm_tm(wi2, 0, 256 if so else 512, hT, "hT", range(KC), cpt, vg_cb)
                for c in range(cpt):
                    cs = slice(c * 128, (c + 1) * 128)
                    if not so:
                        pbs = bank()
                        mm_group(PS[pbs][:, 0:128], [(kbf[:, k2, cs], qrot[:, k2, cs]) for k2 in range(2)], ["kbf", "qrot"], "ps%d" % pbs)
                        tt(inn[:], PS[pbs][:, 0:128], K["decT"][:, h, :], ALU.mult, ["ps%d" % pbs, "c_decT"], ["inn"])
                        pbo = bank()
                        pairs = [(inn[:], vret[:, c, :])] + [(qhat[:, k2, cs], Rbf[:, h, k2 * 256:(k2 + 1) * 256]) for k2 in range(2)]
                        mm_group(PS[pbo][:, 0:256], pairs, ["inn", "vret", "qhat", "Rbf"], "ps%d" % pbo)
                        st_ = small[:, 16:24]
                        o_ = osb[:, 0:256]
                        act(o_, PS[pbo][:, 0:256], AF.Copy, ["ps%d" % pbo], ["osb", "st0"], accum_out=st_[:, 0:1])
                        act(osq[:, 0:256], o_, AF.Square, ["osb"], ["osq", "st1"], accum_out=st_[:, 1:2])
                        ts(st_[:, 2:3], st_[:, 0:1], 1.0 / 256, None, ALU.mult, None, ["st0"], ["st2"])
                        tt(st_[:, 3:4], st_[:, 2:3], st_[:, 2:3], ALU.mult, ["st2"], ["st3"])
                        stt(st_[:, 4:5], st_[:, 1:2], 1.0 / 256, st_[:, 3:4], ALU.mult, ALU.subtract, ["st1", "st3"], ["st4"])
                        rsqrt_small(st_[:, 5:6], st_[:, 4:5], 1.0, HEPS, ["st4"], "st5")
                        ts(o_, o_, st_[:, 2:3], st_[:, 5:6], ALU.subtract, ALU.mult, ["osb", "st2", "st5"], ["osb"])
                        tt(o_, o_, sgr[:, c, :], ALU.mult, ["osb", "sgr"], ["osb"])
                        pbt = bank()
                        tr_multi([(PS[pbt][:, k2 * 128:(k2 + 1) * 128], osb[:, k2 * 128:(k2 + 1) * 128]) for k2 in range(2)], ["osb"], "ps%d" % pbt)
                        for k2 in range(2):
                            act(ogT[:, h * 2 + k2, cs], PS[pbt][:, k2 * 128:(k2 + 1) * 128], AF.Copy, ["ps%d" % pbt], ["ogT"])
                    pbr = bank()
                    mm_multi([(PS[pbr][:, k2 * 256:(k2 + 1) * 256], kzt[:, c, k2 * 128:(k2 + 1) * 128], vret[:, c, :]) for k2 in range(2)],
                             ["kzt", "vret"], "ps%d" % pbr)
                    stt(Rst[:, h, :], Rst[:, h, :], gchunk[h], PS[pbr][:, 0:512], ALU.mult, ALU.add, ["Rst", "ps%d" % pbr], ["Rst"])
                    if not so:
                        act(Rbf[:, h, :], Rst[:, h, :], AF.Copy, ["Rst"], ["Rbf"])

            def hg_group(g, ch0, cpt, so):
                tw = cpt * 128
                c0 = 4096 + g * 512
                wif = wload([(w_in, 0, c0 + 1024, 512)])

                def hf_cb(c, pb):
                    act(sgm[:], PS[pb][:, 0:512], AF.Sigmoid, ["ps%d" % pb], ["sgm"])
                    tt(sgm[:], sgm[:], OMLb[:, g * 512:(g + 1) * 512], ALU.mult, ["sgm", "OMLb"], ["sgm"])
                    tt(sgm[:], sgm[:], LBb[:, g * 512:(g + 1) * 512], ALU.add, ["sgm", "LBb"], ["sgm"])
                    ts(ktm[:], sgm[:], -1.0, 1.0, ALU.mult, ALU.add, ["sgm"], ["ktm"])
                    act(logf[:], sgm[:], AF.Ln, ["sgm"], ["logf"])
                    pbc = bank()
                    mm_multi([(PS[pbc][:, 0:512], K["trirev"][:], logf[:])], ["c_trirev", "logf"], "ps%d" % pbc)
                    act(ecb[:], PS[pbc][:, 0:512], AF.Exp, ["ps%d" % pbc], ["ecb"])
                    tt(khat[:, c, :], ktm[:], ecb[:], ALU.mult, ["ktm", "ecb"], ["khat"])
                    pbb = bank()
                    mm_multi([(PS[pbb][:, hh * 128:(hh + 1) * 128], logf[:, hh * 128:(hh + 1) * 128], K["triinc"][:]) for hh in range(4)],
                             ["logf", "c_triinc"], "ps%d" % pbb)
                    act(eb[:, c, :], PS[pbb][:, 0:512], AF.Exp, ["ps%d" % pbb], ["eb"])
                    if not so:
                        act(enb[:, c, :], PS[pbb][:, 0:512], AF.Exp, ["ps%d" % pbb], ["enb"], scale=-1.0)
                gemm_tm(wif, 0, 512, hT, "hT", range(KC), cpt, hf_cb)
                if not so:
                    def hfT_cb(mb, pb):
                        act(t1[:, 0:tw], PS[pb][:, 0:tw], AF.Sigmoid, ["ps%d" % pb], ["t1"], scale=-1.0)
                        stt(kt[:, mb, 0:tw].rearrange("p (c t) -> p c t", c=cpt), t1[:, 0:tw].rearrange("p (c t) -> p c t", c=cpt),
                            omlf[:, g * 4 + mb: g * 4 + mb + 1], enb[:, 0:cpt, mb * 128:(mb + 1) * 128], ALU.mult, ALU.mult, ["t1", "omlf", "enb"], ["kt"])
                    gemm_fm(wif, 0, 4, hT, "hT", range(KC), tw, hfT_cb)
                    wiq = wload([(w_in, 0, c0, 512)])

                    def hqT_cb(mb, pb):
                        act(t2[:, 0:tw], PS[pb][:, 0:tw], AF.Silu, ["ps%d" % pb], ["t2"])
                        tt(qt[:, mb, 0:tw].rearrange("p (c t) -> p c t", c=cpt), t2[:, 0:tw].rearrange("p (c t) -> p c t", c=cpt),
                           eb[:, 0:cpt, mb * 128:(mb + 1) * 128], ALU.mult, ["t2", "eb"], ["qt"])
                    gemm_fm(wiq, 0, 4, hT, "hT", range(KC), tw, hqT_cb)
                wiv = wload([(w_in, 0, c0 + 2048, 512)])

                def v_cb(c, pb):
                    gi = ch0 + c
                    act(vhg[:, c, :], PS[pb][:, 0:512], AF.Identity, ["ps%d" % pb, "vmask"], ["vhg"], scale=vmask[:, gi:gi + 1])
                gemm_tm(wiv, 0, 512, hT, "hT", range(KC), cpt, v_cb)
                if not so:
                    wig = wload([(w_in, 0, c0 + 3072, 512)])

                    def g_cb(c, pb):
                        act(sgate[:, c, :], PS[pb][:, 0:512], AF.Silu, ["ps%d" % pb], ["sgate"])
                    gemm_tm(wig, 0, 512, hT, "hT", range(KC), cpt, g_cb)
                if not so:
                    act(Sbf[:, 0, :, :], Sst[:, g * 4:g * 4 + 4, :], AF.Copy, ["Sst"], ["Sbf0"])
                for c in range(cpt):
                    cs = slice(c * 128, (c + 1) * 128)
                    if not so:
                        pba = bank()
                        mm_multi([(PS[pba][:, hh * 128:(hh + 1) * 128], kt[:, hh, cs], qt[:, hh, cs]) for hh in range(4)], ["kt", "qt"], "ps%d" % pba)
                        tt(amt[:], PS[pba][:, 0:512].rearrange("p (h t) -> p h t", h=4), bc_mid(K["triinc"][:], 4), ALU.mult,
                           ["ps%d" % pba, "c_triinc"], ["amt"])
                        for j in range(4):
                            tt(qz[:, j, :, :], qt[:, :, cs], bc_mid(K["colmask"][:, j, :], 4), ALU.mult, ["qt", "c_colmask"], ["qz"], eng="pool")

                    def sub_update(j):
                        act(kz[:], khat[:, c, :], AF.Identity, ["khat", "c_rowmask"], ["kz"], scale=K["rowmask"][:, j:j + 1])
                        pbd = bank()
                        mm_multi([(PS[pbd][:, hh * 128:(hh + 1) * 128], kz[:, hh * 128:(hh + 1) * 128], vhg[:, c, hh * 128:(hh + 1) * 128]) for hh in range(4)],
                                 ["kz", "vhg"], "ps%d" % pbd)
                        for hh in range(4):
                            hd = g * 4 + hh
                            col = hh * 128 + j * 32 + 31
                            stt(Sst[:, hd, :], Sst[:, hd, :], eb[:, c, col:col + 1], PS[pbd][:, hh * 128:(hh + 1) * 128], ALU.mult, ALU.add,
                                ["Sst", "eb", "ps%d" % pbd], ["Sst"])
                        jn = (j + 1) % 4
                        if not so:
                            act(Sbf[:, jn, :, :], Sst[:, g * 4:g * 4 + 4, :], AF.Copy, ["Sst"], ["Sbf%d" % jn])
                    for j in range(3):
                        sub_update(j)
                    if not so:
                        pbo = bank()

                        def f(e, pbo=pbo, c=c):
                            r = None
                            for hh in range(4):
                                o = PS[pbo][:, hh * 128:(hh + 1) * 128]
                                e.matmul(o, lhsT=amt[:, hh, :], rhs=vhg[:, c, hh * 128:(hh + 1) * 128], start=True, stop=False)
                                for j in range(4):
                                    r = e.matmul(o, lhsT=qz[:, j, hh, :], rhs=Sbf[:, j, hh, :], start=False, stop=(j == 3))
                            return r
                        S.op("pe", f, reads=["amt", "vhg", "qz", "Sbf0", "Sbf1", "Sbf2", "Sbf3"], writes=["ps%d" % pbo])
                        ss_ = small[:, 24:28]
                        rs_ = small[:, 28:32]
                        for hh in range(4):
                            act(osq[:, hh * 128:(hh + 1) * 128], PS[pbo][:, hh * 128:(hh + 1) * 128], AF.Square, ["ps%d" % pbo], ["osq", "hss"], accum_out=ss_[:, hh:hh + 1])
                        rsqrt_small(rs_, ss_, 1.0 / 128, HEPS, ["hss"], "hrs")
                        for hh in range(4):
                            stt(osb[:, hh * 128:(hh + 1) * 128], PS[pbo][:, hh * 128:(hh + 1) * 128], rs_[:, hh:hh + 1], sgate[:, c, hh * 128:(hh + 1) * 128],
                                ALU.mult, ALU.mult, ["ps%d" % pbo, "hrs", "sgate"], ["osb"])
                        pbt = bank()
                        tr_multi([(PS[pbt][:, hh * 128:(hh + 1) * 128], osb[:, hh * 128:(hh + 1) * 128]) for hh in range(4)], ["osb"], "ps%d" % pbt)
                        for hh in range(4):
                            act(ogT[:, 8 + g * 4 + hh, cs], PS[pbt][:, hh * 128:(hh + 1) * 128], AF.Copy, ["ps%d" % pbt], ["ogT"])
                    sub_update(3)


            def merge_and_out2(cpt):
                tw = cpt * 128
                for jb in range(4):
                    wa = wload([(w_in, 0, 8192 + jb * 512, 512)])
                    wb = wload([(w_in, 0, 10240 + jb * 512, 512)])
                    sig = []
                    for mb in range(4):
                        kcb = jb * 4 + mb
                        pga, pgb = bank(), bank()
                        mm_group(PS[pga][:, 0:tw], [(WS[wa][:, kc, mb * 128:(mb + 1) * 128], hT[:, kc, 0:tw]) for kc in range(KC)], wk(wa) + ["hT"], "ps%d" % pga)
                        mm_group(PS[pgb][:, 0:tw], [(WS[wb][:, kc, mb * 128:(mb + 1) * 128], hT[:, kc, 0:tw]) for kc in range(KC)], wk(wb) + ["hT"], "ps%d" % pgb)
                        act(sgA[:, mb, 0:tw], PS[pga][:, 0:tw], AF.Sigmoid, ["ps%d" % pga], ["eb"])
                        act(sgB[:, mb, 0:tw], PS[pgb][:, 0:tw], AF.Sigmoid, ["ps%d" % pgb], ["enb"])
                    wo = wload([(w_ro, 0, jb * 512, 512)], krows=1024)
                    srcb = WBF["w_hg_o"][:, jb * 512:(jb + 1) * 512].rearrange("(kc p) n -> p kc n", p=128)
                    S.dma("sp", "ws%d_1" % wo, (lambda e, wo=wo, srcb=srcb: e.dma_start(out=WS[wo][:, 8:16, :], in_=srcb)), writes=["ws%d_s1" % wo],
                          reads=["wb_w_hg_o_%d" % ((jb * 512) // PCB)])
                    for mb in range(4):
                        kcb = jb * 4 + mb
                        pya, pyb = bank(), bank()
                        mm_group(PS[pya][:, 0:tw], [(WS[wo][:, kc, mb * 128:(mb + 1) * 128], ogT[:, kc, 0:tw]) for kc in range(8)], wk(wo) + ["ogT"], "ps%d" % pya)
                        mm_group(PS[pyb][:, 0:tw], [(WS[wo][:, 8 + kc, mb * 128:(mb + 1) * 128], ogT[:, 8 + kc, 0:tw]) for kc in range(8)], wk(wo) + ["ogT"], "ps%d" % pyb)
                        tt(ua[:, 0:tw], sgA[:, mb, 0:tw], PS[pya][:, 0:tw], ALU.mult, ["eb", "ps%d" % pya], ["ua"])
                        tt(ub[:, 0:tw], sgB[:, mb, 0:tw], PS[pyb][:, 0:tw], ALU.mult, ["enb", "ps%d" % pyb], ["ub"])
                        tt(mgT[:, kcb, 0:tw], ua[:, 0:tw], ub[:, 0:tw], ALU.add, ["ua", "ub"], ["mgT"], eng="pool")
                load(Gb[:], mod_d[0:1, 2 * D:3 * D].partition_broadcast(128), "Gb", reads=["mod_d2"])
                for jb in range(4):
                    wo = wload([(w_out, 0, jb * 512, 512)])

                    gemm_tm(wo, 0, 512, mgT, "mgT", range(KC), cpt, lambda c, pb, jb=jb: resid_cb(c, pb, jb))

            def resid_cb(c, pb, jb):
                tt(osq[:, 0:512], PS[pb][:, 0:512], Gb[:, jb * 512:(jb + 1) * 512], ALU.mult, ["ps%d" % pb, "Gb"], ["osq"])
                tt(xt[:, c, jb * 512:(jb + 1) * 512], xt[:, c, jb * 512:(jb + 1) * 512], osq[:, 0:512], ALU.add, ["xt", "osq"], ["xt"], eng="pool")

            def ffn(ti, cpt, store=True):
                tw = cpt * 128
                for fb in range(11):
                    wa = wload([(w_up, 0, fb * 512, 512)])
                    wb = wload([(w_up, 0, DFF + fb * 512, 512)])
                    for mb in range(4):
                        blk = fb * 4 + mb
                        res = {}
                        for which, wi in ((0, wa), (1, wb)):
                            ch = which * NFB + blk
                            pb = bank()
                            mm_group(PS[pb][:, 0:tw], [(WS[wi][:, kc, mb * 128:(mb + 1) * 128], hT[:, kc, 0:tw]) for kc in range(KC)], wk(wi) + ["hT"], "ps%d" % pb)
                            if not store:
                                act(halo[:, ch, :], PS[pb][:, tw - 2:tw], AF.Copy, ["ps%d" % pb], ["halo"])
                                continue
                            yb = ybuf[:, which, :]
                            yk = "ybuf%d" % which
                            act(yb[:, 2:2 + tw], PS[pb][:, 0:tw], AF.Copy, ["ps%d" % pb], [yk])
                            ts(yb[:, 0:2], halo[:, ch, :], pmask[:, ti:ti + 1], None, ALU.mult, None, ["halo", "pmask"], [yk], eng="pool")
                            u = ua if which == 0 else ub
                            uk = "ua" if which == 0 else "ub"
                            act(u[:, 0:tw], yb[:, 2:2 + tw], AF.Identity, [yk, "cwf"], [uk], scale=cwf[:, ch, 2:3], bias=cwf[:, ch, 3:4])
                            stt(u[:, 0:tw], yb[:, 1:1 + tw], cwf[:, ch, 1:2], u[:, 0:tw], ALU.mult, ALU.add, [yk, "cwf", uk], [uk])
                            stt(u[:, 0:tw], yb[:, 0:tw], cwf[:, ch, 0:1], u[:, 0:tw], ALU.mult, ALU.add, [yk, "cwf", uk], [uk])
                            S.op("pool", lambda e, yb=yb, ch=ch: e.tensor_copy(halo[:, ch, :], yb[:, tw:tw + 2]), reads=[yk], writes=["halo"])
                        if store:
                            act(t1[:, 0:tw], ua[:, 0:tw], AF.Silu, ["ua"], ["t1"])
                            tt(gT[:, blk, 0:tw], t1[:, 0:tw], ub[:, 0:tw], ALU.mult, ["t1", "ub"], ["gT"])
                if ti == 0:
                    store_dbg("gT", gT[:], "gT")
                if not store:
                    return
                load(Gb[:], mod_d[0:1, 5 * D:6 * D].partition_broadcast(128), "Gb", reads=["mod_d5"])
                for jb in range(4):
                    pbs = [bank() for _ in range(cpt)]
                    parts = [(0, 16), (16, 16), (32, 12)]
                    for pi, (f0, nf) in enumerate(parts):
                        wd = wload([(w_dn, f0 * 128, jb * 512, 512)], krows=nf * 128)
                        for c in range(cpt):
                            def f(e, c=c, wd=wd, f0=f0, nf=nf, pi=pi, pbs=pbs):
                                r = None
                                for k in range(nf):
                                    r = e.matmul(PS[pbs[c]][:, 0:512], lhsT=gT[:, f0 + k, c * 128:(c + 1) * 128], rhs=WS[wd][:, k, 0:512],
                                                 start=(pi == 0 and k == 0), stop=(pi == 2 and k == nf - 1))
                                return r
                            S.op("pe", f, reads=wk(wd) + ["gT"], writes=["ps%d" % pbs[c]])
                    for c in range(cpt):
                        resid_cb(c, pbs[c], jb)

            def final_store(cpt, row0):
                if row0 == 0:
                    store_dbg("x2", xt[:], "xt")
                load(Gb[:], gf_d.partition_broadcast(128), "Gb")
                ss = small[:, 0:cpt]
                rs = small[:, 8:8 + cpt]
                for c in range(cpt):
                    act(sqjb[:, 0:2048], xt[:, c, :], AF.Square, ["xt"], ["sqj", "ss"], accum_out=ss[:, c:c + 1])
                rsqrt_small(rs, ss, 1.0 / D, EPS, ["ss"], "rs")
                for c in range(cpt):
                    stt(xt[:, c, :], xt[:, c, :], rs[:, c:c + 1], Gb[:], ALU.mult, ALU.mult, ["xt", "rs", "Gb"], ["xt"])
                    S.dma("sp", "st_out", lambda e, c=c: e.dma_start(out=out_d[row0 + c * 128: row0 + (c + 1) * 128, :], in_=xt[:, c, :]), reads=["xt"])


            return dict(norm_to_hT=norm_to_hT, rope_tables=rope_tables, ret_head=ret_head, hg_group=hg_group,
                        merge_and_out2=merge_and_out2, ffn=ffn, final_store=final_store, xt=xt)

        OPS = {"s": make_ops(LS), "f": make_ops(LF)}
        print("SBUF bytes/partition (final):", sbytes[0])
        ch0 = 0
        prev_mode = None
        S.barrier()
        for ti, (cpt, mode, row0) in enumerate(tiles):
            so = mode == "s"
            if mode == "f" and prev_mode != "f":
                while precast_todo:
                    precast(*precast_todo.pop(0))
                late_mod()
            if prev_mode is not None and prev_mode != mode:
                S.barrier()
            prev_mode = mode
            O = OPS[mode]
            for c in range(cpt):
                load(O["xt"][:, c, :], x_d[(ch0 + c) * 128:(ch0 + c + 1) * 128, :], "xt")
            O["norm_to_hT"](0, 1, cpt)
            if so and precast_todo:
                precast(*precast_todo.pop(0))
            O["rope_tables"](ch0 * 128, cpt * 128)
            if ti == 0 and "cos" in dbg_d:
                store_dbg("cos", (LS if so else LF)["cosT"], "cosT")
                store_dbg("sin", (LS if so else LF)["sinT"], "sinT")
                store_dbg("sqj", (LS if so else LF)["sqj"], "sqj")
            for h in range(RET_H):
                O["ret_head"](h, ch0, cpt, so)
                if ti == 0 and h == 0 and "kf32" in dbg_d:
                    store_dbg("kf32", (LS if so else LF)["kf32"], "kf32")
                    store_dbg("R0", Rst[:, 0, :], "Rst")
            for g in range(2):
                O["hg_group"](g, ch0, cpt, so)
            if not so and row0 == 0 and "ogT" in dbg_d:
                store_dbg("ogT", LF["ogT"], "ogT")
            if not so:
                O["merge_and_out2"](cpt)
                O["norm_to_hT"](2, 3, cpt)
                O["ffn"](ti, cpt, row0 is not None)
                if row0 is not None:
                    O["final_store"](cpt, row0)
            if so and gemv_todo:
                gemv_block(*gemv_todo.pop(0))
            ch0 += cpt

        fin = [(k, v) for k, v in S.dma_cnt.items() if k.startswith("st_") or k.startswith("dbg_")]
        S.wait_all("sp", fin)
        with nc.Block() as block:
            S.replay(block)
    return nc


CPS = 4


def make_tiles(nstate_chunks, pre_chunks, own_chunks):
    tiles = []
    n = nstate_chunks
    while n > 0:
        c = min(CPS, n)
        tiles.append((c, "s", None))
        n -= c
    n = pre_chunks
    while n > 0:
        c = min(CPTMAX, n)
        tiles.append((c, "f", None))
        n -= c
    row = 0
    n = own_chunks
    while n > 0:
        c = min(CPTMAX, n)
        tiles.append((c, "f", row))
        row += c * 128
        n -= c
    return tiles


_CACHE = {}


def kernel(x, c, positions, w_ada, b_ada, g_norm1, w_in, w_ret_o, w_hg_o, w_out,
           hg_lb, g_norm2, w_up, conv_w, conv_b, w_down, g_final):
    own = SEQ // NCORES // 128
    pre = 1
    nstate = (NCORES - 1) * own - pre
    tiles = make_tiles(nstate, pre, own)
    if "nc" not in _CACHE:
        _CACHE["nc"] = build(tiles)
    nc = _CACHE["nc"]
    nch = nstate + pre + own
    ntok = nch * 128
    xs = np.asarray(x, np.float32).reshape(SEQ, D)
    ps = np.asarray(positions, np.int32).reshape(SEQ)
    HC = host_consts()
    shared = {
        "c": np.asarray(c, np.float32).reshape(1, D),
        "w_ada": np.ascontiguousarray(np.asarray(w_ada, np.float32).reshape(D, 6 * D)),
        "b_ada": np.asarray(b_ada, np.float32).reshape(1, 6 * D),
        "g_norm1": np.asarray(g_norm1, np.float32).reshape(1, D),
        "w_in": np.ascontiguousarray(np.asarray(w_in, np.float32).reshape(D, IN_COLS)),
        "w_ret_o": np.ascontiguousarray(np.asarray(w_ret_o, np.float32).reshape(1024, D)),
        "w_hg_o": np.ascontiguousarray(np.asarray(w_hg_o, np.float32).reshape(1024, D)),
        "w_out": np.ascontiguousarray(np.asarray(w_out, np.float32).reshape(D, D)),
        "hg_lb": np.asarray(hg_lb, np.float32).reshape(2, 1024),
        "g_norm2": np.asarray(g_norm2, np.float32).reshape(1, D),
        "w_up": np.ascontiguousarray(np.asarray(w_up, np.float32).reshape(D, 2 * DFF)),
        "conv_w": np.asarray(conv_w, np.float32).reshape(3, 2 * DFF),
        "conv_b": np.asarray(conv_b, np.float32).reshape(1, 2 * DFF),
        "w_down": np.ascontiguousarray(np.asarray(w_down, np.float32).reshape(DFF, D)),
        "g_final": np.asarray(g_final, np.float32).reshape(1, D),
    }
    for k in CONST_SHAPES:
        shared["k_" + k] = np.ascontiguousarray(HC[k]).reshape(CONST_SHAPES[k])
    in_maps = []
    for core in range(NCORES):
        nreal = (core + 1) * own * 128
        xc = np.zeros((ntok, D), np.float32)
        xc[ntok - nreal:] = xs[:nreal]
        pc = np.zeros((1, ntok), np.int32)
        pc[0, ntok - nreal:] = ps[:nreal]
        tokmask = np.zeros(ntok, np.float32)
        tokmask[ntok - nreal:] = 1.0
        vm = np.ascontiguousarray(tokmask.reshape(nch, 128).T)
        pm = np.zeros((128, len(tiles)), np.float32)
        chs = 0
        for ti, (cpt, mode, row0) in enumerate(tiles):
            prev_real = 1.0 if (chs * 128 - 1) >= (ntok - nreal) else 0.0
            pm[:, ti] = prev_real
            chs += cpt
        m = dict(shared)
        m.update({"x": xc, "positions": pc, "vmask": vm, "pmask": pm})
        in_maps.append(m)
    res = run_bass_kernel_spmd(nc, in_maps, core_ids=list(range(NCORES)))
    outs = [np.asarray(r["out"])[: own * 128] for r in res.results]
    return np.concatenate(outs, axis=0).reshape(1, SEQ, D).astype(np.float32)
```

```python
import contextlib
import numpy as np
import concourse.bass as bass
import concourse.mybir as mybir
from concourse.bass_utils import run_bass_kernel_spmd

F32 = mybir.dt.float32
BF16 = mybir.dt.bfloat16
I32 = mybir.dt.int32
ALU = mybir.AluOpType
AF = mybir.ActivationFunctionType

D = 2048
KC = 16
SEQ = 16384
NCORES = 8
CPTMAX = 2
TMAX = 128 * CPTMAX
DFF = 5632
NFB = DFF // 128
RET_H, HG_H = 4, 8
IN_COLS = 12288
EPS = 1e-6
HEPS = 1e-5


class Sched:
    ENGS = ("pe", "act", "dve", "pool", "sp")

    def __init__(self, nc, stack):
        self.nc = nc
        self.stack = stack
        self.q = {e: [] for e in self.ENGS}
        self.cnt = {e: 0 for e in self.ENGS}
        self.known = {e: {} for e in self.ENGS}
        self.last_w = {}
        self.readers = {}
        self.sems = {}
        self.dma_cnt = {}
        for e in ("pe", "act", "dve", "pool"):
            self.sems[e] = stack.enter_context(nc.semaphore("c_" + e))

    def _sem(self, key):
        if key not in self.sems:
            self.sems[key] = self.stack.enter_context(self.nc.semaphore("d_" + str(key)))
            self.dma_cnt[key] = 0
        return self.sems[key]

    def _deps(self, eng, reads, writes, extra=()):
        need = {}

        def add(t):
            if t is None:
                return
            sk, v = t
            if need.get(sk, 0) < v:
                need[sk] = v
        for k in reads:
            add(self.last_w.get(k))
        for k in writes:
            add(self.last_w.get(k))
            for t in self.readers.get(k, ()):
                add(t)
        for t in extra:
            add(t)
        waits = []
        for sk, v in need.items():
            if sk == eng and eng == "pe":
                continue
            if self.known[eng].get(sk, 0) >= v:
                continue
            self.known[eng][sk] = v
            waits.append((sk, v))
        return waits

    def _commit(self, tok, reads, writes):
        for k in reads:
            self.readers.setdefault(k, []).append(tok)
        for k in writes:
            self.last_w[k] = tok
            self.readers[k] = []

    def op(self, eng, fn, reads=(), writes=()):
        waits = self._deps(eng, reads, writes)
        self.cnt[eng] += 1
        tok = (eng, self.cnt[eng])
        self.q[eng].append((fn, waits, (eng, 1)))
        self._commit(tok, reads, writes)
        return tok

    def dma(self, eng, slot, fn, reads=(), writes=()):
        self._sem(slot)
        waits = self._deps(eng, reads, writes)
        self.dma_cnt[slot] += 16
        tok = (slot, self.dma_cnt[slot])
        self.q[eng].append((fn, waits, (slot, 16)))
        self._commit(tok, reads, writes)
        return tok

    def barrier(self):
        toks = [(e, self.cnt[e]) for e in ("pe", "act", "dve", "pool") if self.cnt[e] > 0]
        toks += [(k, v) for k, v in self.dma_cnt.items() if v > 0]
        for e in self.ENGS:
            self.wait_all(e, toks)

    def wait_all(self, eng, toks):
        waits = self._deps(eng, (), (), toks)
        self.q[eng].append((None, waits, None))

    def replay(self, block):
        engmap = {"pe": block.tensor, "act": block.scalar, "dve": block.vector,
                  "pool": block.gpsimd, "sp": block.sync}
        sems = self.sems
        for e in self.ENGS:
            def body(engine, items=self.q[e]):
                for fn, waits, inc in items:
                    for sk, v in waits:
                        engine.wait_ge(sems[sk], v)
                    if fn is None:
                        continue
                    ins = fn(engine)
                    ins.then_inc(sems[inc[0]], inc[1])
            engmap[e](body)


def host_consts():
    c = {}
    c["ident"] = np.eye(128, dtype=np.float32)
    idx = np.arange(128, dtype=np.float64)
    gam = 1.0 - np.exp2(-5.0 - np.arange(RET_H, dtype=np.float64))
    lg = np.log(gam)
    diff = idx[None, :] - idx[:, None]
    dec = np.where(diff >= 0, np.exp(np.maximum(diff, 0)[None] * lg[:, None, None]), 0.0)
    c["decT"] = np.ascontiguousarray((dec * 256 ** -0.5).transpose(1, 0, 2)).astype(np.float32)
    xi = np.exp((idx[None, :] + 1.0) * lg[:, None])
    c["xib"] = np.ascontiguousarray(np.broadcast_to(xi[None], (128, RET_H, 128))).astype(np.float32)
    zeta = np.exp((127.0 - idx[None, :]) * lg[:, None]) * 256 ** -0.5
    c["zeta"] = np.ascontiguousarray(zeta.T).astype(np.float32)
    c["gchunk"] = np.exp(128 * lg)
    sub = (np.arange(128) // 32)
    same = sub[:, None] == sub[None, :]
    s = np.arange(128)
    c["triinc"] = (same & (s[:, None] <= s[None, :])).astype(np.float32)
    c["trirev"] = (same & (s[:, None] > s[None, :])).astype(np.float32)
    c["rowmask"] = (sub[:, None] == np.arange(4)[None, :]).astype(np.float32)
    c["colmask"] = np.ascontiguousarray(np.broadcast_to((sub[None, :] == np.arange(4)[:, None])[None], (128, 4, 128))).astype(np.float32)
    c["invf"] = (10000.0 ** (-np.arange(0, 256, 2, dtype=np.float32) / 256)).astype(np.float32)[:, None]
    c["neghalf"] = np.full((128, 8), -0.5, np.float32)
    return c


CONST_SHAPES = {"ident": [128, 128], "decT": [128, 4, 128], "xib": [128, 4, 128], "zeta": [128, 4],
                "triinc": [128, 128], "trirev": [128, 128], "rowmask": [128, 4], "colmask": [128, 4, 128],
                "invf": [128, 1], "neghalf": [128, 8]}


def bc_mid(a, n):
    return bass.AP(a.tensor, a.offset, [list(a.ap[0]), [0, n], list(a.ap[-1])])


def build(tiles, dbg=None):
    NCH = sum(t[0] for t in tiles)
    NTOK = NCH * 128
    NT = len(tiles)
    NOUT = sum(t[0] * 128 for t in tiles if t[1] == "f" and t[2] is not None)
    HC = host_consts()
    gchunk = [float(v) for v in HC["gchunk"]]
    nc = bass.Bass("TRN2", target_bir_lowering=False)
    din = lambda n, s, dt=F32: nc.dram_tensor(n, s, dt, kind="ExternalInput").ap()
    x_d = din("x", [NTOK, D])
    pos_d = din("positions", [1, NTOK], I32)
    vmask_d = din("vmask", [128, NCH])
    pmask_d = din("pmask", [128, NT])
    c_d = din("c", [1, D])
    w_ada = din("w_ada", [D, 6 * D])
    b_ada = din("b_ada", [1, 6 * D])
    g1_d = din("g_norm1", [1, D])
    w_in = din("w_in", [D, IN_COLS])
    w_ro = din("w_ret_o", [1024, D])
    w_ho = din("w_hg_o", [1024, D])
    w_out = din("w_out", [D, D])
    lb_d = din("hg_lb", [2, 1024])
    g2_d = din("g_norm2", [1, D])
    w_up = din("w_up", [D, 2 * DFF])
    cw_d = din("conv_w", [3, 2 * DFF])
    cb_d = din("conv_b", [1, 2 * DFF])
    w_dn = din("w_down", [DFF, D])
    gf_d = din("g_final", [1, D])
    cd = {k: din("k_" + k, s) for k, s in CONST_SHAPES.items()}
    out_d = nc.dram_tensor("out", [max(NOUT, 128), D], F32, kind="ExternalOutput").ap()
    mod_d = nc.dram_tensor("mod_scratch", [1, 6 * D], F32).ap()
    WSRC = {"w_in": w_in, "w_ret_o": w_ro, "w_hg_o": w_ho, "w_out": w_out, "w_up": w_up, "w_down": w_dn}
    WBF = {k: nc.dram_tensor("bf_" + k, list(v.shape), BF16).ap() for k, v in WSRC.items()}
    WNAME = {id(v): k for k, v in WSRC.items()}
    PCB = 1024
    dbg_d = {}
    if dbg:
        for k, (s, dt_) in dbg.items():
            dbg_d[k] = nc.dram_tensor("dbg_" + k, s, dt_, kind="ExternalOutput").ap()

    with contextlib.ExitStack() as st:
        S = Sched(nc, st)
        sbytes = [0]

        def sb(name, shape, dt=F32):
            n = int(np.prod(shape[1:])) * (4 if dt in (F32, I32) else 2)
            sbytes[0] += n
            return st.enter_context(nc.sbuf_tensor(name, shape, dt))
        K = {k: sb("c_" + k, s) for k, s in CONST_SHAPES.items()}
        WS = [sb("ws%d" % i, [128, KC, 512], BF16) for i in range(2)]
        AB = sb("AB", [128, 4, KC])
        LBb = sb("LBb", [128, 1024])
        OMLb = sb("OMLb", [128, 1024])
        omlf = sb("omlf", [128, 8])
        cwf = sb("cwf", [128, 2 * NFB, 4])
        halo = sb("halo", [128, 2 * NFB, 2])
        vmask = sb("vmask_s", [128, NCH])
        pmask = sb("pmask_s", [128, NT])
        Rst = sb("Rst", [128, RET_H, 512])
        Rbf = sb("Rbf", [128, RET_H, 512], BF16)
        Sst = sb("Sst", [128, HG_H, 128])
        Sbf = sb("Sbf", [128, 4, 4, 128], BF16)
        small = sb("small", [128, 96])

        def spec(cp, full):
            T_ = cp * 128
            L = [("xt", [128, cp, D], F32), ("hT", [128, KC, T_], BF16), ("cosT", [128, T_], F32), ("sinT", [128, T_], F32),
                 ("posi", [128, min(T_, 256)], I32), ("sqj", [128, 1024], F32), ("dgs", [128, cp, 128], F32),
                 ("t1", [128, T_], F32), ("t2", [128, T_], F32), ("kf32", [128, 2, T_], F32),
                 ("logf", [128, 512], F32), ("ktm", [128, 512], F32), ("sgm", [128, 512], F32), ("ecb", [128, 512], F32),
                 ("eb", [128, cp, 512], F32), ("kbf", [128, 2, T_], BF16), ("kzt", [128, cp, 256], BF16), ("vret", [128, cp, 256], BF16),
                 ("khat", [128, cp, 512], BF16), ("vhg", [128, cp, 512], BF16), ("kz", [128, 512], BF16)]
            if full:
                L += [("ogT", [128, KC, T_], BF16), ("mgT", [128, KC, T_], BF16), ("gT", [128, NFB, T_], BF16), ("Gb", [128, D], F32),
                      ("sgr", [128, cp, 256], F32), ("osb", [128, 512], F32), ("osq", [128, 512], F32), ("enb", [128, cp, 512], F32),
                      ("sgate", [128, cp, 512], F32), ("ybuf", [128, 2, T_ + 2], F32), ("ua", [128, T_], F32), ("ub", [128, T_], F32),
                      ("qrot", [128, 2, T_], BF16), ("qhat", [128, 2, T_], BF16), ("inn", [128, 128], BF16), ("qt", [128, 4, T_], BF16),
                      ("kt", [128, 4, T_], BF16), ("amt", [128, 4, 128], BF16), ("qz", [128, 4, 4, 128], BF16)]
            return L

        def nf32(shape, dt_):
            n = int(np.prod(shape[1:]))
            return (n + 1) // 2 if dt_ == BF16 else n

        CPS_ = max([t[0] for t in tiles if t[1] == "s"] + [1])
        CPF_ = max([t[0] for t in tiles if t[1] == "f"] + [1])
        ls_tot = sum(nf32(sh, d_) for _, sh, d_ in spec(CPS_, False))
        lf_tot = sum(nf32(sh, d_) for _, sh, d_ in spec(CPF_, True))
        need = max(ls_tot + 6656, lf_tot)
        AR = sb("arena", [128, need])

        def layout(cp, full):
            L = {}
            off = 0
            for name, shape, dt_ in spec(cp, full):
                n = int(np.prod(shape[1:]))
                w = nf32(shape, dt_)
                v = AR[:, off:off + w]
                if dt_ == BF16:
                    v = v.bitcast(BF16)[:, 0:n]
                elif dt_ == I32:
                    v = v.bitcast(I32)
                if len(shape) == 3:
                    v = v.rearrange("p (a b) -> p a b", a=shape[1])
                elif len(shape) == 4:
                    v = v.rearrange("p (a b c) -> p a b c", a=shape[1], b=shape[2])
                L[name] = v
                L["_off_" + name] = off
                off += w
            return L
        LF = layout(CPF_, True)
        LS = layout(CPS_, False)
        xt, Gb, sqj, gT = LF["xt"], LF["Gb"], LF["sqj"], LF["gT"]
        tgA = AR[:, ls_tot:ls_tot + 2048]
        tgB = AR[:, ls_tot + 2048:ls_tot + 4096]
        mrow = AR[0:1, ls_tot + 4096:ls_tot + 6144]
        brow2 = AR[0:1, ls_tot + 6144:ls_tot + 6656]
        sqjb = sqj.bitcast(BF16)
        rowb = Gb
        print("SBUF bytes/partition:", sbytes[0])
        PS = [st.enter_context(nc.psum_tensor("ps%d" % i, [128, 512], F32)) for i in range(8)]
        psn = [0]

        def bank():
            i = psn[0] % 8
            psn[0] += 1
            return i

        def load(dst_ap, src_ap, key, eng="sp", reads=()):
            return S.dma(eng, "ld_" + key, lambda e: e.dma_start(out=dst_ap, in_=src_ap, allow_slow_non_contiguous=True), writes=[key], reads=reads)

        wsn = [0]

        def wload(segs, krows=D):
            i = wsn[0] % 2
            wsn[0] += 1
            ws = WS[i]
            off = 0
            kc = krows // 128
            for si, (w, row0, c0, n) in enumerate(segs):
                name = WNAME[id(w)]
                src = WBF[name][row0:row0 + krows, c0:c0 + n].rearrange("(kc p) n -> p kc n", p=128)
                dst = ws[:, 0:kc, off:off + n]
                rk = sorted(set(["wb_%s_%d" % (name, c // PCB) for c in (c0, c0 + n - 1)]))
                S.dma("sp", "ws%d_%d" % (i, si), (lambda e, dst=dst, src=src: e.dma_start(out=dst, in_=src)), writes=["ws%d_s%d" % (i, si)], reads=rk)
                off += n
            return i

        def wk(i):
            return ["ws%d_s0" % i, "ws%d_s1" % i]

        def mm_group(out_ap, pairs, reads, wkey):
            def f(e):
                r = None
                n = len(pairs)
                for i, (l, rh) in enumerate(pairs):
                    r = e.matmul(out_ap, lhsT=l, rhs=rh, start=(i == 0), stop=(i == n - 1))
                return r
            return S.op("pe", f, reads=reads, writes=[wkey])

        def mm_multi(items, reads, wkey):
            def f(e):
                r = None
                for (o, l, rh) in items:
                    r = e.matmul(o, lhsT=l, rhs=rh, start=True, stop=True)
                return r
            return S.op("pe", f, reads=reads, writes=[wkey])

        def tr_multi(items, reads, wkey):
            def f(e):
                r = None
                for (o, i_) in items:
                    r = e.transpose(o, i_, K["ident"][:])
                return r
            return S.op("pe", f, reads=list(reads) + ["c_ident"], writes=[wkey])

        def act(out, in_, func, reads, writes, **kw):
            return S.op("act", lambda e: e.activation(out, in_, func, **kw), reads=reads, writes=writes)

        def tt(out, a, b, op, reads, writes, eng="dve"):
            return S.op(eng, lambda e: e.tensor_tensor(out, a, b, op), reads=reads, writes=writes)

        def ts(out, a, s1, s2, op0, op1, reads, writes, eng="dve"):
            if op1 is None:
                return S.op(eng, lambda e: e.tensor_scalar(out, a, s1, s2, op0), reads=reads, writes=writes)
            return S.op(eng, lambda e: e.tensor_scalar(out, a, s1, s2, op0, op1), reads=reads, writes=writes)

        def stt(out, a, s, b, op0, op1, reads, writes, eng="dve"):
            return S.op(eng, lambda e: e.scalar_tensor_tensor(out, a, s, b, op0, op1), reads=reads, writes=writes)

        def rsqrt_small(dst, src, scale, eps, reads, wkey):
            n = src.shape[1]
            ts(dst, src, scale, eps, ALU.mult, ALU.add, reads, [wkey])
            S.op("pool", lambda e: e.tensor_tensor(dst, dst, K["neghalf"][:, 0:n], ALU.pow), reads=[wkey, "c_neghalf"], writes=[wkey])

        def store_dbg(name, ap, key):
            if name in dbg_d:
                S.dma("sp", "dbg_" + name, lambda e: e.dma_start(out=dbg_d[name], in_=ap), reads=[key])

        for k in CONST_SHAPES:
            load(K[k][:], cd[k], "c_" + k)
        def precast(name, cb):
            src = WSRC[name]
            ncols = src.shape[1]
            c0 = cb * PCB
            n = min(PCB, ncols - c0)
            key = "wb_%s_%d" % (name, cb)
            S.dma("pool", "pc_" + key, (lambda e, src=src, name=name, c0=c0, n=n: e.dma_start(out=WBF[name][:, c0:c0 + n], in_=src[:, c0:c0 + n])), writes=[key])

        precast_todo = [("w_in", cb) for cb in (0, 3, 4, 7, 8, 9, 10, 11)]
        for name in ("w_ret_o", "w_hg_o", "w_out"):
            precast_todo += [(name, cb) for cb in range(2)]
        precast_todo += [("w_up", cb) for cb in range(11)] + [("w_down", cb) for cb in range(2)]
        for cb in (1, 2, 5, 6):
            precast("w_in", cb)
        if not any(t[1] == "s" for t in tiles):
            while precast_todo:
                precast(*precast_todo.pop(0))
        load(vmask[:], vmask_d, "vmask")
        load(pmask[:], pmask_d, "pmask")
        S.op("pool", lambda e: e.memset(halo[:], 0.0), writes=["halo"])
        S.op("pool", lambda e: e.memset(Rst[:], 0.0), writes=["Rst"])
        S.op("pool", lambda e: e.memset(Rbf[:], 0.0), writes=["Rbf"])
        S.op("pool", lambda e: e.memset(Sst[:], 0.0), writes=["Sst"])
        S.op("pool", lambda e: e.memset(Sbf[:], 0.0), writes=["Sbf0", "Sbf1", "Sbf2", "Sbf3"])
        cf = small[:, 0:16]
        g1f = small[:, 16:32]
        g2f = small[:, 32:48]
        scf = small[:, 48:64]
        lbf = small[:, 64:80]
        with nc.allow_non_contiguous_dma(reason="tiny one-time strided loads"):
            for j in range(3):
                load(cwf[:, :, j:j + 1], cw_d[j:j + 1, :].rearrange("o (b p) -> p b o", p=128), "cwf")
            load(cwf[:, :, 3:4], cb_d.rearrange("o (b p) -> p b o", p=128), "cwf")
            load(cf, c_d.rearrange("o (k p) -> p (o k)", p=128), "small_c")
            load(g1f, g1_d.rearrange("o (k p) -> p (o k)", p=128), "small_g1")
            load(g2f, g2_d.rearrange("o (k p) -> p (o k)", p=128), "small_g2")
            load(lbf.rearrange("p (l h) -> p l h", l=2), lb_d.rearrange("l (h p) -> p l h", p=128), "small_lb")
        act(scf, cf, AF.Silu, ["small_c"], ["small_sc"])
        sn_ = [0]

        def gemv_block(g, nb):
            col = g * 2048 + nb * 512
            load(brow2, b_ada[0:1, col:col + 512], "brow2")
            pb = bank()
            for kq in range(4):
                buf = (tgA, tgB)[sn_[0] % 2]
                key = "tg%d" % (sn_[0] % 2)
                sn_[0] += 1
                src = w_ada[kq * 512:(kq + 1) * 512, col:col + 512].rearrange("(k p) n -> p k n", p=128)
                load(buf.rearrange("p (k n) -> p k n", k=4), src, key)

                def f(e, kq=kq, pb=pb, buf=buf):
                    r = None
                    for k4 in range(4):
                        kc = kq * 4 + k4
                        r = e.matmul(PS[pb][0:1, :], lhsT=scf[:, kc:kc + 1], rhs=buf[:, k4 * 512:(k4 + 1) * 512],
                                     start=(kc == 0), stop=(kc == 15))
                    return r
                S.op("pe", f, reads=[key, "small_sc"], writes=["ps%d" % pb])
            tt(mrow[:, nb * 512:(nb + 1) * 512], PS[pb][0:1, :], brow2, ALU.add, ["ps%d" % pb, "brow2"], ["mrow"])
            if nb == 3:
                S.dma("sp", "st_mod", lambda e, g=g: e.dma_start(out=mod_d[0:1, g * 2048:(g + 1) * 2048], in_=mrow), reads=["mrow"], writes=["mod_d%d" % g])

        for g in range(2):
            for nb in range(4):
                gemv_block(g, nb)
        gemv_todo = [(g, nb) for g in range(2, 6) for nb in range(4)]
        s1f = small[:, 80:96]
        fmv = lambda g: mod_d[0:1, g * 2048:(g + 1) * 2048].rearrange("o (k p) -> p (o k)", p=128)
        load(AB[:, 1, :], fmv(0), "AB1", reads=["mod_d0"])
        load(s1f, fmv(1), "small_s1", reads=["mod_d1"])
        stt(AB[:, 0, :], s1f, 1.0, g1f, ALU.add, ALU.mult, ["small_s1", "small_g1"], ["AB0"])

        def late_mod():
            while gemv_todo:
                gemv_block(*gemv_todo.pop(0))
            load(AB[:, 3, :], fmv(3), "AB3", reads=["mod_d3"])
            load(s1f, fmv(4), "small_s1", reads=["mod_d4"])
            stt(AB[:, 2, :], s1f, 1.0, g2f, ALU.add, ALU.mult, ["small_s1", "small_g2"], ["AB2"])
        load(tgA[:, 0:1024], lb_d[0:1, :].partition_broadcast(128), "tg0")
        load(tgA[:, 1024:2048], lb_d[1:2, :].partition_broadcast(128), "tg0")
        act(tgA, tgA, AF.Exp, ["tg0"], ["tg0"])
        tt(tgB[:, 0:1024], tgA[:, 0:1024], tgA[:, 1024:2048], ALU.add, ["tg0"], ["tg1"])
        S.op("dve", lambda e: e.reciprocal(tgB[:, 0:1024], tgB[:, 0:1024]), reads=["tg1"], writes=["tg1"])
        tt(LBb[:], tgA[:, 0:1024], tgB[:, 0:1024], ALU.mult, ["tg0", "tg1"], ["LBb"])
        ts(OMLb[:], LBb[:], -1.0, 1.0, ALU.mult, ALU.add, ["LBb"], ["OMLb"])
        act(lbf, lbf, AF.Exp, ["small_lb"], ["small_lb"])
        tt(omlf[:], lbf[:, 0:8], lbf[:, 8:16], ALU.add, ["small_lb"], ["omlf"])
        S.op("dve", lambda e: e.reciprocal(omlf[:], omlf[:]), reads=["omlf"], writes=["omlf"])
        tt(omlf[:], omlf[:], lbf[:, 8:16], ALU.mult, ["omlf", "small_lb"], ["omlf"])

        def make_ops(L):
            xt = L.get('xt')
            hT = L.get('hT')
            cosT = L.get('cosT')
            sinT = L.get('sinT')
            posi = L.get('posi')
            sqj = L.get('sqj')
            dgs = L.get('dgs')
            t1 = L.get('t1')
            t2 = L.get('t2')
            kf32 = L.get('kf32')
            logf = L.get('logf')
            ktm = L.get('ktm')
            sgm = L.get('sgm')
            ecb = L.get('ecb')
            eb = L.get('eb')
            kbf = L.get('kbf')
            kzt = L.get('kzt')
            vret = L.get('vret')
            khat = L.get('khat')
            vhg = L.get('vhg')
            kz = L.get('kz')
            ogT = L.get('ogT')
            mgT = L.get('mgT')
            gT = L.get('gT')
            Gb = L.get('Gb')
            sgr = L.get('sgr')
            osb = L.get('osb')
            osq = L.get('osq')
            enb = L.get('enb')
            sgate = L.get('sgate')
            ybuf = L.get('ybuf')
            ua = L.get('ua')
            ub = L.get('ub')
            qrot = L.get('qrot')
            qhat = L.get('qhat')
            inn = L.get('inn')
            qt = L.get('qt')
            kt = L.get('kt')
            amt = L.get('amt')
            qz = L.get('qz')
            sqjb = sqj.bitcast(BF16)
            sgA = sgB = None
            if enb is not None:
                tq = eb.shape[1] * 512 // 4
                sgA = eb.rearrange("p c n -> p (c n)").rearrange("p (m t) -> p m t", m=4)
                sgB = enb.rearrange("p c n -> p (c n)").rearrange("p (m t) -> p m t", m=4)
            def norm_to_hT(ai, bi, cpt):
                ss = small[:, 0:cpt]
                rs = small[:, 8:8 + cpt]
                for c in range(cpt):
                    act(sqjb[:, 0:2048], xt[:, c, :], AF.Square, ["xt"], ["sqj", "ss"], accum_out=ss[:, c:c + 1])
                rsqrt_small(rs, ss, 1.0 / D, EPS, ["ss"], "rs")
                for c in range(cpt):
                    ts(dgs[:, c, :], K["ident"][:], rs[:, c:c + 1], None, ALU.mult, None, ["c_ident", "rs"], ["dgs%d" % c])
                    for q in range(4):
                        pb = bank()
                        mm_multi([(PS[pb][:, k4 * 128:(k4 + 1) * 128], xt[:, c, (q * 4 + k4) * 128:(q * 4 + k4 + 1) * 128], dgs[:, c, :]) for k4 in range(4)],
                                 ["xt", "dgs%d" % c], "ps%d" % pb)
                        for k4 in range(4):
                            kc = q * 4 + k4
                            act(hT[:, kc, c * 128:(c + 1) * 128], PS[pb][:, k4 * 128:(k4 + 1) * 128], AF.Identity,
                                ["ps%d" % pb, "AB%d" % ai, "AB%d" % bi], ["hT"], scale=AB[:, ai, kc:kc + 1], bias=AB[:, bi, kc:kc + 1])

            def gemm_fm(wi, col0, nmb, actT, akey, kcs, tw, cb):
                for mb in range(nmb):
                    pb = bank()
                    pairs = [(WS[wi][:, kci, col0 + mb * 128: col0 + (mb + 1) * 128], actT[:, kc, 0:tw]) for kci, kc in enumerate(kcs)]
                    mm_group(PS[pb][:, 0:tw], pairs, wk(wi) + [akey], "ps%d" % pb)
                    cb(mb, pb)

            def gemm_tm(wi, col0, ncols, actT, akey, kcs, cpt, cb):
                for c in range(cpt):
                    pb = bank()
                    pairs = [(actT[:, kc, c * 128:(c + 1) * 128], WS[wi][:, kci, col0:col0 + ncols]) for kci, kc in enumerate(kcs)]
                    mm_group(PS[pb][:, 0:ncols], pairs, wk(wi) + [akey], "ps%d" % pb)
                    cb(c, pb)

            def rope_tables(tok0, tw_all):
                for off in range(0, tw_all, 256):
                    rope_piece(tok0 + off, off, min(256, tw_all - off))

            def rope_piece(tok0, off, tw):
                load(posi[:, 0:tw], pos_d[0:1, tok0:tok0 + tw].partition_broadcast(128), "posi")
                ang = sqj[:, 0:tw]
                kf = sqj[:, 256:256 + tw]
                ki = sqj[:, 512:512 + tw].bitcast(I32)
                a2 = sqj[:, 768:768 + tw]
                kk = ["sqj"]
                S.op("dve", lambda e: e.tensor_copy(ang, posi[:, 0:tw]), reads=["posi"], writes=kk)
                ts(ang, ang, K["invf"][:, 0:1], None, ALU.mult, None, kk + ["c_invf"], kk)
                for shift, dst, key in ((0.0, sinT, "sinT"), (float(np.pi / 2), cosT, "cosT")):
                    ts(a2, ang, shift, None, ALU.add, None, kk, kk)
                    ts(kf, a2, float(1.0 / (2 * np.pi)), None, ALU.mult, None, kk, kk)
                    S.op("dve", lambda e: e.tensor_copy(ki, kf), reads=kk, writes=kk)
                    S.op("dve", lambda e: e.tensor_copy(kf, ki), reads=kk, writes=kk)
                    stt(a2, kf, -6.28125, a2, ALU.mult, ALU.add, kk, kk)
                    stt(a2, kf, -float(2 * np.pi - 6.28125), a2, ALU.mult, ALU.add, kk, kk)
                    ts(kf, a2, float(np.pi), None, ALU.is_gt, None, kk, kk)
                    stt(a2, kf, -float(2 * np.pi), a2, ALU.mult, ALU.add, kk, kk)
                    ts(kf, a2, -float(np.pi), None, ALU.is_lt, None, kk, kk)
                    stt(a2, kf, float(2 * np.pi), a2, ALU.mult, ALU.add, kk, kk)
                    ts(a2, a2, 3.1415925, -3.1415925, ALU.min, ALU.max, kk, kk)
                    act(dst[:, off:off + tw], a2, AF.Sin, kk, [key])

            def ret_head(h, ch0, cpt, so):
                tw = cpt * 128
                if not so:
                    act(Rbf[:, h, :], Rst[:, h, :], AF.Copy, ["Rst"], ["Rbf"])
                segs = [(w_in, 0, 1024 + h * 256, 256)]
                if not so:
                    segs = [(w_in, 0, h * 256, 256)] + segs
                wi = wload(segs)
                kcol = 0 if so else 256
                held = {}

                def rope_cb(is_k):
                    def cb(mb, pb):
                        held[mb] = pb
                        if mb != 1:
                            return
                        p1, p2 = PS[held[0]][:, 0:tw], PS[held[1]][:, 0:tw]
                        k12 = ["ps%d" % held[0], "ps%d" % held[1]]
                        for half, (pa, pbb, op) in enumerate(((p1, p2, ALU.subtract), (p2, p1, ALU.add))):
                            tt(t1[:, 0:tw], pa, cosT[:, 0:tw], ALU.mult, k12 + ["cosT"], ["t1"])
                            tt(t2[:, 0:tw], pbb, sinT[:, 0:tw], ALU.mult, k12 + ["sinT"], ["t2"])
                            if is_k:
                                tt(kf32[:, half, 0:tw], t1[:, 0:tw], t2[:, 0:tw], op, ["t1", "t2"], ["kf32"])
                                if not so:
                                    act(kbf[:, half, 0:tw], kf32[:, half, 0:tw], AF.Copy, ["kf32"], ["kbf"])
                            else:
                                tt(t1[:, 0:tw], t1[:, 0:tw], t2[:, 0:tw], op, ["t1", "t2"], ["t1"])
                                act(qrot[:, half, 0:tw], t1[:, 0:tw], AF.Copy, ["t1"], ["qrot"])
                                tt(qhat[:, half, 0:tw].rearrange("p (c t) -> p c t", c=cpt), t1[:, 0:tw].rearrange("p (c t) -> p c t", c=cpt),
                                   bc_mid(K["xib"][:, h, :], cpt), ALU.mult, ["t1", "c_xib"], ["qhat"])
                    return cb
                if not so:
                    gemm_fm(wi, 0, 2, hT, "hT", range(KC), tw, rope_cb(False))
                gemm_fm(wi, kcol, 2, hT, "hT", range(KC), tw, rope_cb(True))
                for c in range(cpt):
                    pb = bank()
                    tr_multi([(PS[pb][:, k2 * 128:(k2 + 1) * 128], kf32[:, k2, c * 128:(c + 1) * 128]) for k2 in range(2)], ["kf32"], "ps%d" % pb)
                    act(kzt[:, c, :], PS[pb][:, 0:256], AF.Identity, ["ps%d" % pb, "c_zeta"], ["kzt"], scale=K["zeta"][:, h:h + 1])
                wi2 = wload([(w_in, 0, 2048 + h * 256, 256)] + ([] if so else [(w_in, 0, 3072 + h * 256, 256)]))

                def vg_cb(c, pb):
                    gi = ch0 + c
                    act(vret[:, c, :], PS[pb][:, 0:256], AF.Identity, ["ps%d" % pb, "vmask"], ["vret"], scale=vmask[:, gi:gi + 1])
                    if not so:
                        act(sgr[:, c, :], PS[pb][:, 256:512], AF.Silu, ["ps%d" % pb], ["sgr"])
                gemm_tm(wi2, 0, 256 if so else 512, hT, "hT", range(KC), cpt, vg_cb)
                for c in range(cpt):
                    cs = slice(c * 128, (c + 1) * 128)
                    if not so:
                        pbs = bank()
                        mm_group(PS[pbs][:, 0:128], [(kbf[:, k2, cs], qrot[:, k2, cs]) for k2 in range(2)], ["kbf", "qrot"], "ps%d" % pbs)
                        tt(inn[:], PS[pbs][:, 0:128], K["decT"][:, h, :], ALU.mult, ["ps%d" % pbs, "c_decT"], ["inn"])
                        pbo = bank()
                        pairs = [(inn[:], vret[:, c, :])] + [(qhat[:, k2, cs], Rbf[:, h, k2 * 256:(k2 + 1) * 256]) for k2 in range(2)]
                        mm_group(PS[pbo][:, 0:256], pairs, ["inn", "vret", "qhat", "Rbf"], "ps%d" % pbo)
                        st_ = small[:, 16:24]
                        o_ = osb[:, 0:256]
                        act(o_, PS[pbo][:, 0:256], AF.Copy, ["ps%d" % pbo], ["osb", "st0"], accum_out=st_[:, 0:1])
                        act(osq[:, 0:256], o_, AF.Square, ["osb"], ["osq", "st1"], accum_out=st_[:, 1:2])
                        ts(st_[:, 2:3], st_[:, 0:1], 1.0 / 256, None, ALU.mult, None, ["st0"], ["st2"])
                        tt(st_[:, 3:4], st_[:, 2:3], st_[:, 2:3], ALU.mult, ["st2"], ["st3"])
                        stt(st_[:, 4:5], st_[:, 1:2], 1.0 / 256, st_[:, 3:4], ALU.mult, ALU.subtract, ["st1", "st3"], ["st4"])
                        rsqrt_small(st_[:, 5:6], st_[:, 4:5], 1.0, HEPS, ["st4"], "st5")
                        ts(o_, o_, st_[:, 2:3], st_[:, 5:6], ALU.subtract, ALU.mult, ["osb", "st2", "st5"], ["osb"])
                        tt(o_, o_, sgr[:, c, :], ALU.mult, ["osb", "sgr"], ["osb"])
                        pbt = bank()
                        tr_multi([(PS[pbt][:, k2 * 128:(k2 + 1) * 128], osb[:, k2 * 128:(k2 + 1) * 128]) for k2 in range(2)], ["osb"], "ps%d" % pbt)
                        for k2 in range(2):
                            act(ogT[:, h * 2 + k2, cs], PS[pbt][:, k2 * 128:(k2 + 1) * 128], AF.Copy, ["ps%d" % pbt], ["ogT"])
                    pbr = bank()
                    mm_multi([(PS[pbr][:, k2 * 256:(k2 + 1) * 256], kzt[:, c, k2 * 128:(k2 + 1) * 128], vret[:, c, :]) for k2 in range(2)],
                             ["kzt", "vret"], "ps%d" % pbr)
                    stt(Rst[:, h, :], Rst[:, h, :], gchunk[h], PS[pbr][:, 0:512], ALU.mult, ALU.add, ["Rst", "ps%d" % pbr], ["Rst"])
                    if not so:
                        act(Rbf[:, h, :], Rst[:, h, :], AF.Copy, ["Rst"], ["Rbf"])

            def hg_group(g, ch0, cpt, so):
                tw = cpt * 128
                c0 = 4096 + g * 512
                wif = wload([(w_in, 0, c0 + 1024, 512)])

                o1 = L.get("_off_t1")
                alt = [AR[:, o1 + i * 512: o1 + (i + 1) * 512] for i in range(4)]
                BS = [dict(sgm=sgm, ktm=ktm, logf=logf, ecb=ecb, k="", g=[]),
                      dict(sgm=alt[0], ktm=alt[1], logf=alt[2], ecb=alt[3], k="_b", g=["t1", "t2", "kf32"])]

                def hf_p1(c, pb, B, first):
                    kx, gd = B["k"], B["g"]
                    act(B["sgm"], PS[pb][:, 0:512], AF.Sigmoid, ["ps%d" % pb], ["sgm" + kx] + (gd if first else []))
                    tt(B["sgm"], B["sgm"], OMLb[:, g * 512:(g + 1) * 512], ALU.mult, ["sgm" + kx, "OMLb"] + gd, ["sgm" + kx])
                    tt(B["sgm"], B["sgm"], LBb[:, g * 512:(g + 1) * 512], ALU.add, ["sgm" + kx, "LBb"] + gd, ["sgm" + kx])
                    ts(B["ktm"], B["sgm"], -1.0, 1.0, ALU.mult, ALU.add, ["sgm" + kx] + gd, ["ktm" + kx])
                    act(B["logf"], B["sgm"], AF.Ln, ["sgm" + kx] + gd, ["logf" + kx])

                def hf_p2(c, B):
                    kx, gd = B["k"], B["g"]
                    pbc = bank()
                    mm_multi([(PS[pbc][:, 0:512], K["trirev"][:], B["logf"])], ["c_trirev", "logf" + kx] + gd, "ps%d" % pbc)
                    act(B["ecb"], PS[pbc][:, 0:512], AF.Exp, ["ps%d" % pbc] + gd, ["ecb" + kx])
                    tt(khat[:, c, :], B["ktm"], B["ecb"], ALU.mult, ["ktm" + kx, "ecb" + kx] + gd, ["khat"])
                    pbb = bank()
                    mm_multi([(PS[pbb][:, hh * 128:(hh + 1) * 128], B["logf"][:, hh * 128:(hh + 1) * 128], K["triinc"][:]) for hh in range(4)],
                             ["logf" + kx, "c_triinc"] + gd, "ps%d" % pbb)
                    act(eb[:, c, :], PS[pbb][:, 0:512], AF.Exp, ["ps%d" % pbb], ["eb"])
                    if not so:
                        act(enb[:, c, :], PS[pbb][:, 0:512], AF.Exp, ["ps%d" % pbb], ["enb"], scale=-1.0)

                use_alt = so and cpt >= 2 and o1 is not None
                seen_alt = [False]

                def hf_cb(c, pb):
                    B = BS[c % 2] if use_alt else BS[0]
                    first = False
                    if B is BS[1] and not seen_alt[0]:
                        first = True
                        seen_alt[0] = True
                    hf_p1(c, pb, B, first)
                    if use_alt:
                        if c >= 1:
                            hf_p2(c - 1, BS[(c - 1) % 2])
                        if c == cpt - 1:
                            hf_p2(c, B)
                    else:
                        hf_p2(c, B)
                gemm_tm(wif, 0, 512, hT, "hT", range(KC), cpt, hf_cb)
                if not so:
                    def hfT_cb(mb, pb):
                        act(t1[:, 0:tw], PS[pb][:, 0:tw], AF.Sigmoid, ["ps%d" % pb], ["t1"], scale=-1.0)
                        stt(kt[:, mb, 0:tw].rearrange("p (c t) -> p c t", c=cpt), t1[:, 0:tw].rearrange("p (c t) -> p c t", c=cpt),
                            omlf[:, g * 4 + mb: g * 4 + mb + 1], enb[:, 0:cpt, mb * 128:(mb + 1) * 128], ALU.mult, ALU.mult, ["t1", "omlf", "enb"], ["kt"])
                    gemm_fm(wif, 0, 4, hT, "hT", range(KC), tw, hfT_cb)
                    wiq = wload([(w_in, 0, c0, 512)])

                    def hqT_cb(mb, pb):
                        act(t2[:, 0:tw], PS[pb][:, 0:tw], AF.Silu, ["ps%d" % pb], ["t2"])
                        tt(qt[:, mb, 0:tw].rearrange("p (c t) -> p c t", c=cpt), t2[:, 0:tw].rearrange("p (c t) -> p c t", c=cpt),
                           eb[:, 0:cpt, mb * 128:(mb + 1) * 128], ALU.mult, ["t2", "eb"], ["qt"])
                    gemm_fm(wiq, 0, 4, hT, "hT", range(KC), tw, hqT_cb)
                wiv = wload([(w_in, 0, c0 + 2048, 512)])

                def v_cb(c, pb):
                    gi = ch0 + c
                    act(vhg[:, c, :], PS[pb][:, 0:512], AF.Identity, ["ps%d" % pb, "vmask"], ["vhg"], scale=vmask[:, gi:gi + 1])
                gemm_tm(wiv, 0, 512, hT, "hT", range(KC), cpt, v_cb)
                if not so:
                    wig = wload([(w_in, 0, c0 + 3072, 512)])

                    def g_cb(c, pb):
                        act(sgate[:, c, :], PS[pb][:, 0:512], AF.Silu, ["ps%d" % pb], ["sgate"])
                    gemm_tm(wig, 0, 512, hT, "hT", range(KC), cpt, g_cb)
                if not so:
                    act(Sbf[:, 0, :, :], Sst[:, g * 4:g * 4 + 4, :], AF.Copy, ["Sst"], ["Sbf0"])
                for c in range(cpt):
                    cs = slice(c * 128, (c + 1) * 128)
                    if not so:
                        pba = bank()
                        mm_multi([(PS[pba][:, hh * 128:(hh + 1) * 128], kt[:, hh, cs], qt[:, hh, cs]) for hh in range(4)], ["kt", "qt"], "ps%d" % pba)
                        tt(amt[:], PS[pba][:, 0:512].rearrange("p (h t) -> p h t", h=4), bc_mid(K["triinc"][:], 4), ALU.mult,
                           ["ps%d" % pba, "c_triinc"], ["amt"])
                        for j in range(4):
                            tt(qz[:, j, :, :], qt[:, :, cs], bc_mid(K["colmask"][:, j, :], 4), ALU.mult, ["qt", "c_colmask"], ["qz"], eng="pool")

                    def sub_update(j):
                        act(kz[:], khat[:, c, :], AF.Identity, ["khat", "c_rowmask"], ["kz"], scale=K["rowmask"][:, j:j + 1])
                        pbd = bank()
                        mm_multi([(PS[pbd][:, hh * 128:(hh + 1) * 128], kz[:, hh * 128:(hh + 1) * 128], vhg[:, c, hh * 128:(hh + 1) * 128]) for hh in range(4)],
                                 ["kz", "vhg"], "ps%d" % pbd)
                        for hh in range(4):
                            hd = g * 4 + hh
                            col = hh * 128 + j * 32 + 31
                            stt(Sst[:, hd, :], Sst[:, hd, :], eb[:, c, col:col + 1], PS[pbd][:, hh * 128:(hh + 1) * 128], ALU.mult, ALU.add,
                                ["Sst", "eb", "ps%d" % pbd], ["Sst"])
                        jn = (j + 1) % 4
                        if not so:
                            act(Sbf[:, jn, :, :], Sst[:, g * 4:g * 4 + 4, :], AF.Copy, ["Sst"], ["Sbf%d" % jn])
                    for j in range(3):
                        sub_update(j)
                    if not so:
                        pbo = bank()

                        def f(e, pbo=pbo, c=c):
                            r = None
                            for hh in range(4):
                                o = PS[pbo][:, hh * 128:(hh + 1) * 128]
                                e.matmul(o, lhsT=amt[:, hh, :], rhs=vhg[:, c, hh * 128:(hh + 1) * 128], start=True, stop=False)
                                for j in range(4):
                                    r = e.matmul(o, lhsT=qz[:, j, hh, :], rhs=Sbf[:, j, hh, :], start=False, stop=(j == 3))
                            return r
                        S.op("pe", f, reads=["amt", "vhg", "qz", "Sbf0", "Sbf1", "Sbf2", "Sbf3"], writes=["ps%d" % pbo])
                        ss_ = small[:, 24:28]
                        rs_ = small[:, 28:32]
                        for hh in range(4):
                            act(osq[:, hh * 128:(hh + 1) * 128], PS[pbo][:, hh * 128:(hh + 1) * 128], AF.Square, ["ps%d" % pbo], ["osq", "hss"], accum_out=ss_[:, hh:hh + 1])
                        rsqrt_small(rs_, ss_, 1.0 / 128, HEPS, ["hss"], "hrs")
                        for hh in range(4):
                            stt(osb[:, hh * 128:(hh + 1) * 128], PS[pbo][:, hh * 128:(hh + 1) * 128], rs_[:, hh:hh + 1], sgate[:, c, hh * 128:(hh + 1) * 128],
                                ALU.mult, ALU.mult, ["ps%d" % pbo, "hrs", "sgate"], ["osb"])
                        pbt = bank()
                        tr_multi([(PS[pbt][:, hh * 128:(hh + 1) * 128], osb[:, hh * 128:(hh + 1) * 128]) for hh in range(4)], ["osb"], "ps%d" % pbt)
                        for hh in range(4):
                            act(ogT[:, 8 + g * 4 + hh, cs], PS[pbt][:, hh * 128:(hh + 1) * 128], AF.Copy, ["ps%d" % pbt], ["ogT"])
                    sub_update(3)


            def merge_and_out2(cpt):
                tw = cpt * 128
                for jb in range(4):
                    wa = wload([(w_in, 0, 8192 + jb * 512, 512)])
                    wb = wload([(w_in, 0, 10240 + jb * 512, 512)])
                    sig = []
                    for mb in range(4):
                        kcb = jb * 4 + mb
                        pga, pgb = bank(), bank()
                        mm_group(PS[pga][:, 0:tw], [(WS[wa][:, kc, mb * 128:(mb + 1) * 128], hT[:, kc, 0:tw]) for kc in range(KC)], wk(wa) + ["hT"], "ps%d" % pga)
                        mm_group(PS[pgb][:, 0:tw], [(WS[wb][:, kc, mb * 128:(mb + 1) * 128], hT[:, kc, 0:tw]) for kc in range(KC)], wk(wb) + ["hT"], "ps%d" % pgb)
                        act(sgA[:, mb, 0:tw], PS[pga][:, 0:tw], AF.Sigmoid, ["ps%d" % pga], ["eb"])
                        act(sgB[:, mb, 0:tw], PS[pgb][:, 0:tw], AF.Sigmoid, ["ps%d" % pgb], ["enb"])
                    wo = wload([(w_ro, 0, jb * 512, 512)], krows=1024)
                    srcb = WBF["w_hg_o"][:, jb * 512:(jb + 1) * 512].rearrange("(kc p) n -> p kc n", p=128)
                    S.dma("sp", "ws%d_1" % wo, (lambda e, wo=wo, srcb=srcb: e.dma_start(out=WS[wo][:, 8:16, :], in_=srcb)), writes=["ws%d_s1" % wo],
                          reads=["wb_w_hg_o_%d" % ((jb * 512) // PCB)])
                    for mb in range(4):
                        kcb = jb * 4 + mb
                        pya, pyb = bank(), bank()
                        mm_group(PS[pya][:, 0:tw], [(WS[wo][:, kc, mb * 128:(mb + 1) * 128], ogT[:, kc, 0:tw]) for kc in range(8)], wk(wo) + ["ogT"], "ps%d" % pya)
                        mm_group(PS[pyb][:, 0:tw], [(WS[wo][:, 8 + kc, mb * 128:(mb + 1) * 128], ogT[:, 8 + kc, 0:tw]) for kc in range(8)], wk(wo) + ["ogT"], "ps%d" % pyb)
                        tt(ua[:, 0:tw], sgA[:, mb, 0:tw], PS[pya][:, 0:tw], ALU.mult, ["eb", "ps%d" % pya], ["ua"])
                        tt(ub[:, 0:tw], sgB[:, mb, 0:tw], PS[pyb][:, 0:tw], ALU.mult, ["enb", "ps%d" % pyb], ["ub"])
                        tt(mgT[:, kcb, 0:tw], ua[:, 0:tw], ub[:, 0:tw], ALU.add, ["ua", "ub"], ["mgT"], eng="pool")
                load(Gb[:], mod_d[0:1, 2 * D:3 * D].partition_broadcast(128), "Gb", reads=["mod_d2"])
                for jb in range(4):
                    wo = wload([(w_out, 0, jb * 512, 512)])

                    gemm_tm(wo, 0, 512, mgT, "mgT", range(KC), cpt, lambda c, pb, jb=jb: resid_cb(c, pb, jb))

            def resid_cb(c, pb, jb):
                tt(osq[:, 0:512], PS[pb][:, 0:512], Gb[:, jb * 512:(jb + 1) * 512], ALU.mult, ["ps%d" % pb, "Gb"], ["osq"])
                tt(xt[:, c, jb * 512:(jb + 1) * 512], xt[:, c, jb * 512:(jb + 1) * 512], osq[:, 0:512], ALU.add, ["xt", "osq"], ["xt"], eng="pool")

            def ffn(ti, cpt, store=True):
                tw = cpt * 128
                for fb in range(11):
                    wa = wload([(w_up, 0, fb * 512, 512)])
                    wb = wload([(w_up, 0, DFF + fb * 512, 512)])
                    for mb in range(4):
                        blk = fb * 4 + mb
                        res = {}
                        for which, wi in ((0, wa), (1, wb)):
                            ch = which * NFB + blk
                            pb = bank()
                            mm_group(PS[pb][:, 0:tw], [(WS[wi][:, kc, mb * 128:(mb + 1) * 128], hT[:, kc, 0:tw]) for kc in range(KC)], wk(wi) + ["hT"], "ps%d" % pb)
                            if not store:
                                act(halo[:, ch, :], PS[pb][:, tw - 2:tw], AF.Copy, ["ps%d" % pb], ["halo"])
                                continue
                            yb = ybuf[:, which, :]
                            yk = "ybuf%d" % which
                            act(yb[:, 2:2 + tw], PS[pb][:, 0:tw], AF.Copy, ["ps%d" % pb], [yk])
                            ts(yb[:, 0:2], halo[:, ch, :], pmask[:, ti:ti + 1], None, ALU.mult, None, ["halo", "pmask"], [yk], eng="pool")
                            u = ua if which == 0 else ub
                            uk = "ua" if which == 0 else "ub"
                            act(u[:, 0:tw], yb[:, 2:2 + tw], AF.Identity, [yk, "cwf"], [uk], scale=cwf[:, ch, 2:3], bias=cwf[:, ch, 3:4])
                            stt(u[:, 0:tw], yb[:, 1:1 + tw], cwf[:, ch, 1:2], u[:, 0:tw], ALU.mult, ALU.add, [yk, "cwf", uk], [uk])
                            stt(u[:, 0:tw], yb[:, 0:tw], cwf[:, ch, 0:1], u[:, 0:tw], ALU.mult, ALU.add, [yk, "cwf", uk], [uk])
                            S.op("pool", lambda e, yb=yb, ch=ch: e.tensor_copy(halo[:, ch, :], yb[:, tw:tw + 2]), reads=[yk], writes=["halo"])
                        if store:
                            act(t1[:, 0:tw], ua[:, 0:tw], AF.Silu, ["ua"], ["t1"])
                            tt(gT[:, blk, 0:tw], t1[:, 0:tw], ub[:, 0:tw], ALU.mult, ["t1", "ub"], ["gT"])
                if ti == 0:
                    store_dbg("gT", gT[:], "gT")
                if not store:
                    return
                load(Gb[:], mod_d[0:1, 5 * D:6 * D].partition_broadcast(128), "Gb", reads=["mod_d5"])
                for jb in range(4):
                    pbs = [bank() for _ in range(cpt)]
                    parts = [(0, 16), (16, 16), (32, 12)]
                    for pi, (f0, nf) in enumerate(parts):
                        wd = wload([(w_dn, f0 * 128, jb * 512, 512)], krows=nf * 128)
                        for c in range(cpt):
                            def f(e, c=c, wd=wd, f0=f0, nf=nf, pi=pi, pbs=pbs):
                                r = None
                                for k in range(nf):
                                    r = e.matmul(PS[pbs[c]][:, 0:512], lhsT=gT[:, f0 + k, c * 128:(c + 1) * 128], rhs=WS[wd][:, k, 0:512],
                                                 start=(pi == 0 and k == 0), stop=(pi == 2 and k == nf - 1))
                                return r
                            S.op("pe", f, reads=wk(wd) + ["gT"], writes=["ps%d" % pbs[c]])
                    for c in range(cpt):
                        resid_cb(c, pbs[c], jb)

            def final_store(cpt, row0):
                if row0 == 0:
                    store_dbg("x2", xt[:], "xt")
                load(Gb[:], gf_d.partition_broadcast(128), "Gb")
                ss = small[:, 0:cpt]
                rs = small[:, 8:8 + cpt]
                for c in range(cpt):
                    act(sqjb[:, 0:2048], xt[:, c, :], AF.Square, ["xt"], ["sqj", "ss"], accum_out=ss[:, c:c + 1])
                rsqrt_small(rs, ss, 1.0 / D, EPS, ["ss"], "rs")
                for c in range(cpt):
                    stt(xt[:, c, :], xt[:, c, :], rs[:, c:c + 1], Gb[:], ALU.mult, ALU.mult, ["xt", "rs", "Gb"], ["xt"])
                    S.dma("sp", "st_out", lambda e, c=c: e.dma_start(out=out_d[row0 + c * 128: row0 + (c + 1) * 128, :], in_=xt[:, c, :]), reads=["xt"])


            return dict(norm_to_hT=norm_to_hT, rope_tables=rope_tables, ret_head=ret_head, hg_group=hg_group,
                        merge_and_out2=merge_and_out2, ffn=ffn, final_store=final_store, xt=xt)

        OPS = {"s": make_ops(LS), "f": make_ops(LF)}
        print("SBUF bytes/partition (final):", sbytes[0])
        ch0 = 0
        prev_mode = None
        S.barrier()
        for ti, (cpt, mode, row0) in enumerate(tiles):
            so = mode == "s"
            if mode == "f" and prev_mode != "f":
                while precast_todo:
                    precast(*precast_todo.pop(0))
                late_mod()
            if prev_mode is not None and prev_mode != mode:
                S.barrier()
            prev_mode = mode
            O = OPS[mode]
            for c in range(cpt):
                load(O["xt"][:, c, :], x_d[(ch0 + c) * 128:(ch0 + c + 1) * 128, :], "xt")
            O["norm_to_hT"](0, 1, cpt)
            if so and precast_todo:
                precast(*precast_todo.pop(0))
            O["rope_tables"](ch0 * 128, cpt * 128)
            if ti == 0 and "cos" in dbg_d:
                store_dbg("cos", (LS if so else LF)["cosT"], "cosT")
                store_dbg("sin", (LS if so else LF)["sinT"], "sinT")
                store_dbg("sqj", (LS if so else LF)["sqj"], "sqj")
            for h in range(RET_H):
                O["ret_head"](h, ch0, cpt, so)
                if ti == 0 and h == 0 and "kf32" in dbg_d:
                    store_dbg("kf32", (LS if so else LF)["kf32"], "kf32")
                    store_dbg("R0", Rst[:, 0, :], "Rst")
            for g in range(2):
                O["hg_group"](g, ch0, cpt, so)
            if not so and row0 == 0 and "ogT" in dbg_d:
                store_dbg("ogT", LF["ogT"], "ogT")
            if not so:
                O["merge_and_out2"](cpt)
                O["norm_to_hT"](2, 3, cpt)
                O["ffn"](ti, cpt, row0 is not None)
                if row0 is not None:
                    O["final_store"](cpt, row0)
            if so and gemv_todo:
                gemv_block(*gemv_todo.pop(0))
            ch0 += cpt

        fin = [(k, v) for k, v in S.dma_cnt.items() if k.startswith("st_") or k.startswith("dbg_")]
        S.wait_all("sp", fin)
        with nc.Block() as block:
            S.replay(block)
    return nc


CPS = 4


def make_tiles(nstate_chunks, pre_chunks, own_chunks):
    tiles = []
    n = nstate_chunks
    while n > 0:
        c = min(CPS, n)
        tiles.append((c, "s", None))
        n -= c
    n = pre_chunks
    while n > 0:
        c = min(CPTMAX, n)
        tiles.append((c, "f", None))
        n -= c
    row = 0
    n = own_chunks
    while n > 0:
        c = min(CPTMAX, n)
        tiles.append((c, "f", row))
        row += c * 128
        n -= c
    return tiles


_CACHE = {}


def kernel(x, c, positions, w_ada, b_ada, g_norm1, w_in, w_ret_o, w_hg_o, w_out,
           hg_lb, g_norm2, w_up, conv_w, conv_b, w_down, g_final):
    own = SEQ // NCORES // 128
    pre = 1
    nstate = (NCORES - 1) * own - pre
    tiles = make_tiles(nstate, pre, own)
    if "nc" not in _CACHE:
        _CACHE["nc"] = build(tiles)
    nc = _CACHE["nc"]
    nch = nstate + pre + own
    ntok = nch * 128
    xs = np.asarray(x, np.float32).reshape(SEQ, D)
    ps = np.asarray(positions, np.int32).reshape(SEQ)
    HC = host_consts()
    shared = {
        "c": np.asarray(c, np.float32).reshape(1, D),
        "w_ada": np.ascontiguousarray(np.asarray(w_ada, np.float32).reshape(D, 6 * D)),
        "b_ada": np.asarray(b_ada, np.float32).reshape(1, 6 * D),
        "g_norm1": np.asarray(g_norm1, np.float32).reshape(1, D),
        "w_in": np.ascontiguousarray(np.asarray(w_in, np.float32).reshape(D, IN_COLS)),
        "w_ret_o": np.ascontiguousarray(np.asarray(w_ret_o, np.float32).reshape(1024, D)),
        "w_hg_o": np.ascontiguousarray(np.asarray(w_hg_o, np.float32).reshape(1024, D)),
        "w_out": np.ascontiguousarray(np.asarray(w_out, np.float32).reshape(D, D)),
        "hg_lb": np.asarray(hg_lb, np.float32).reshape(2, 1024),
        "g_norm2": np.asarray(g_norm2, np.float32).reshape(1, D),
        "w_up": np.ascontiguousarray(np.asarray(w_up, np.float32).reshape(D, 2 * DFF)),
        "conv_w": np.asarray(conv_w, np.float32).reshape(3, 2 * DFF),
        "conv_b": np.asarray(conv_b, np.float32).reshape(1, 2 * DFF),
        "w_down": np.ascontiguousarray(np.asarray(w_down, np.float32).reshape(DFF, D)),
        "g_final": np.asarray(g_final, np.float32).reshape(1, D),
    }
    for k in CONST_SHAPES:
        shared["k_" + k] = np.ascontiguousarray(HC[k]).reshape(CONST_SHAPES[k])
    in_maps = []
    for core in range(NCORES):
        nreal = (core + 1) * own * 128
        xc = np.zeros((ntok, D), np.float32)
        xc[ntok - nreal:] = xs[:nreal]
        pc = np.zeros((1, ntok), np.int32)
        pc[0, ntok - nreal:] = ps[:nreal]
        tokmask = np.zeros(ntok, np.float32)
        tokmask[ntok - nreal:] = 1.0
        vm = np.ascontiguousarray(tokmask.reshape(nch, 128).T)
        pm = np.zeros((128, len(tiles)), np.float32)
        chs = 0
        for ti, (cpt, mode, row0) in enumerate(tiles):
            prev_real = 1.0 if (chs * 128 - 1) >= (ntok - nreal) else 0.0
            pm[:, ti] = prev_real
            chs += cpt
        m = dict(shared)
        m.update({"x": xc, "positions": pc, "vmask": vm, "pmask": pm})
        in_maps.append(m)
    res = run_bass_kernel_spmd(nc, in_maps, core_ids=list(range(NCORES)))
    outs = [np.asarray(r["out"])[: own * 128] for r in res.results]
    return np.concatenate(outs, axis=0).reshape(1, SEQ, D).astype(np.float32)
```

```python
import contextlib
import numpy as np
import concourse.bass as bass
import concourse.mybir as mybir
from concourse.bass_utils import run_bass_kernel_spmd

F32 = mybir.dt.float32
BF16 = mybir.dt.bfloat16
I32 = mybir.dt.int32
ALU = mybir.AluOpType
AF = mybir.ActivationFunctionType

D = 2048
KC = 16
SEQ = 16384
NCORES = 8
CPTMAX = 2
TMAX = 128 * CPTMAX
DFF = 5632
NFB = DFF // 128
RET_H, HG_H = 4, 8
IN_COLS = 12288
EPS = 1e-6
HEPS = 1e-5


class Sched:
    ENGS = ("pe", "act", "dve", "pool", "sp")

    def __init__(self, nc, stack):
        self.nc = nc
        self.stack = stack
        self.q = {e: [] for e in self.ENGS}
        self.cnt = {e: 0 for e in self.ENGS}
        self.known = {e: {} for e in self.ENGS}
        self.last_w = {}
        self.readers = {}
        self.sems = {}
        self.dma_cnt = {}
        for e in ("pe", "act", "dve", "pool"):
            self.sems[e] = stack.enter_context(nc.semaphore("c_" + e))

    def _sem(self, key):
        if key not in self.sems:
            self.sems[key] = self.stack.enter_context(self.nc.semaphore("d_" + str(key)))
            self.dma_cnt[key] = 0
        return self.sems[key]

    def _deps(self, eng, reads, writes, extra=()):
        need = {}

        def add(t):
            if t is None:
                return
            sk, v = t
            if need.get(sk, 0) < v:
                need[sk] = v
        for k in reads:
            add(self.last_w.get(k))
        for k in writes:
            add(self.last_w.get(k))
            for t in self.readers.get(k, ()):
                add(t)
        for t in extra:
            add(t)
        waits = []
        for sk, v in need.items():
            if sk == eng and eng == "pe":
                continue
            if self.known[eng].get(sk, 0) >= v:
                continue
            self.known[eng][sk] = v
            waits.append((sk, v))
        return waits

    def _commit(self, tok, reads, writes):
        for k in reads:
            self.readers.setdefault(k, []).append(tok)
        for k in writes:
            self.last_w[k] = tok
            self.readers[k] = []

    def op(self, eng, fn, reads=(), writes=()):
        waits = self._deps(eng, reads, writes)
        self.cnt[eng] += 1
        tok = (eng, self.cnt[eng])
        self.q[eng].append((fn, waits, (eng, 1)))
        self._commit(tok, reads, writes)
        return tok

    def dma(self, eng, slot, fn, reads=(), writes=()):
        self._sem(slot)
        waits = self._deps(eng, reads, writes)
        self.dma_cnt[slot] += 16
        tok = (slot, self.dma_cnt[slot])
        self.q[eng].append((fn, waits, (slot, 16)))
        self._commit(tok, reads, writes)
        return tok

    def barrier(self):
        toks = [(e, self.cnt[e]) for e in ("pe", "act", "dve", "pool") if self.cnt[e] > 0]
        toks += [(k, v) for k, v in self.dma_cnt.items() if v > 0]
        for e in self.ENGS:
            self.wait_all(e, toks)

    def wait_all(self, eng, toks):
        waits = self._deps(eng, (), (), toks)
        self.q[eng].append((None, waits, None))

    def replay(self, block):
        engmap = {"pe": block.tensor, "act": block.scalar, "dve": block.vector,
                  "pool": block.gpsimd, "sp": block.sync}
        sems = self.sems
        for e in self.ENGS:
            def body(engine, items=self.q[e]):
                for fn, waits, inc in items:
                    for sk, v in waits:
                        engine.wait_ge(sems[sk], v)
                    if fn is None:
                        continue
                    ins = fn(engine)
                    ins.then_inc(sems[inc[0]], inc[1])
            engmap[e](body)


def host_consts():
    c = {}
    c["ident"] = np.eye(128, dtype=np.float32)
    idx = np.arange(128, dtype=np.float64)
    gam = 1.0 - np.exp2(-5.0 - np.arange(RET_H, dtype=np.float64))
    lg = np.log(gam)
    diff = idx[None, :] - idx[:, None]
    dec = np.where(diff >= 0, np.exp(np.maximum(diff, 0)[None] * lg[:, None, None]), 0.0)
    c["decT"] = np.ascontiguousarray((dec * 256 ** -0.5).transpose(1, 0, 2)).astype(np.float32)
    xi = np.exp((idx[None, :] + 1.0) * lg[:, None])
    c["xib"] = np.ascontiguousarray(np.broadcast_to(xi[None], (128, RET_H, 128))).astype(np.float32)
    zeta = np.exp((127.0 - idx[None, :]) * lg[:, None]) * 256 ** -0.5
    c["zeta"] = np.ascontiguousarray(zeta.T).astype(np.float32)
    c["gchunk"] = np.exp(128 * lg)
    sub = (np.arange(128) // 32)
    same = sub[:, None] == sub[None, :]
    s = np.arange(128)
    c["triinc"] = (same & (s[:, None] <= s[None, :])).astype(np.float32)
    c["trirev"] = (same & (s[:, None] > s[None, :])).astype(np.float32)
    c["rowmask"] = (sub[:, None] == np.arange(4)[None, :]).astype(np.float32)
    c["colmask"] = np.ascontiguousarray(np.broadcast_to((sub[None, :] == np.arange(4)[:, None])[None], (128, 4, 128))).astype(np.float32)
    c["invf"] = (10000.0 ** (-np.arange(0, 256, 2, dtype=np.float32) / 256)).astype(np.float32)[:, None]
    c["neghalf"] = np.full((128, 8), -0.5, np.float32)
    return c


CONST_SHAPES = {"ident": [128, 128], "decT": [128, 4, 128], "xib": [128, 4, 128], "zeta": [128, 4],
                "triinc": [128, 128], "trirev": [128, 128], "rowmask": [128, 4], "colmask": [128, 4, 128],
                "invf": [128, 1], "neghalf": [128, 8]}


def bc_mid(a, n):
    return bass.AP(a.tensor, a.offset, [list(a.ap[0]), [0, n], list(a.ap[-1])])


def build(tiles, dbg=None):
    NCH = sum(t[0] for t in tiles)
    NTOK = NCH * 128
    NT = len(tiles)
    NOUT = sum(t[0] * 128 for t in tiles if t[1] == "f" and t[2] is not None)
    HC = host_consts()
    gchunk = [float(v) for v in HC["gchunk"]]
    nc = bass.Bass("TRN2", target_bir_lowering=False)
    din = lambda n, s, dt=F32: nc.dram_tensor(n, s, dt, kind="ExternalInput").ap()
    x_d = din("x", [NTOK, D])
    pos_d = din("positions", [1, NTOK], I32)
    vmask_d = din("vmask", [128, NCH])
    pmask_d = din("pmask", [128, NT])
    c_d = din("c", [1, D])
    w_ada = din("w_ada", [D, 6 * D])
    b_ada = din("b_ada", [1, 6 * D])
    g1_d = din("g_norm1", [1, D])
    w_in = din("w_in", [D, IN_COLS])
    w_ro = din("w_ret_o", [1024, D])
    w_ho = din("w_hg_o", [1024, D])
    w_out = din("w_out", [D, D])
    lb_d = din("hg_lb", [2, 1024])
    g2_d = din("g_norm2", [1, D])
    w_up = din("w_up", [D, 2 * DFF])
    cw_d = din("conv_w", [3, 2 * DFF])
    cb_d = din("conv_b", [1, 2 * DFF])
    w_dn = din("w_down", [DFF, D])
    gf_d = din("g_final", [1, D])
    cd = {k: din("k_" + k, s) for k, s in CONST_SHAPES.items()}
    out_d = nc.dram_tensor("out", [max(NOUT, 128), D], F32, kind="ExternalOutput").ap()
    mod_d = nc.dram_tensor("mod_scratch", [1, 6 * D], F32).ap()
    WSRC = {"w_in": w_in, "w_ret_o": w_ro, "w_hg_o": w_ho, "w_out": w_out, "w_up": w_up, "w_down": w_dn}
    WBF = {k: nc.dram_tensor("bf_" + k, list(v.shape), BF16).ap() for k, v in WSRC.items()}
    WNAME = {id(v): k for k, v in WSRC.items()}
    PCB = 1024
    dbg_d = {}
    if dbg:
        for k, (s, dt_) in dbg.items():
            dbg_d[k] = nc.dram_tensor("dbg_" + k, s, dt_, kind="ExternalOutput").ap()

    with contextlib.ExitStack() as st:
        S = Sched(nc, st)
        sbytes = [0]

        def sb(name, shape, dt=F32):
            n = int(np.prod(shape[1:])) * (4 if dt in (F32, I32) else 2)
            sbytes[0] += n
            return st.enter_context(nc.sbuf_tensor(name, shape, dt))
        K = {k: sb("c_" + k, s) for k, s in CONST_SHAPES.items()}
        WS = [sb("ws%d" % i, [128, KC, 512], BF16) for i in range(2)]
        AB = sb("AB", [128, 4, KC])
        LBb = sb("LBb", [128, 1024])
        OMLb = sb("OMLb", [128, 1024])
        omlf = sb("omlf", [128, 8])
        cwf = sb("cwf", [128, 2 * NFB, 4])
        halo = sb("halo", [128, 2 * NFB, 2])
        vmask = sb("vmask_s", [128, NCH])
        pmask = sb("pmask_s", [128, NT])
        Rst = sb("Rst", [128, RET_H, 512])
        Rbf = sb("Rbf", [128, RET_H, 512], BF16)
        Sst = sb("Sst", [128, HG_H, 128])
        Sbf = sb("Sbf", [128, 4, 4, 128], BF16)
        small = sb("small", [128, 96])

        def spec(cp, full):
            T_ = cp * 128
            L = [("xt", [128, cp, D], F32), ("hT", [128, KC, T_], BF16), ("cosT", [128, T_], F32), ("sinT", [128, T_], F32),
                 ("posi", [128, min(T_, 256)], I32), ("sqj", [128, 1024], F32), ("dgs", [128, cp, 128], F32),
                 ("t1", [128, T_], F32), ("t2", [128, T_], F32), ("kf32", [128, 2, T_], F32),
                 ("logf", [128, 512], F32), ("ktm", [128, 512], F32), ("sgm", [128, 512], F32), ("ecb", [128, 512], F32),
                 ("eb", [128, cp, 512], F32), ("kbf", [128, 2, T_], BF16), ("kzt", [128, cp, 256], BF16), ("vret", [128, cp, 256], BF16),
                 ("khat", [128, cp, 512], BF16), ("vhg", [128, cp, 512], BF16), ("kz", [128, 512], BF16)]
            if full:
                L += [("ogT", [128, KC, T_], BF16), ("mgT", [128, KC, T_], BF16), ("gT", [128, NFB, T_], BF16), ("Gb", [128, D], F32),
                      ("sgr", [128, cp, 256], F32), ("osb", [128, 512], F32), ("osq", [128, 512], F32), ("enb", [128, cp, 512], F32),
                      ("sgate", [128, cp, 512], F32), ("ybuf", [128, 2, T_ + 2], F32), ("ua", [128, T_], F32), ("ub", [128, T_], F32),
                      ("qrot", [128, 2, T_], BF16), ("qhat", [128, 2, T_], BF16), ("inn", [128, 128], BF16), ("qt", [128, 4, T_], BF16),
                      ("kt", [128, 4, T_], BF16), ("amt", [128, 4, 128], BF16), ("qz", [128, 4, 4, 128], BF16)]
            return L

        def nf32(shape, dt_):
            n = int(np.prod(shape[1:]))
            return (n + 1) // 2 if dt_ == BF16 else n

        CPS_ = max([t[0] for t in tiles if t[1] == "s"] + [1])
        CPF_ = max([t[0] for t in tiles if t[1] == "f"] + [1])
        ls_tot = sum(nf32(sh, d_) for _, sh, d_ in spec(CPS_, False))
        lf_tot = sum(nf32(sh, d_) for _, sh, d_ in spec(CPF_, True))
        need = max(ls_tot + 6656, lf_tot)
        AR = sb("arena", [128, need])

        def layout(cp, full):
            L = {}
            off = 0
            for name, shape, dt_ in spec(cp, full):
                n = int(np.prod(shape[1:]))
                w = nf32(shape, dt_)
                v = AR[:, off:off + w]
                if dt_ == BF16:
                    v = v.bitcast(BF16)[:, 0:n]
                elif dt_ == I32:
                    v = v.bitcast(I32)
                if len(shape) == 3:
                    v = v.rearrange("p (a b) -> p a b", a=shape[1])
                elif len(shape) == 4:
                    v = v.rearrange("p (a b c) -> p a b c", a=shape[1], b=shape[2])
                L[name] = v
                L["_off_" + name] = off
                off += w
            return L
        LF = layout(CPF_, True)
        LS = layout(CPS_, False)
        xt, Gb, sqj, gT = LF["xt"], LF["Gb"], LF["sqj"], LF["gT"]
        tgA = AR[:, ls_tot:ls_tot + 2048]
        tgB = AR[:, ls_tot + 2048:ls_tot + 4096]
        mrow = AR[0:1, ls_tot + 4096:ls_tot + 6144]
        brow2 = AR[0:1, ls_tot + 6144:ls_tot + 6656]
        kz2 = AR[:, ls_tot + 6656:ls_tot + 6912].bitcast(BF16)
        assert need >= ls_tot + 6912
        sqjb = sqj.bitcast(BF16)
        rowb = Gb
        print("SBUF bytes/partition:", sbytes[0])
        PS = [st.enter_context(nc.psum_tensor("ps%d" % i, [128, 512], F32)) for i in range(8)]
        psn = [0]

        def bank():
            i = psn[0] % 8
            psn[0] += 1
            return i

        def load(dst_ap, src_ap, key, eng="sp", reads=()):
            return S.dma(eng, "ld_" + key, lambda e: e.dma_start(out=dst_ap, in_=src_ap, allow_slow_non_contiguous=True), writes=[key], reads=reads)

        wsn = [0]

        def wload(segs, krows=D):
            i = wsn[0] % 2
            wsn[0] += 1
            ws = WS[i]
            off = 0
            kc = krows // 128
            for si, (w, row0, c0, n) in enumerate(segs):
                name = WNAME[id(w)]
                src = WBF[name][row0:row0 + krows, c0:c0 + n].rearrange("(kc p) n -> p kc n", p=128)
                dst = ws[:, 0:kc, off:off + n]
                rk = sorted(set(["wb_%s_%d" % (name, c // PCB) for c in (c0, c0 + n - 1)]))
                S.dma("sp", "ws%d_%d" % (i, si), (lambda e, dst=dst, src=src: e.dma_start(out=dst, in_=src)), writes=["ws%d_s%d" % (i, si)], reads=rk)
                off += n
            return i

        def wk(i):
            return ["ws%d_s0" % i, "ws%d_s1" % i]

        def mm_group(out_ap, pairs, reads, wkey):
            def f(e):
                r = None
                n = len(pairs)
                for i, (l, rh) in enumerate(pairs):
                    r = e.matmul(out_ap, lhsT=l, rhs=rh, start=(i == 0), stop=(i == n - 1))
                return r
            return S.op("pe", f, reads=reads, writes=[wkey])

        def mm_multi(items, reads, wkey):
            def f(e):
                r = None
                for (o, l, rh) in items:
                    r = e.matmul(o, lhsT=l, rhs=rh, start=True, stop=True)
                return r
            return S.op("pe", f, reads=reads, writes=[wkey])

        def tr_multi(items, reads, wkey):
            def f(e):
                r = None
                for (o, i_) in items:
                    r = e.transpose(o, i_, K["ident"][:])
                return r
            return S.op("pe", f, reads=list(reads) + ["c_ident"], writes=[wkey])

        def act(out, in_, func, reads, writes, **kw):
            return S.op("act", lambda e: e.activation(out, in_, func, **kw), reads=reads, writes=writes)

        def tt(out, a, b, op, reads, writes, eng="dve"):
            return S.op(eng, lambda e: e.tensor_tensor(out, a, b, op), reads=reads, writes=writes)

        def ts(out, a, s1, s2, op0, op1, reads, writes, eng="dve"):
            if op1 is None:
                return S.op(eng, lambda e: e.tensor_scalar(out, a, s1, s2, op0), reads=reads, writes=writes)
            return S.op(eng, lambda e: e.tensor_scalar(out, a, s1, s2, op0, op1), reads=reads, writes=writes)

        def stt(out, a, s, b, op0, op1, reads, writes, eng="dve"):
            return S.op(eng, lambda e: e.scalar_tensor_tensor(out, a, s, b, op0, op1), reads=reads, writes=writes)

        def rsqrt_small(dst, src, scale, eps, reads, wkey):
            n = src.shape[1]
            ts(dst, src, scale, eps, ALU.mult, ALU.add, reads, [wkey])
            S.op("pool", lambda e: e.tensor_tensor(dst, dst, K["neghalf"][:, 0:n], ALU.pow), reads=[wkey, "c_neghalf"], writes=[wkey])

        def store_dbg(name, ap, key):
            if name in dbg_d:
                S.dma("sp", "dbg_" + name, lambda e: e.dma_start(out=dbg_d[name], in_=ap), reads=[key])

        for k in CONST_SHAPES:
            load(K[k][:], cd[k], "c_" + k)
        def precast(name, cb):
            src = WSRC[name]
            ncols = src.shape[1]
            c0 = cb * PCB
            n = min(PCB, ncols - c0)
            key = "wb_%s_%d" % (name, cb)
            S.dma("pool", "pc_" + key, (lambda e, src=src, name=name, c0=c0, n=n: e.dma_start(out=WBF[name][:, c0:c0 + n], in_=src[:, c0:c0 + n])), writes=[key])

        precast_todo = [("w_in", cb) for cb in (0, 3, 4, 7, 8, 9, 10, 11)]
        for name in ("w_ret_o", "w_hg_o", "w_out"):
            precast_todo += [(name, cb) for cb in range(2)]
        precast_todo += [("w_up", cb) for cb in range(11)] + [("w_down", cb) for cb in range(2)]
        for cb in (1, 2, 5, 6):
            precast("w_in", cb)
        if not any(t[1] == "s" for t in tiles):
            while precast_todo:
                precast(*precast_todo.pop(0))
        load(vmask[:], vmask_d, "vmask")
        load(pmask[:], pmask_d, "pmask")
        S.op("pool", lambda e: e.memset(halo[:], 0.0), writes=["halo"])
        S.op("pool", lambda e: e.memset(Rst[:], 0.0), writes=["Rst"])
        S.op("pool", lambda e: e.memset(Rbf[:], 0.0), writes=["Rbf"])
        S.op("pool", lambda e: e.memset(Sst[:], 0.0), writes=["Sst"])
        S.op("pool", lambda e: e.memset(Sbf[:], 0.0), writes=["Sbf0", "Sbf1", "Sbf2", "Sbf3"])
        cf = small[:, 0:16]
        g1f = small[:, 16:32]
        g2f = small[:, 32:48]
        scf = small[:, 48:64]
        lbf = small[:, 64:80]
        with nc.allow_non_contiguous_dma(reason="tiny one-time strided loads"):
            for j in range(3):
                load(cwf[:, :, j:j + 1], cw_d[j:j + 1, :].rearrange("o (b p) -> p b o", p=128), "cwf")
            load(cwf[:, :, 3:4], cb_d.rearrange("o (b p) -> p b o", p=128), "cwf")
            load(cf, c_d.rearrange("o (k p) -> p (o k)", p=128), "small_c")
            load(g1f, g1_d.rearrange("o (k p) -> p (o k)", p=128), "small_g1")
            load(g2f, g2_d.rearrange("o (k p) -> p (o k)", p=128), "small_g2")
            load(lbf.rearrange("p (l h) -> p l h", l=2), lb_d.rearrange("l (h p) -> p l h", p=128), "small_lb")
        act(scf, cf, AF.Silu, ["small_c"], ["small_sc"])
        sn_ = [0]

        def gemv_block(g, nb):
            col = g * 2048 + nb * 512
            load(brow2, b_ada[0:1, col:col + 512], "brow2")
            pb = bank()
            for kq in range(4):
                buf = (tgA, tgB)[sn_[0] % 2]
                key = "tg%d" % (sn_[0] % 2)
                sn_[0] += 1
                src = w_ada[kq * 512:(kq + 1) * 512, col:col + 512].rearrange("(k p) n -> p k n", p=128)
                load(buf.rearrange("p (k n) -> p k n", k=4), src, key)

                def f(e, kq=kq, pb=pb, buf=buf):
                    r = None
                    for k4 in range(4):
                        kc = kq * 4 + k4
                        r = e.matmul(PS[pb][0:1, :], lhsT=scf[:, kc:kc + 1], rhs=buf[:, k4 * 512:(k4 + 1) * 512],
                                     start=(kc == 0), stop=(kc == 15))
                    return r
                S.op("pe", f, reads=[key, "small_sc"], writes=["ps%d" % pb])
            tt(mrow[:, nb * 512:(nb + 1) * 512], PS[pb][0:1, :], brow2, ALU.add, ["ps%d" % pb, "brow2"], ["mrow"])
            if nb == 3:
                S.dma("sp", "st_mod", lambda e, g=g: e.dma_start(out=mod_d[0:1, g * 2048:(g + 1) * 2048], in_=mrow), reads=["mrow"], writes=["mod_d%d" % g])

        for g in range(2):
            for nb in range(4):
                gemv_block(g, nb)
        gemv_todo = [(g, nb) for g in range(2, 6) for nb in range(4)]
        s1f = small[:, 80:96]
        fmv = lambda g: mod_d[0:1, g * 2048:(g + 1) * 2048].rearrange("o (k p) -> p (o k)", p=128)
        load(AB[:, 1, :], fmv(0), "AB1", reads=["mod_d0"])
        load(s1f, fmv(1), "small_s1", reads=["mod_d1"])
        stt(AB[:, 0, :], s1f, 1.0, g1f, ALU.add, ALU.mult, ["small_s1", "small_g1"], ["AB0"])

        def late_mod():
            while gemv_todo:
                gemv_block(*gemv_todo.pop(0))
            load(AB[:, 3, :], fmv(3), "AB3", reads=["mod_d3"])
            load(s1f, fmv(4), "small_s1", reads=["mod_d4"])
            stt(AB[:, 2, :], s1f, 1.0, g2f, ALU.add, ALU.mult, ["small_s1", "small_g2"], ["AB2"])
        load(tgA[:, 0:1024], lb_d[0:1, :].partition_broadcast(128), "tg0")
        load(tgA[:, 1024:2048], lb_d[1:2, :].partition_broadcast(128), "tg0")
        act(tgA, tgA, AF.Exp, ["tg0"], ["tg0"])
        tt(tgB[:, 0:1024], tgA[:, 0:1024], tgA[:, 1024:2048], ALU.add, ["tg0"], ["tg1"])
        S.op("dve", lambda e: e.reciprocal(tgB[:, 0:1024], tgB[:, 0:1024]), reads=["tg1"], writes=["tg1"])
        tt(LBb[:], tgA[:, 0:1024], tgB[:, 0:1024], ALU.mult, ["tg0", "tg1"], ["LBb"])
        ts(OMLb[:], LBb[:], -1.0, 1.0, ALU.mult, ALU.add, ["LBb"], ["OMLb"])
        act(lbf, lbf, AF.Exp, ["small_lb"], ["small_lb"])
        tt(omlf[:], lbf[:, 0:8], lbf[:, 8:16], ALU.add, ["small_lb"], ["omlf"])
        S.op("dve", lambda e: e.reciprocal(omlf[:], omlf[:]), reads=["omlf"], writes=["omlf"])
        tt(omlf[:], omlf[:], lbf[:, 8:16], ALU.mult, ["omlf", "small_lb"], ["omlf"])

        def make_ops(L):
            xt = L.get('xt')
            hT = L.get('hT')
            cosT = L.get('cosT')
            sinT = L.get('sinT')
            posi = L.get('posi')
            sqj = L.get('sqj')
            dgs = L.get('dgs')
            t1 = L.get('t1')
            t2 = L.get('t2')
            kf32 = L.get('kf32')
            logf = L.get('logf')
            ktm = L.get('ktm')
            sgm = L.get('sgm')
            ecb = L.get('ecb')
            eb = L.get('eb')
            kbf = L.get('kbf')
            kzt = L.get('kzt')
            vret = L.get('vret')
            khat = L.get('khat')
            vhg = L.get('vhg')
            kz = L.get('kz')
            ogT = L.get('ogT')
            mgT = L.get('mgT')
            gT = L.get('gT')
            Gb = L.get('Gb')
            sgr = L.get('sgr')
            osb = L.get('osb')
            osq = L.get('osq')
            enb = L.get('enb')
            sgate = L.get('sgate')
            ybuf = L.get('ybuf')
            ua = L.get('ua')
            ub = L.get('ub')
            qrot = L.get('qrot')
            qhat = L.get('qhat')
            inn = L.get('inn')
            qt = L.get('qt')
            kt = L.get('kt')
            amt = L.get('amt')
            qz = L.get('qz')
            sqjb = sqj.bitcast(BF16)
            sgA = sgB = None
            if enb is not None:
                tq = eb.shape[1] * 512 // 4
                sgA = eb.rearrange("p c n -> p (c n)").rearrange("p (m t) -> p m t", m=4)
                sgB = enb.rearrange("p c n -> p (c n)").rearrange("p (m t) -> p m t", m=4)
            def norm_to_hT(ai, bi, cpt):
                ss = small[:, 0:cpt]
                rs = small[:, 8:8 + cpt]
                for c in range(cpt):
                    act(sqjb[:, 0:2048], xt[:, c, :], AF.Square, ["xt"], ["sqj", "ss"], accum_out=ss[:, c:c + 1])
                rsqrt_small(rs, ss, 1.0 / D, EPS, ["ss"], "rs")
                for c in range(cpt):
                    ts(dgs[:, c, :], K["ident"][:], rs[:, c:c + 1], None, ALU.mult, None, ["c_ident", "rs"], ["dgs%d" % c])
                    for q in range(4):
                        pb = bank()
                        mm_multi([(PS[pb][:, k4 * 128:(k4 + 1) * 128], xt[:, c, (q * 4 + k4) * 128:(q * 4 + k4 + 1) * 128], dgs[:, c, :]) for k4 in range(4)],
                                 ["xt", "dgs%d" % c], "ps%d" % pb)
                        for k4 in range(4):
                            kc = q * 4 + k4
                            act(hT[:, kc, c * 128:(c + 1) * 128], PS[pb][:, k4 * 128:(k4 + 1) * 128], AF.Identity,
                                ["ps%d" % pb, "AB%d" % ai, "AB%d" % bi], ["hT"], scale=AB[:, ai, kc:kc + 1], bias=AB[:, bi, kc:kc + 1])

            def gemm_fm(wi, col0, nmb, actT, akey, kcs, tw, cb):
                for mb in range(nmb):
                    pb = bank()
                    pairs = [(WS[wi][:, kci, col0 + mb * 128: col0 + (mb + 1) * 128], actT[:, kc, 0:tw]) for kci, kc in enumerate(kcs)]
                    mm_group(PS[pb][:, 0:tw], pairs, wk(wi) + [akey], "ps%d" % pb)
                    cb(mb, pb)

            def gemm_tm(wi, col0, ncols, actT, akey, kcs, cpt, cb):
                for c in range(cpt):
                    pb = bank()
                    pairs = [(actT[:, kc, c * 128:(c + 1) * 128], WS[wi][:, kci, col0:col0 + ncols]) for kci, kc in enumerate(kcs)]
                    mm_group(PS[pb][:, 0:ncols], pairs, wk(wi) + [akey], "ps%d" % pb)
                    cb(c, pb)

            def rope_tables(tok0, tw_all):
                for off in range(0, tw_all, 256):
                    rope_piece(tok0 + off, off, min(256, tw_all - off))

            def rope_piece(tok0, off, tw):
                load(posi[:, 0:tw], pos_d[0:1, tok0:tok0 + tw].partition_broadcast(128), "posi")
                ang = sqj[:, 0:tw]
                kf = sqj[:, 256:256 + tw]
                ki = sqj[:, 512:512 + tw].bitcast(I32)
                a2 = sqj[:, 768:768 + tw]
                kk = ["sqj"]
                S.op("dve", lambda e: e.tensor_copy(ang, posi[:, 0:tw]), reads=["posi"], writes=kk)
                ts(ang, ang, K["invf"][:, 0:1], None, ALU.mult, None, kk + ["c_invf"], kk)
                for shift, dst, key in ((0.0, sinT, "sinT"), (float(np.pi / 2), cosT, "cosT")):
                    ts(a2, ang, shift, None, ALU.add, None, kk, kk)
                    ts(kf, a2, float(1.0 / (2 * np.pi)), None, ALU.mult, None, kk, kk)
                    S.op("dve", lambda e: e.tensor_copy(ki, kf), reads=kk, writes=kk)
                    S.op("dve", lambda e: e.tensor_copy(kf, ki), reads=kk, writes=kk)
                    stt(a2, kf, -6.28125, a2, ALU.mult, ALU.add, kk, kk)
                    stt(a2, kf, -float(2 * np.pi - 6.28125), a2, ALU.mult, ALU.add, kk, kk)
                    ts(kf, a2, float(np.pi), None, ALU.is_gt, None, kk, kk)
                    stt(a2, kf, -float(2 * np.pi), a2, ALU.mult, ALU.add, kk, kk)
                    ts(kf, a2, -float(np.pi), None, ALU.is_lt, None, kk, kk)
                    stt(a2, kf, float(2 * np.pi), a2, ALU.mult, ALU.add, kk, kk)
                    ts(a2, a2, 3.1415925, -3.1415925, ALU.min, ALU.max, kk, kk)
                    act(dst[:, off:off + tw], a2, AF.Sin, kk, [key])

            def ret_head(h, ch0, cpt, so):
                tw = cpt * 128
                if not so:
                    act(Rbf[:, h, :], Rst[:, h, :], AF.Copy, ["Rst"], ["Rbf"])
                segs = [(w_in, 0, 1024 + h * 256, 256)]
                if not so:
                    segs = [(w_in, 0, h * 256, 256)] + segs
                wi = wload(segs)
                kcol = 0 if so else 256
                held = {}

                def rope_cb(is_k):
                    def cb(mb, pb):
                        held[mb] = pb
                        if mb != 1:
                            return
                        p1, p2 = PS[held[0]][:, 0:tw], PS[held[1]][:, 0:tw]
                        k12 = ["ps%d" % held[0], "ps%d" % held[1]]
                        for half, (pa, pbb, op) in enumerate(((p1, p2, ALU.subtract), (p2, p1, ALU.add))):
                            tt(t1[:, 0:tw], pa, cosT[:, 0:tw], ALU.mult, k12 + ["cosT"], ["t1"])
                            tt(t2[:, 0:tw], pbb, sinT[:, 0:tw], ALU.mult, k12 + ["sinT"], ["t2"])
                            if is_k:
                                tt(kf32[:, half, 0:tw], t1[:, 0:tw], t2[:, 0:tw], op, ["t1", "t2"], ["kf32"])
                                if not so:
                                    act(kbf[:, half, 0:tw], kf32[:, half, 0:tw], AF.Copy, ["kf32"], ["kbf"])
                            else:
                                tt(t1[:, 0:tw], t1[:, 0:tw], t2[:, 0:tw], op, ["t1", "t2"], ["t1"])
                                act(qrot[:, half, 0:tw], t1[:, 0:tw], AF.Copy, ["t1"], ["qrot"])
                                tt(qhat[:, half, 0:tw].rearrange("p (c t) -> p c t", c=cpt), t1[:, 0:tw].rearrange("p (c t) -> p c t", c=cpt),
                                   bc_mid(K["xib"][:, h, :], cpt), ALU.mult, ["t1", "c_xib"], ["qhat"])
                    return cb
                gemm_fm(wi, kcol, 2, hT, "hT", range(KC), tw, rope_cb(True))
                if not so:
                    gemm_fm(wi, 0, 2, hT, "hT", range(KC), tw, rope_cb(False))
                wi2 = wload([(w_in, 0, 2048 + h * 256, 256)] + ([] if so else [(w_in, 0, 3072 + h * 256, 256)]))

                def vg_cb(c, pb):
                    gi = ch0 + c
                    act(vret[:, c, :], PS[pb][:, 0:256], AF.Identity, ["ps%d" % pb, "vmask"], ["vret"], scale=vmask[:, gi:gi + 1])
                    if not so:
                        act(sgr[:, c, :], PS[pb][:, 256:512], AF.Silu, ["ps%d" % pb], ["sgr"])
                gemm_tm(wi2, 0, 256 if so else 512, hT, "hT", range(KC), cpt, vg_cb)
                for c in range(cpt):
                    pb = bank()
                    tr_multi([(PS[pb][:, k2 * 128:(k2 + 1) * 128], kf32[:, k2, c * 128:(c + 1) * 128]) for k2 in range(2)], ["kf32"], "ps%d" % pb)
                    act(kzt[:, c, :], PS[pb][:, 0:256], AF.Identity, ["ps%d" % pb, "c_zeta"], ["kzt"], scale=K["zeta"][:, h:h + 1])
                for c in range(cpt):
                    cs = slice(c * 128, (c + 1) * 128)
                    if not so:
                        pbs = bank()
                        mm_group(PS[pbs][:, 0:128], [(kbf[:, k2, cs], qrot[:, k2, cs]) for k2 in range(2)], ["kbf", "qrot"], "ps%d" % pbs)
                        tt(inn[:], PS[pbs][:, 0:128], K["decT"][:, h, :], ALU.mult, ["ps%d" % pbs, "c_decT"], ["inn"])
                        pbo = bank()
                        pairs = [(inn[:], vret[:, c, :])] + [(qhat[:, k2, cs], Rbf[:, h, k2 * 256:(k2 + 1) * 256]) for k2 in range(2)]
                        mm_group(PS[pbo][:, 0:256], pairs, ["inn", "vret", "qhat", "Rbf"], "ps%d" % pbo)
                        st_ = small[:, 16:24]
                        o_ = osb[:, 0:256]
                        act(o_, PS[pbo][:, 0:256], AF.Copy, ["ps%d" % pbo], ["osb", "st0"], accum_out=st_[:, 0:1])
                        act(osq[:, 0:256], o_, AF.Square, ["osb"], ["osq", "st1"], accum_out=st_[:, 1:2])
                        ts(st_[:, 2:3], st_[:, 0:1], 1.0 / 256, None, ALU.mult, None, ["st0"], ["st2"])
                        tt(st_[:, 3:4], st_[:, 2:3], st_[:, 2:3], ALU.mult, ["st2"], ["st3"])
                        stt(st_[:, 4:5], st_[:, 1:2], 1.0 / 256, st_[:, 3:4], ALU.mult, ALU.subtract, ["st1", "st3"], ["st4"])
                        rsqrt_small(st_[:, 5:6], st_[:, 4:5], 1.0, HEPS, ["st4"], "st5")
                        ts(o_, o_, st_[:, 2:3], st_[:, 5:6], ALU.subtract, ALU.mult, ["osb", "st2", "st5"], ["osb"])
                        tt(o_, o_, sgr[:, c, :], ALU.mult, ["osb", "sgr"], ["osb"])
                        pbt = bank()
                        tr_multi([(PS[pbt][:, k2 * 128:(k2 + 1) * 128], osb[:, k2 * 128:(k2 + 1) * 128]) for k2 in range(2)], ["osb"], "ps%d" % pbt)
                        for k2 in range(2):
                            act(ogT[:, h * 2 + k2, cs], PS[pbt][:, k2 * 128:(k2 + 1) * 128], AF.Copy, ["ps%d" % pbt], ["ogT"])
                    pbr = bank()
                    mm_multi([(PS[pbr][:, k2 * 256:(k2 + 1) * 256], kzt[:, c, k2 * 128:(k2 + 1) * 128], vret[:, c, :]) for k2 in range(2)],
                             ["kzt", "vret"], "ps%d" % pbr)
                    stt(Rst[:, h, :], Rst[:, h, :], gchunk[h], PS[pbr][:, 0:512], ALU.mult, ALU.add, ["Rst", "ps%d" % pbr], ["Rst"])
                    if not so:
                        act(Rbf[:, h, :], Rst[:, h, :], AF.Copy, ["Rst"], ["Rbf"])

            def hg_group(g, ch0, cpt, so):
                tw = cpt * 128
                c0 = 4096 + g * 512
                wif = wload([(w_in, 0, c0 + 1024, 512)])

                o1 = L.get("_off_t1")
                alt = [AR[:, o1 + i * 512: o1 + (i + 1) * 512] for i in range(4)]
                BS = [dict(sgm=sgm, ktm=ktm, logf=logf, ecb=ecb, k="", g=[]),
                      dict(sgm=alt[0], ktm=alt[1], logf=alt[2], ecb=alt[3], k="_b", g=["t1", "t2", "kf32"])]

                def hf_p1(c, pb, B, first):
                    kx, gd = B["k"], B["g"]
                    act(B["sgm"], PS[pb][:, 0:512], AF.Sigmoid, ["ps%d" % pb], ["sgm" + kx] + (gd if first else []))
                    tt(B["sgm"], B["sgm"], OMLb[:, g * 512:(g + 1) * 512], ALU.mult, ["sgm" + kx, "OMLb"] + gd, ["sgm" + kx])
                    tt(B["sgm"], B["sgm"], LBb[:, g * 512:(g + 1) * 512], ALU.add, ["sgm" + kx, "LBb"] + gd, ["sgm" + kx])
                    ts(B["ktm"], B["sgm"], -1.0, 1.0, ALU.mult, ALU.add, ["sgm" + kx] + gd, ["ktm" + kx])
                    act(B["logf"], B["sgm"], AF.Ln, ["sgm" + kx] + gd, ["logf" + kx])

                def hf_p2(c, B):
                    kx, gd = B["k"], B["g"]
                    pbc = bank()
                    mm_multi([(PS[pbc][:, 0:512], K["trirev"][:], B["logf"])], ["c_trirev", "logf" + kx] + gd, "ps%d" % pbc)
                    act(B["ecb"], PS[pbc][:, 0:512], AF.Exp, ["ps%d" % pbc] + gd, ["ecb" + kx])
                    tt(khat[:, c, :], B["ktm"], B["ecb"], ALU.mult, ["ktm" + kx, "ecb" + kx] + gd, ["khat"])
                    pbb = bank()
                    mm_multi([(PS[pbb][:, hh * 128:(hh + 1) * 128], B["logf"][:, hh * 128:(hh + 1) * 128], K["triinc"][:]) for hh in range(4)],
                             ["logf" + kx, "c_triinc"] + gd, "ps%d" % pbb)
                    act(eb[:, c, :], PS[pbb][:, 0:512], AF.Exp, ["ps%d" % pbb], ["eb"])
                    if not so:
                        act(enb[:, c, :], PS[pbb][:, 0:512], AF.Exp, ["ps%d" % pbb], ["enb"], scale=-1.0)

                use_alt = so and cpt >= 2 and o1 is not None
                seen_alt = [False]

                def hf_cb(c, pb):
                    B = BS[c % 2] if use_alt else BS[0]
                    first = False
                    if B is BS[1] and not seen_alt[0]:
                        first = True
                        seen_alt[0] = True
                    hf_p1(c, pb, B, first)
                    if use_alt:
                        if c >= 1:
                            hf_p2(c - 1, BS[(c - 1) % 2])
                        if c == cpt - 1:
                            hf_p2(c, B)
                    else:
                        hf_p2(c, B)
                gemm_tm(wif, 0, 512, hT, "hT", range(KC), cpt, hf_cb)
                if not so:
                    def hfT_cb(mb, pb):
                        act(t1[:, 0:tw], PS[pb][:, 0:tw], AF.Sigmoid, ["ps%d" % pb], ["t1"], scale=-1.0)
                        stt(kt[:, mb, 0:tw].rearrange("p (c t) -> p c t", c=cpt), t1[:, 0:tw].rearrange("p (c t) -> p c t", c=cpt),
                            omlf[:, g * 4 + mb: g * 4 + mb + 1], enb[:, 0:cpt, mb * 128:(mb + 1) * 128], ALU.mult, ALU.mult, ["t1", "omlf", "enb"], ["kt"])
                    gemm_fm(wif, 0, 4, hT, "hT", range(KC), tw, hfT_cb)
                    wiq = wload([(w_in, 0, c0, 512)])

                    def hqT_cb(mb, pb):
                        act(t2[:, 0:tw], PS[pb][:, 0:tw], AF.Silu, ["ps%d" % pb], ["t2"])
                        tt(qt[:, mb, 0:tw].rearrange("p (c t) -> p c t", c=cpt), t2[:, 0:tw].rearrange("p (c t) -> p c t", c=cpt),
                           eb[:, 0:cpt, mb * 128:(mb + 1) * 128], ALU.mult, ["t2", "eb"], ["qt"])
                    gemm_fm(wiq, 0, 4, hT, "hT", range(KC), tw, hqT_cb)
                wiv = wload([(w_in, 0, c0 + 2048, 512)])

                def v_cb(c, pb):
                    gi = ch0 + c
                    act(vhg[:, c, :], PS[pb][:, 0:512], AF.Identity, ["ps%d" % pb, "vmask"], ["vhg"], scale=vmask[:, gi:gi + 1])
                gemm_tm(wiv, 0, 512, hT, "hT", range(KC), cpt, v_cb)
                if not so:
                    wig = wload([(w_in, 0, c0 + 3072, 512)])

                    def g_cb(c, pb):
                        act(sgate[:, c, :], PS[pb][:, 0:512], AF.Silu, ["ps%d" % pb], ["sgate"])
                    gemm_tm(wig, 0, 512, hT, "hT", range(KC), cpt, g_cb)
                if not so:
                    act(Sbf[:, 0, :, :], Sst[:, g * 4:g * 4 + 4, :], AF.Copy, ["Sst"], ["Sbf0"])
                for c in range(cpt):
                    cs = slice(c * 128, (c + 1) * 128)
                    if not so:
                        pba = bank()
                        mm_multi([(PS[pba][:, hh * 128:(hh + 1) * 128], kt[:, hh, cs], qt[:, hh, cs]) for hh in range(4)], ["kt", "qt"], "ps%d" % pba)
                        tt(amt[:], PS[pba][:, 0:512].rearrange("p (h t) -> p h t", h=4), bc_mid(K["triinc"][:], 4), ALU.mult,
                           ["ps%d" % pba, "c_triinc"], ["amt"])
                        for j in range(4):
                            tt(qz[:, j, :, :], qt[:, :, cs], bc_mid(K["colmask"][:, j, :], 4), ALU.mult, ["qt", "c_colmask"], ["qz"], eng="pool")

                    def sub_update(j):
                        kzb, kzk = kz, "kz"
                        if so and (j % 2 == 1):
                            kzb, kzk = kz2, "kz2"
                        act(kzb, khat[:, c, :], AF.Identity, ["khat", "c_rowmask"], [kzk], scale=K["rowmask"][:, j:j + 1])
                        pbd = bank()
                        mm_multi([(PS[pbd][:, hh * 128:(hh + 1) * 128], kzb[:, hh * 128:(hh + 1) * 128], vhg[:, c, hh * 128:(hh + 1) * 128]) for hh in range(4)],
                                 [kzk, "vhg"], "ps%d" % pbd)
                        for hh in range(4):
                            hd = g * 4 + hh
                            col = hh * 128 + j * 32 + 31
                            stt(Sst[:, hd, :], Sst[:, hd, :], eb[:, c, col:col + 1], PS[pbd][:, hh * 128:(hh + 1) * 128], ALU.mult, ALU.add,
                                ["Sst", "eb", "ps%d" % pbd], ["Sst"])
                        jn = (j + 1) % 4
                        if not so:
                            act(Sbf[:, jn, :, :], Sst[:, g * 4:g * 4 + 4, :], AF.Copy, ["Sst"], ["Sbf%d" % jn])
                    for j in range(3):
                        sub_update(j)
                    if not so:
                        pbo = bank()

                        def f(e, pbo=pbo, c=c):
                            r = None
                            for hh in range(4):
                                o = PS[pbo][:, hh * 128:(hh + 1) * 128]
                                e.matmul(o, lhsT=amt[:, hh, :], rhs=vhg[:, c, hh * 128:(hh + 1) * 128], start=True, stop=False)
                                for j in range(4):
                                    r = e.matmul(o, lhsT=qz[:, j, hh, :], rhs=Sbf[:, j, hh, :], start=False, stop=(j == 3))
                            return r
                        S.op("pe", f, reads=["amt", "vhg", "qz", "Sbf0", "Sbf1", "Sbf2", "Sbf3"], writes=["ps%d" % pbo])
                        ss_ = small[:, 24:28]
                        rs_ = small[:, 28:32]
                        for hh in range(4):
                            act(osq[:, hh * 128:(hh + 1) * 128], PS[pbo][:, hh * 128:(hh + 1) * 128], AF.Square, ["ps%d" % pbo], ["osq", "hss"], accum_out=ss_[:, hh:hh + 1])
                        rsqrt_small(rs_, ss_, 1.0 / 128, HEPS, ["hss"], "hrs")
                        for hh in range(4):
                            stt(osb[:, hh * 128:(hh + 1) * 128], PS[pbo][:, hh * 128:(hh + 1) * 128], rs_[:, hh:hh + 1], sgate[:, c, hh * 128:(hh + 1) * 128],
                                ALU.mult, ALU.mult, ["ps%d" % pbo, "hrs", "sgate"], ["osb"])
                        pbt = bank()
                        tr_multi([(PS[pbt][:, hh * 128:(hh + 1) * 128], osb[:, hh * 128:(hh + 1) * 128]) for hh in range(4)], ["osb"], "ps%d" % pbt)
                        for hh in range(4):
                            act(ogT[:, 8 + g * 4 + hh, cs], PS[pbt][:, hh * 128:(hh + 1) * 128], AF.Copy, ["ps%d" % pbt], ["ogT"])
                    sub_update(3)


            def merge_and_out2(cpt):
                tw = cpt * 128
                for jb in range(4):
                    wa = wload([(w_in, 0, 8192 + jb * 512, 512)])
                    wb = wload([(w_in, 0, 10240 + jb * 512, 512)])
                    sig = []
                    for mb in range(4):
                        kcb = jb * 4 + mb
                        pga, pgb = bank(), bank()
                        mm_group(PS[pga][:, 0:tw], [(WS[wa][:, kc, mb * 128:(mb + 1) * 128], hT[:, kc, 0:tw]) for kc in range(KC)], wk(wa) + ["hT"], "ps%d" % pga)
                        mm_group(PS[pgb][:, 0:tw], [(WS[wb][:, kc, mb * 128:(mb + 1) * 128], hT[:, kc, 0:tw]) for kc in range(KC)], wk(wb) + ["hT"], "ps%d" % pgb)
                        act(sgA[:, mb, 0:tw], PS[pga][:, 0:tw], AF.Sigmoid, ["ps%d" % pga], ["eb"])
                        act(sgB[:, mb, 0:tw], PS[pgb][:, 0:tw], AF.Sigmoid, ["ps%d" % pgb], ["enb"])
                    wo = wload([(w_ro, 0, jb * 512, 512)], krows=1024)
                    srcb = WBF["w_hg_o"][:, jb * 512:(jb + 1) * 512].rearrange("(kc p) n -> p kc n", p=128)
                    S.dma("sp", "ws%d_1" % wo, (lambda e, wo=wo, srcb=srcb: e.dma_start(out=WS[wo][:, 8:16, :], in_=srcb)), writes=["ws%d_s1" % wo],
                          reads=["wb_w_hg_o_%d" % ((jb * 512) // PCB)])
                    for mb in range(4):
                        kcb = jb * 4 + mb
                        pya, pyb = bank(), bank()
                        mm_group(PS[pya][:, 0:tw], [(WS[wo][:, kc, mb * 128:(mb + 1) * 128], ogT[:, kc, 0:tw]) for kc in range(8)], wk(wo) + ["ogT"], "ps%d" % pya)
                        mm_group(PS[pyb][:, 0:tw], [(WS[wo][:, 8 + kc, mb * 128:(mb + 1) * 128], ogT[:, 8 + kc, 0:tw]) for kc in range(8)], wk(wo) + ["ogT"], "ps%d" % pyb)
                        tt(ua[:, 0:tw], sgA[:, mb, 0:tw], PS[pya][:, 0:tw], ALU.mult, ["eb", "ps%d" % pya], ["ua"])
                        tt(ub[:, 0:tw], sgB[:, mb, 0:tw], PS[pyb][:, 0:tw], ALU.mult, ["enb", "ps%d" % pyb], ["ub"])
                        tt(mgT[:, kcb, 0:tw], ua[:, 0:tw], ub[:, 0:tw], ALU.add, ["ua", "ub"], ["mgT"], eng="pool")
                load(Gb[:], mod_d[0:1, 2 * D:3 * D].partition_broadcast(128), "Gb", reads=["mod_d2"])
                for jb in range(4):
                    wo = wload([(w_out, 0, jb * 512, 512)])

                    gemm_tm(wo, 0, 512, mgT, "mgT", range(KC), cpt, lambda c, pb, jb=jb: resid_cb(c, pb, jb))

            def resid_cb(c, pb, jb):
                tt(osq[:, 0:512], PS[pb][:, 0:512], Gb[:, jb * 512:(jb + 1) * 512], ALU.mult, ["ps%d" % pb, "Gb"], ["osq"])
                tt(xt[:, c, jb * 512:(jb + 1) * 512], xt[:, c, jb * 512:(jb + 1) * 512], osq[:, 0:512], ALU.add, ["xt", "osq"], ["xt"], eng="pool")

            def ffn(ti, cpt, store=True):
                tw = cpt * 128
                for fb in range(11):
                    wa = wload([(w_up, 0, fb * 512, 512)])
                    wb = wload([(w_up, 0, DFF + fb * 512, 512)])
                    for mb in range(4):
                        blk = fb * 4 + mb
                        res = {}
                        for which, wi in ((0, wa), (1, wb)):
                            ch = which * NFB + blk
                            pb = bank()
                            mm_group(PS[pb][:, 0:tw], [(WS[wi][:, kc, mb * 128:(mb + 1) * 128], hT[:, kc, 0:tw]) for kc in range(KC)], wk(wi) + ["hT"], "ps%d" % pb)
                            if not store:
                                act(halo[:, ch, :], PS[pb][:, tw - 2:tw], AF.Copy, ["ps%d" % pb], ["halo"])
                                continue
                            yb = ybuf[:, which, :]
                            yk = "ybuf%d" % which
                            act(yb[:, 2:2 + tw], PS[pb][:, 0:tw], AF.Copy, ["ps%d" % pb], [yk])
                            ts(yb[:, 0:2], halo[:, ch, :], pmask[:, ti:ti + 1], None, ALU.mult, None, ["halo", "pmask"], [yk], eng="pool")
                            u = ua if which == 0 else ub
                            uk = "ua" if which == 0 else "ub"
                            act(u[:, 0:tw], yb[:, 2:2 + tw], AF.Identity, [yk, "cwf"], [uk], scale=cwf[:, ch, 2:3], bias=cwf[:, ch, 3:4])
                            stt(u[:, 0:tw], yb[:, 1:1 + tw], cwf[:, ch, 1:2], u[:, 0:tw], ALU.mult, ALU.add, [yk, "cwf", uk], [uk])
                            stt(u[:, 0:tw], yb[:, 0:tw], cwf[:, ch, 0:1], u[:, 0:tw], ALU.mult, ALU.add, [yk, "cwf", uk], [uk])
                            S.op("pool", lambda e, yb=yb, ch=ch: e.tensor_copy(halo[:, ch, :], yb[:, tw:tw + 2]), reads=[yk], writes=["halo"])
                        if store:
                            act(t1[:, 0:tw], ua[:, 0:tw], AF.Silu, ["ua"], ["t1"])
                            tt(gT[:, blk, 0:tw], t1[:, 0:tw], ub[:, 0:tw], ALU.mult, ["t1", "ub"], ["gT"])
                if ti == 0:
                    store_dbg("gT", gT[:], "gT")
                if not store:
                    return
                load(Gb[:], mod_d[0:1, 5 * D:6 * D].partition_broadcast(128), "Gb", reads=["mod_d5"])
                for jb in range(4):
                    pbs = [bank() for _ in range(cpt)]
                    parts = [(0, 16), (16, 16), (32, 12)]
                    for pi, (f0, nf) in enumerate(parts):
                        wd = wload([(w_dn, f0 * 128, jb * 512, 512)], krows=nf * 128)
                        for c in range(cpt):
                            def f(e, c=c, wd=wd, f0=f0, nf=nf, pi=pi, pbs=pbs):
                                r = None
                                for k in range(nf):
                                    r = e.matmul(PS[pbs[c]][:, 0:512], lhsT=gT[:, f0 + k, c * 128:(c + 1) * 128], rhs=WS[wd][:, k, 0:512],
                                                 start=(pi == 0 and k == 0), stop=(pi == 2 and k == nf - 1))
                                return r
                            S.op("pe", f, reads=wk(wd) + ["gT"], writes=["ps%d" % pbs[c]])
                    for c in range(cpt):
                        resid_cb(c, pbs[c], jb)

            def final_store(cpt, row0):
                if row0 == 0:
                    store_dbg("x2", xt[:], "xt")
                load(Gb[:], gf_d.partition_broadcast(128), "Gb")
                ss = small[:, 0:cpt]
                rs = small[:, 8:8 + cpt]
                for c in range(cpt):
                    act(sqjb[:, 0:2048], xt[:, c, :], AF.Square, ["xt"], ["sqj", "ss"], accum_out=ss[:, c:c + 1])
                rsqrt_small(rs, ss, 1.0 / D, EPS, ["ss"], "rs")
                for c in range(cpt):
                    stt(xt[:, c, :], xt[:, c, :], rs[:, c:c + 1], Gb[:], ALU.mult, ALU.mult, ["xt", "rs", "Gb"], ["xt"])
                    S.dma("sp", "st_out", lambda e, c=c: e.dma_start(out=out_d[row0 + c * 128: row0 + (c + 1) * 128, :], in_=xt[:, c, :]), reads=["xt"])


            return dict(norm_to_hT=norm_to_hT, rope_tables=rope_tables, ret_head=ret_head, hg_group=hg_group,
                        merge_and_out2=merge_and_out2, ffn=ffn, final_store=final_store, xt=xt)

        OPS = {"s": make_ops(LS), "f": make_ops(LF)}
        print("SBUF bytes/partition (final):", sbytes[0])
        ch0 = 0
        prev_mode = None
        S.barrier()
        for ti, (cpt, mode, row0) in enumerate(tiles):
            so = mode == "s"
            if mode == "f" and prev_mode != "f":
                while precast_todo:
                    precast(*precast_todo.pop(0))
                late_mod()
            if prev_mode is not None and prev_mode != mode:
                S.barrier()
            prev_mode = mode
            O = OPS[mode]
            for c in range(cpt):
                load(O["xt"][:, c, :], x_d[(ch0 + c) * 128:(ch0 + c + 1) * 128, :], "xt")
            O["norm_to_hT"](0, 1, cpt)
            if so and precast_todo:
                precast(*precast_todo.pop(0))
            O["rope_tables"](ch0 * 128, cpt * 128)
            if ti == 0 and "cos" in dbg_d:
                store_dbg("cos", (LS if so else LF)["cosT"], "cosT")
                store_dbg("sin", (LS if so else LF)["sinT"], "sinT")
                store_dbg("sqj", (LS if so else LF)["sqj"], "sqj")
            for h in range(RET_H):
                O["ret_head"](h, ch0, cpt, so)
                if ti == 0 and h == 0 and "kf32" in dbg_d:
                    store_dbg("kf32", (LS if so else LF)["kf32"], "kf32")
                    store_dbg("R0", Rst[:, 0, :], "Rst")
            for g in range(2):
                O["hg_group"](g, ch0, cpt, so)
            if not so and row0 == 0 and "ogT" in dbg_d:
                store_dbg("ogT", LF["ogT"], "ogT")
            if not so:
                O["merge_and_out2"](cpt)
                O["norm_to_hT"](2, 3, cpt)
                O["ffn"](ti, cpt, row0 is not None)
                if row0 is not None:
                    O["final_store"](cpt, row0)
            if so and gemv_todo:
                gemv_block(*gemv_todo.pop(0))
            ch0 += cpt

        fin = [(k, v) for k, v in S.dma_cnt.items() if k.startswith("st_") or k.startswith("dbg_")]
        S.wait_all("sp", fin)
        with nc.Block() as block:
            S.replay(block)
    return nc


CPS = 4


def make_tiles(nstate_chunks, pre_chunks, own_chunks):
    tiles = []
    n = nstate_chunks
    while n > 0:
        c = min(CPS, n)
        tiles.append((c, "s", None))
        n -= c
    n = pre_chunks
    while n > 0:
        c = min(CPTMAX, n)
        tiles.append((c, "f", None))
        n -= c
    row = 0
    n = own_chunks
    while n > 0:
        c = min(CPTMAX, n)
        tiles.append((c, "f", row))
        row += c * 128
        n -= c
    return tiles


_CACHE = {}


def kernel(x, c, positions, w_ada, b_ada, g_norm1, w_in, w_ret_o, w_hg_o, w_out,
           hg_lb, g_norm2, w_up, conv_w, conv_b, w_down, g_final):
    own = SEQ // NCORES // 128
    pre = 1
    nstate = (NCORES - 1) * own - pre
    tiles = make_tiles(nstate, pre, own)
    if "nc" not in _CACHE:
        _CACHE["nc"] = build(tiles)
    nc = _CACHE["nc"]
    nch = nstate + pre + own
    ntok = nch * 128
    xs = np.asarray(x, np.float32).reshape(SEQ, D)
    ps = np.asarray(positions, np.int32).reshape(SEQ)
    HC = host_consts()
    shared = {
        "c": np.asarray(c, np.float32).reshape(1, D),
        "w_ada": np.ascontiguousarray(np.asarray(w_ada, np.float32).reshape(D, 6 * D)),
        "b_ada": np.asarray(b_ada, np.float32).reshape(1, 6 * D),
        "g_norm1": np.asarray(g_norm1, np.float32).reshape(1, D),
        "w_in": np.ascontiguousarray(np.asarray(w_in, np.float32).reshape(D, IN_COLS)),
        "w_ret_o": np.ascontiguousarray(np.asarray(w_ret_o, np.float32).reshape(1024, D)),
        "w_hg_o": np.ascontiguousarray(np.asarray(w_hg_o, np.float32).reshape(1024, D)),
        "w_out": np.ascontiguousarray(np.asarray(w_out, np.float32).reshape(D, D)),
        "hg_lb": np.asarray(hg_lb, np.float32).reshape(2, 1024),
        "g_norm2": np.asarray(g_norm2, np.float32).reshape(1, D),
        "w_up": np.ascontiguousarray(np.asarray(w_up, np.float32).reshape(D, 2 * DFF)),
        "conv_w": np.asarray(conv_w, np.float32).reshape(3, 2 * DFF),
        "conv_b": np.asarray(conv_b, np.float32).reshape(1, 2 * DFF),
        "w_down": np.ascontiguousarray(np.asarray(w_down, np.float32).reshape(DFF, D)),
        "g_final": np.asarray(g_final, np.float32).reshape(1, D),
    }
    for k in CONST_SHAPES:
        shared["k_" + k] = np.ascontiguousarray(HC[k]).reshape(CONST_SHAPES[k])
    in_maps = []
    for core in range(NCORES):
        nreal = (core + 1) * own * 128
        xc = np.zeros((ntok, D), np.float32)
        xc[ntok - nreal:] = xs[:nreal]
        pc = np.zeros((1, ntok), np.int32)
        pc[0, ntok - nreal:] = ps[:nreal]
        tokmask = np.zeros(ntok, np.float32)
        tokmask[ntok - nreal:] = 1.0
        vm = np.ascontiguousarray(tokmask.reshape(nch, 128).T)
        pm = np.zeros((128, len(tiles)), np.float32)
        chs = 0
        for ti, (cpt, mode, row0) in enumerate(tiles):
            prev_real = 1.0 if (chs * 128 - 1) >= (ntok - nreal) else 0.0
            pm[:, ti] = prev_real
            chs += cpt
        m = dict(shared)
        m.update({"x": xc, "positions": pc, "vmask": vm, "pmask": pm})
        in_maps.append(m)
    res = run_bass_kernel_spmd(nc, in_maps, core_ids=list(range(NCORES)))
    outs = [np.asarray(r["out"])[: own * 128] for r in res.results]
    return np.concatenate(outs, axis=0).reshape(1, SEQ, D).astype(np.float32)
```

```python
import contextlib
import numpy as np
import concourse.bass as bass
import concourse.mybir as mybir
from concourse.bass_utils import run_bass_kernel_spmd

F32 = mybir.dt.float32
BF16 = mybir.dt.bfloat16
I32 = mybir.dt.int32
ALU = mybir.AluOpType
AF = mybir.ActivationFunctionType

D = 2048
KC = 16
SEQ = 16384
NCORES = 8
CPTMAX = 2
TMAX = 128 * CPTMAX
DFF = 5632
NFB = DFF // 128
RET_H, HG_H = 4, 8
IN_COLS = 12288
EPS = 1e-6
HEPS = 1e-5


class Sched:
    ENGS = ("pe", "act", "dve", "pool", "sp")

    def __init__(self, nc, stack):
        self.nc = nc
        self.stack = stack
        self.q = {e: [] for e in self.ENGS}
        self.cnt = {e: 0 for e in self.ENGS}
        self.known = {e: {} for e in self.ENGS}
        self.last_w = {}
        self.readers = {}
        self.sems = {}
        self.dma_cnt = {}
        for e in ("pe", "act", "dve", "pool"):
            self.sems[e] = stack.enter_context(nc.semaphore("c_" + e))

    def _sem(self, key):
        if key not in self.sems:
            self.sems[key] = self.stack.enter_context(self.nc.semaphore("d_" + str(key)))
            self.dma_cnt[key] = 0
        return self.sems[key]

    def _deps(self, eng, reads, writes, extra=()):
        need = {}

        def add(t):
            if t is None:
                return
            sk, v = t
            if need.get(sk, 0) < v:
                need[sk] = v
        for k in reads:
            add(self.last_w.get(k))
        for k in writes:
            add(self.last_w.get(k))
            for t in self.readers.get(k, ()):
                add(t)
        for t in extra:
            add(t)
        waits = []
        for sk, v in need.items():
            if sk == eng and eng == "pe":
                continue
            if self.known[eng].get(sk, 0) >= v:
                continue
            self.known[eng][sk] = v
            waits.append((sk, v))
        return waits

    def _commit(self, tok, reads, writes):
        for k in reads:
            self.readers.setdefault(k, []).append(tok)
        for k in writes:
            self.last_w[k] = tok
            self.readers[k] = []

    def op(self, eng, fn, reads=(), writes=()):
        waits = self._deps(eng, reads, writes)
        self.cnt[eng] += 1
        tok = (eng, self.cnt[eng])
        self.q[eng].append((fn, waits, (eng, 1)))
        self._commit(tok, reads, writes)
        return tok

    def dma(self, eng, slot, fn, reads=(), writes=()):
        self._sem(slot)
        waits = self._deps(eng, reads, writes)
        self.dma_cnt[slot] += 16
        tok = (slot, self.dma_cnt[slot])
        self.q[eng].append((fn, waits, (slot, 16)))
        self._commit(tok, reads, writes)
        return tok

    def barrier(self):
        toks = [(e, self.cnt[e]) for e in ("pe", "act", "dve", "pool") if self.cnt[e] > 0]
        toks += [(k, v) for k, v in self.dma_cnt.items() if v > 0]
        for e in self.ENGS:
            self.wait_all(e, toks)

    def wait_all(self, eng, toks):
        waits = self._deps(eng, (), (), toks)
        self.q[eng].append((None, waits, None))

    def replay(self, block):
        engmap = {"pe": block.tensor, "act": block.scalar, "dve": block.vector,
                  "pool": block.gpsimd, "sp": block.sync}
        sems = self.sems
        for e in self.ENGS:
            def body(engine, items=self.q[e]):
                for fn, waits, inc in items:
                    for sk, v in waits:
                        engine.wait_ge(sems[sk], v)
                    if fn is None:
                        continue
                    ins = fn(engine)
                    ins.then_inc(sems[inc[0]], inc[1])
            engmap[e](body)


def host_consts():
    c = {}
    c["ident"] = np.eye(128, dtype=np.float32)
    idx = np.arange(128, dtype=np.float64)
    gam = 1.0 - np.exp2(-5.0 - np.arange(RET_H, dtype=np.float64))
    lg = np.log(gam)
    diff = idx[None, :] - idx[:, None]
    dec = np.where(diff >= 0, np.exp(np.maximum(diff, 0)[None] * lg[:, None, None]), 0.0)
    c["decT"] = np.ascontiguousarray((dec * 256 ** -0.5).transpose(1, 0, 2)).astype(np.float32)
    xi = np.exp((idx[None, :] + 1.0) * lg[:, None])
    c["xib"] = np.ascontiguousarray(np.broadcast_to(xi[None], (128, RET_H, 128))).astype(np.float32)
    zeta = np.exp((127.0 - idx[None, :]) * lg[:, None]) * 256 ** -0.5
    c["zeta"] = np.ascontiguousarray(zeta.T).astype(np.float32)
    c["gchunk"] = np.exp(128 * lg)
    sub = (np.arange(128) // 32)
    same = sub[:, None] == sub[None, :]
    s = np.arange(128)
    c["triinc"] = (same & (s[:, None] <= s[None, :])).astype(np.float32)
    c["trirev"] = (same & (s[:, None] > s[None, :])).astype(np.float32)
    c["rowmask"] = (sub[:, None] == np.arange(4)[None, :]).astype(np.float32)
    c["colmask"] = np.ascontiguousarray(np.broadcast_to((sub[None, :] == np.arange(4)[:, None])[None], (128, 4, 128))).astype(np.float32)
    c["invf"] = (10000.0 ** (-np.arange(0, 256, 2, dtype=np.float32) / 256)).astype(np.float32)[:, None]
    c["neghalf"] = np.full((128, 8), -0.5, np.float32)
    return c


CONST_SHAPES = {"ident": [128, 128], "decT": [128, 4, 128], "xib": [128, 4, 128], "zeta": [128, 4],
                "triinc": [128, 128], "trirev": [128, 128], "rowmask": [128, 4], "colmask": [128, 4, 128],
                "invf": [128, 1], "neghalf": [128, 8]}


def bc_mid(a, n):
    return bass.AP(a.tensor, a.offset, [list(a.ap[0]), [0, n], list(a.ap[-1])])


def build(tiles, dbg=None):
    NCH = sum(t[0] for t in tiles)
    NTOK = NCH * 128
    NT = len(tiles)
    NOUT = sum(t[0] * 128 for t in tiles if t[1] == "f" and t[2] is not None)
    HC = host_consts()
    gchunk = [float(v) for v in HC["gchunk"]]
    nc = bass.Bass("TRN2", target_bir_lowering=False)
    din = lambda n, s, dt=F32: nc.dram_tensor(n, s, dt, kind="ExternalInput").ap()
    x_d = din("x", [NTOK, D])
    pos_d = din("positions", [1, NTOK], I32)
    vmask_d = din("vmask", [128, NCH])
    pmask_d = din("pmask", [128, NT])
    c_d = din("c", [1, D])
    w_ada = din("w_ada", [D, 6 * D])
    b_ada = din("b_ada", [1, 6 * D])
    g1_d = din("g_norm1", [1, D])
    w_in = din("w_in", [D, IN_COLS])
    w_ro = din("w_ret_o", [1024, D])
    w_ho = din("w_hg_o", [1024, D])
    w_out = din("w_out", [D, D])
    lb_d = din("hg_lb", [2, 1024])
    g2_d = din("g_norm2", [1, D])
    w_up = din("w_up", [D, 2 * DFF])
    cw_d = din("conv_w", [3, 2 * DFF])
    cb_d = din("conv_b", [1, 2 * DFF])
    w_dn = din("w_down", [DFF, D])
    gf_d = din("g_final", [1, D])
    cd = {k: din("k_" + k, s) for k, s in CONST_SHAPES.items()}
    out_d = nc.dram_tensor("out", [max(NOUT, 128), D], F32, kind="ExternalOutput").ap()
    mod_d = nc.dram_tensor("mod_scratch", [1, 6 * D], F32).ap()
    WSRC = {"w_in": w_in, "w_ret_o": w_ro, "w_hg_o": w_ho, "w_out": w_out, "w_up": w_up, "w_down": w_dn}
    WBF = {k: nc.dram_tensor("bf_" + k, list(v.shape), BF16).ap() for k, v in WSRC.items()}
    WNAME = {id(v): k for k, v in WSRC.items()}
    PCB = 1024
    dbg_d = {}
    if dbg:
        for k, (s, dt_) in dbg.items():
            dbg_d[k] = nc.dram_tensor("dbg_" + k, s, dt_, kind="ExternalOutput").ap()

    with contextlib.ExitStack() as st:
        S = Sched(nc, st)
        sbytes = [0]

        def sb(name, shape, dt=F32):
            n = int(np.prod(shape[1:])) * (4 if dt in (F32, I32) else 2)
            sbytes[0] += n
            return st.enter_context(nc.sbuf_tensor(name, shape, dt))
        K = {k: sb("c_" + k, s) for k, s in CONST_SHAPES.items()}
        WS = [sb("ws%d" % i, [128, KC, 512], BF16) for i in range(2)]
        AB = sb("AB", [128, 4, KC])
        LBb = sb("LBb", [128, 1024])
        OMLb = sb("OMLb", [128, 1024])
        omlf = sb("omlf", [128, 8])
        cwf = sb("cwf", [128, 2 * NFB, 4])
        halo = sb("halo", [128, 2 * NFB, 2])
        vmask = sb("vmask_s", [128, NCH])
        pmask = sb("pmask_s", [128, NT])
        Rst = sb("Rst", [128, RET_H, 512])
        Rbf = sb("Rbf", [128, RET_H, 512], BF16)
        Sst = sb("Sst", [128, HG_H, 128])
        Sbf = sb("Sbf", [128, 4, 4, 128], BF16)
        small = sb("small", [128, 96])

        def spec(cp, full):
            T_ = cp * 128
            L = [("xt", [128, cp, D], F32), ("hT", [128, KC, T_], BF16), ("cosT", [128, T_], F32), ("sinT", [128, T_], F32),
                 ("posi", [128, min(T_, 256)], I32), ("sqj", [128, 1024], F32), ("dgs", [128, cp, 128], F32),
                 ("t1", [128, T_], F32), ("t2", [128, T_], F32), ("kf32", [128, 2, T_], F32),
                 ("logf", [128, 512], F32), ("ktm", [128, 512], F32), ("sgm", [128, 512], F32), ("ecb", [128, 512], F32),
                 ("eb", [128, cp, 512], F32), ("kbf", [128, 2, T_], BF16), ("kzt", [128, cp, 256], BF16), ("vret", [128, cp, 256], BF16),
                 ("khat", [128, cp, 512], BF16), ("vhg", [128, cp, 512], BF16), ("kz", [128, 512], BF16)]
            if full:
                L += [("ogT", [128, KC, T_], BF16), ("mgT", [128, KC, T_], BF16), ("gT", [128, NFB, T_], BF16), ("Gb", [128, D], F32),
                      ("sgr", [128, cp, 256], F32), ("osb", [128, 512], F32), ("osq", [128, 512], F32), ("enb", [128, cp, 512], F32),
                      ("sgate", [128, cp, 512], F32), ("ybuf", [128, 2, T_ + 2], F32), ("ua", [128, T_], F32), ("ub", [128, T_], F32),
                      ("qrot", [128, 2, T_], BF16), ("qhat", [128, 2, T_], BF16), ("inn", [128, 128], BF16), ("qt", [128, 4, T_], BF16),
                      ("kt", [128, 4, T_], BF16), ("amt", [128, 4, 128], BF16), ("qz", [128, 4, 4, 128], BF16)]
            return L

        def nf32(shape, dt_):
            n = int(np.prod(shape[1:]))
            return (n + 1) // 2 if dt_ == BF16 else n

        CPS_ = max([t[0] for t in tiles if t[1] == "s"] + [1])
        CPF_ = max([t[0] for t in tiles if t[1] == "f"] + [1])
        ls_tot = sum(nf32(sh, d_) for _, sh, d_ in spec(CPS_, False))
        lf_tot = sum(nf32(sh, d_) for _, sh, d_ in spec(CPF_, True))
        need = max(ls_tot + 6656, lf_tot)
        AR = sb("arena", [128, need])

        def layout(cp, full):
            L = {}
            off = 0
            for name, shape, dt_ in spec(cp, full):
                n = int(np.prod(shape[1:]))
                w = nf32(shape, dt_)
                v = AR[:, off:off + w]
                if dt_ == BF16:
                    v = v.bitcast(BF16)[:, 0:n]
                elif dt_ == I32:
                    v = v.bitcast(I32)
                if len(shape) == 3:
                    v = v.rearrange("p (a b) -> p a b", a=shape[1])
                elif len(shape) == 4:
                    v = v.rearrange("p (a b c) -> p a b c", a=shape[1], b=shape[2])
                L[name] = v
                L["_off_" + name] = off
                off += w
            return L
        LF = layout(CPF_, True)
        LS = layout(CPS_, False)
        xt, Gb, sqj, gT = LF["xt"], LF["Gb"], LF["sqj"], LF["gT"]
        tgA = AR[:, ls_tot:ls_tot + 2048]
        tgB = AR[:, ls_tot + 2048:ls_tot + 4096]
        mrow = AR[0:1, ls_tot + 4096:ls_tot + 6144]
        brow2 = AR[0:1, ls_tot + 6144:ls_tot + 6656]
        kz2 = AR[:, ls_tot + 6656:ls_tot + 6912].bitcast(BF16)
        assert need >= ls_tot + 6912
        sqjb = sqj.bitcast(BF16)
        rowb = Gb
        print("SBUF bytes/partition:", sbytes[0])
        PS = [st.enter_context(nc.psum_tensor("ps%d" % i, [128, 512], F32)) for i in range(8)]
        psn = [0]

        def bank():
            i = psn[0] % 8
            psn[0] += 1
            return i

        def load(dst_ap, src_ap, key, eng="sp", reads=()):
            return S.dma(eng, "ld_" + key, lambda e: e.dma_start(out=dst_ap, in_=src_ap, allow_slow_non_contiguous=True), writes=[key], reads=reads)

        wsn = [0]

        def wload(segs, krows=D):
            i = wsn[0] % 2
            wsn[0] += 1
            ws = WS[i]
            off = 0
            kc = krows // 128
            for si, (w, row0, c0, n) in enumerate(segs):
                name = WNAME[id(w)]
                src = WBF[name][row0:row0 + krows, c0:c0 + n].rearrange("(kc p) n -> p kc n", p=128)
                dst = ws[:, 0:kc, off:off + n]
                rk = sorted(set(["wb_%s_%d" % (name, c // PCB) for c in (c0, c0 + n - 1)]))
                S.dma("sp", "ws%d_%d" % (i, si), (lambda e, dst=dst, src=src: e.dma_start(out=dst, in_=src)), writes=["ws%d_s%d" % (i, si)], reads=rk)
                off += n
            return i

        def wk(i):
            return ["ws%d_s0" % i, "ws%d_s1" % i]

        def mm_group(out_ap, pairs, reads, wkey):
            def f(e):
                r = None
                n = len(pairs)
                for i, (l, rh) in enumerate(pairs):
                    r = e.matmul(out_ap, lhsT=l, rhs=rh, start=(i == 0), stop=(i == n - 1))
                return r
            return S.op("pe", f, reads=reads, writes=[wkey])

        def mm_multi(items, reads, wkey):
            def f(e):
                r = None
                for (o, l, rh) in items:
                    r = e.matmul(o, lhsT=l, rhs=rh, start=True, stop=True)
                return r
            return S.op("pe", f, reads=reads, writes=[wkey])

        def tr_multi(items, reads, wkey):
            def f(e):
                r = None
                for (o, i_) in items:
                    r = e.transpose(o, i_, K["ident"][:])
                return r
            return S.op("pe", f, reads=list(reads) + ["c_ident"], writes=[wkey])

        def act(out, in_, func, reads, writes, **kw):
            return S.op("act", lambda e: e.activation(out, in_, func, **kw), reads=reads, writes=writes)

        def tt(out, a, b, op, reads, writes, eng="dve"):
            return S.op(eng, lambda e: e.tensor_tensor(out, a, b, op), reads=reads, writes=writes)

        def ts(out, a, s1, s2, op0, op1, reads, writes, eng="dve"):
            if op1 is None:
                return S.op(eng, lambda e: e.tensor_scalar(out, a, s1, s2, op0), reads=reads, writes=writes)
            return S.op(eng, lambda e: e.tensor_scalar(out, a, s1, s2, op0, op1), reads=reads, writes=writes)

        def stt(out, a, s, b, op0, op1, reads, writes, eng="dve"):
            return S.op(eng, lambda e: e.scalar_tensor_tensor(out, a, s, b, op0, op1), reads=reads, writes=writes)

        def rsqrt_small(dst, src, scale, eps, reads, wkey):
            n = src.shape[1]
            ts(dst, src, scale, eps, ALU.mult, ALU.add, reads, [wkey])
            S.op("pool", lambda e: e.tensor_tensor(dst, dst, K["neghalf"][:, 0:n], ALU.pow), reads=[wkey, "c_neghalf"], writes=[wkey])

        def store_dbg(name, ap, key):
            if name in dbg_d:
                S.dma("sp", "dbg_" + name, lambda e: e.dma_start(out=dbg_d[name], in_=ap), reads=[key])

        for k in CONST_SHAPES:
            load(K[k][:], cd[k], "c_" + k)
        def precast(name, cb):
            src = WSRC[name]
            ncols = src.shape[1]
            c0 = cb * PCB
            n = min(PCB, ncols - c0)
            key = "wb_%s_%d" % (name, cb)
            S.dma("pool", "pc_" + key, (lambda e, src=src, name=name, c0=c0, n=n: e.dma_start(out=WBF[name][:, c0:c0 + n], in_=src[:, c0:c0 + n])), writes=[key])

        precast_todo = [("w_in", cb) for cb in (0, 3, 4, 7, 8, 9, 10, 11)]
        for name in ("w_ret_o", "w_hg_o", "w_out"):
            precast_todo += [(name, cb) for cb in range(2)]
        precast_todo += [("w_up", cb) for cb in range(11)] + [("w_down", cb) for cb in range(2)]
        for cb in (1, 2, 5, 6):
            precast("w_in", cb)
        if not any(t[1] == "s" for t in tiles):
            while precast_todo:
                precast(*precast_todo.pop(0))
        load(vmask[:], vmask_d, "vmask")
        load(pmask[:], pmask_d, "pmask")
        S.op("pool", lambda e: e.memset(halo[:], 0.0), writes=["halo"])
        S.op("pool", lambda e: e.memset(Rst[:], 0.0), writes=["Rst"])
        S.op("pool", lambda e: e.memset(Rbf[:], 0.0), writes=["Rbf"])
        S.op("pool", lambda e: e.memset(Sst[:], 0.0), writes=["Sst"])
        S.op("pool", lambda e: e.memset(Sbf[:], 0.0), writes=["Sbf0", "Sbf1", "Sbf2", "Sbf3"])
        cf = small[:, 0:16]
        g1f = small[:, 16:32]
        g2f = small[:, 32:48]
        scf = small[:, 48:64]
        lbf = small[:, 64:80]
        with nc.allow_non_contiguous_dma(reason="tiny one-time strided loads"):
            for j in range(3):
                load(cwf[:, :, j:j + 1], cw_d[j:j + 1, :].rearrange("o (b p) -> p b o", p=128), "cwf")
            load(cwf[:, :, 3:4], cb_d.rearrange("o (b p) -> p b o", p=128), "cwf")
            load(cf, c_d.rearrange("o (k p) -> p (o k)", p=128), "small_c")
            load(g1f, g1_d.rearrange("o (k p) -> p (o k)", p=128), "small_g1")
            load(g2f, g2_d.rearrange("o (k p) -> p (o k)", p=128), "small_g2")
            load(lbf.rearrange("p (l h) -> p l h", l=2), lb_d.rearrange("l (h p) -> p l h", p=128), "small_lb")
        act(scf, cf, AF.Silu, ["small_c"], ["small_sc"])
        sn_ = [0]

        def gemv_block(g, nb):
            col = g * 2048 + nb * 512
            load(brow2, b_ada[0:1, col:col + 512], "brow2")
            pb = bank()
            for kq in range(4):
                buf = (tgA, tgB)[sn_[0] % 2]
                key = "tg%d" % (sn_[0] % 2)
                sn_[0] += 1
                src = w_ada[kq * 512:(kq + 1) * 512, col:col + 512].rearrange("(k p) n -> p k n", p=128)
                load(buf.rearrange("p (k n) -> p k n", k=4), src, key)

                def f(e, kq=kq, pb=pb, buf=buf):
                    r = None
                    for k4 in range(4):
                        kc = kq * 4 + k4
                        r = e.matmul(PS[pb][0:1, :], lhsT=scf[:, kc:kc + 1], rhs=buf[:, k4 * 512:(k4 + 1) * 512],
                                     start=(kc == 0), stop=(kc == 15))
                    return r
                S.op("pe", f, reads=[key, "small_sc"], writes=["ps%d" % pb])
            tt(mrow[:, nb * 512:(nb + 1) * 512], PS[pb][0:1, :], brow2, ALU.add, ["ps%d" % pb, "brow2"], ["mrow"])
            if nb == 3:
                S.dma("sp", "st_mod", lambda e, g=g: e.dma_start(out=mod_d[0:1, g * 2048:(g + 1) * 2048], in_=mrow), reads=["mrow"], writes=["mod_d%d" % g])

        for g in range(2):
            for nb in range(4):
                gemv_block(g, nb)
        gemv_todo = [(g, nb) for g in range(2, 6) for nb in range(4)]
        s1f = small[:, 80:96]
        fmv = lambda g: mod_d[0:1, g * 2048:(g + 1) * 2048].rearrange("o (k p) -> p (o k)", p=128)
        load(AB[:, 1, :], fmv(0), "AB1", reads=["mod_d0"])
        load(s1f, fmv(1), "small_s1", reads=["mod_d1"])
        stt(AB[:, 0, :], s1f, 1.0, g1f, ALU.add, ALU.mult, ["small_s1", "small_g1"], ["AB0"])

        def late_mod():
            while gemv_todo:
                gemv_block(*gemv_todo.pop(0))
            load(AB[:, 3, :], fmv(3), "AB3", reads=["mod_d3"])
            load(s1f, fmv(4), "small_s1", reads=["mod_d4"])
            stt(AB[:, 2, :], s1f, 1.0, g2f, ALU.add, ALU.mult, ["small_s1", "small_g2"], ["AB2"])
        load(tgA[:, 0:1024], lb_d[0:1, :].partition_broadcast(128), "tg0")
        load(tgA[:, 1024:2048], lb_d[1:2, :].partition_broadcast(128), "tg0")
        act(tgA, tgA, AF.Exp, ["tg0"], ["tg0"])
        tt(tgB[:, 0:1024], tgA[:, 0:1024], tgA[:, 1024:2048], ALU.add, ["tg0"], ["tg1"])
        S.op("dve", lambda e: e.reciprocal(tgB[:, 0:1024], tgB[:, 0:1024]), reads=["tg1"], writes=["tg1"])
        tt(LBb[:], tgA[:, 0:1024], tgB[:, 0:1024], ALU.mult, ["tg0", "tg1"], ["LBb"])
        ts(OMLb[:], LBb[:], -1.0, 1.0, ALU.mult, ALU.add, ["LBb"], ["OMLb"])
        act(lbf, lbf, AF.Exp, ["small_lb"], ["small_lb"])
        tt(omlf[:], lbf[:, 0:8], lbf[:, 8:16], ALU.add, ["small_lb"], ["omlf"])
        S.op("dve", lambda e: e.reciprocal(omlf[:], omlf[:]), reads=["omlf"], writes=["omlf"])
        tt(omlf[:], omlf[:], lbf[:, 8:16], ALU.mult, ["omlf", "small_lb"], ["omlf"])

        def make_ops(L):
            xt = L.get('xt')
            hT = L.get('hT')
            cosT = L.get('cosT')
            sinT = L.get('sinT')
            posi = L.get('posi')
            sqj = L.get('sqj')
            dgs = L.get('dgs')
            t1 = L.get('t1')
            t2 = L.get('t2')
            kf32 = L.get('kf32')
            logf = L.get('logf')
            ktm = L.get('ktm')
            sgm = L.get('sgm')
            ecb = L.get('ecb')
            eb = L.get('eb')
            kbf = L.get('kbf')
            kzt = L.get('kzt')
            vret = L.get('vret')
            khat = L.get('khat')
            vhg = L.get('vhg')
            kz = L.get('kz')
            ogT = L.get('ogT')
            mgT = L.get('mgT')
            gT = L.get('gT')
            Gb = L.get('Gb')
            sgr = L.get('sgr')
            osb = L.get('osb')
            osq = L.get('osq')
            enb = L.get('enb')
            sgate = L.get('sgate')
            ybuf = L.get('ybuf')
            ua = L.get('ua')
            ub = L.get('ub')
            qrot = L.get('qrot')
            qhat = L.get('qhat')
            inn = L.get('inn')
            qt = L.get('qt')
            kt = L.get('kt')
            amt = L.get('amt')
            qz = L.get('qz')
            sqjb = sqj.bitcast(BF16)
            sgA = sgB = None
            if enb is not None:
                tq = eb.shape[1] * 512 // 4
                sgA = eb.rearrange("p c n -> p (c n)").rearrange("p (m t) -> p m t", m=4)
                sgB = enb.rearrange("p c n -> p (c n)").rearrange("p (m t) -> p m t", m=4)
            def norm_to_hT(ai, bi, cpt):
                ss = small[:, 0:cpt]
                rs = small[:, 8:8 + cpt]
                for c in range(cpt):
                    act(sqjb[:, 0:2048], xt[:, c, :], AF.Square, ["xt"], ["sqj", "ss"], accum_out=ss[:, c:c + 1])
                rsqrt_small(rs, ss, 1.0 / D, EPS, ["ss"], "rs")
                for c in range(cpt):
                    ts(dgs[:, c, :], K["ident"][:], rs[:, c:c + 1], None, ALU.mult, None, ["c_ident", "rs"], ["dgs%d" % c])
                    for q in range(4):
                        pb = bank()
                        mm_multi([(PS[pb][:, k4 * 128:(k4 + 1) * 128], xt[:, c, (q * 4 + k4) * 128:(q * 4 + k4 + 1) * 128], dgs[:, c, :]) for k4 in range(4)],
                                 ["xt", "dgs%d" % c], "ps%d" % pb)
                        for k4 in range(4):
                            kc = q * 4 + k4
                            act(hT[:, kc, c * 128:(c + 1) * 128], PS[pb][:, k4 * 128:(k4 + 1) * 128], AF.Identity,
                                ["ps%d" % pb, "AB%d" % ai, "AB%d" % bi], ["hT"], scale=AB[:, ai, kc:kc + 1], bias=AB[:, bi, kc:kc + 1])

            def gemm_fm(wi, col0, nmb, actT, akey, kcs, tw, cb):
                for mb in range(nmb):
                    pb = bank()
                    pairs = [(WS[wi][:, kci, col0 + mb * 128: col0 + (mb + 1) * 128], actT[:, kc, 0:tw]) for kci, kc in enumerate(kcs)]
                    mm_group(PS[pb][:, 0:tw], pairs, wk(wi) + [akey], "ps%d" % pb)
                    cb(mb, pb)

            def gemm_tm(wi, col0, ncols, actT, akey, kcs, cpt, cb):
                for c in range(cpt):
                    pb = bank()
                    pairs = [(actT[:, kc, c * 128:(c + 1) * 128], WS[wi][:, kci, col0:col0 + ncols]) for kci, kc in enumerate(kcs)]
                    mm_group(PS[pb][:, 0:ncols], pairs, wk(wi) + [akey], "ps%d" % pb)
                    cb(c, pb)

            def rope_tables(tok0, tw_all):
                for off in range(0, tw_all, 256):
                    rope_piece(tok0 + off, off, min(256, tw_all - off))

            def rope_piece(tok0, off, tw):
                load(posi[:, 0:tw], pos_d[0:1, tok0:tok0 + tw].partition_broadcast(128), "posi")
                ang = sqj[:, 0:tw]
                kf = sqj[:, 256:256 + tw]
                ki = sqj[:, 512:512 + tw].bitcast(I32)
                a2 = sqj[:, 768:768 + tw]
                kk = ["sqj"]
                S.op("dve", lambda e: e.tensor_copy(ang, posi[:, 0:tw]), reads=["posi"], writes=kk)
                ts(ang, ang, K["invf"][:, 0:1], None, ALU.mult, None, kk + ["c_invf"], kk)
                for shift, dst, key in ((0.0, sinT, "sinT"), (float(np.pi / 2), cosT, "cosT")):
                    ts(a2, ang, shift, None, ALU.add, None, kk, kk)
                    ts(kf, a2, float(1.0 / (2 * np.pi)), None, ALU.mult, None, kk, kk)
                    S.op("dve", lambda e: e.tensor_copy(ki, kf), reads=kk, writes=kk)
                    S.op("dve", lambda e: e.tensor_copy(kf, ki), reads=kk, writes=kk)
                    stt(a2, kf, -6.28125, a2, ALU.mult, ALU.add, kk, kk)
                    stt(a2, kf, -float(2 * np.pi - 6.28125), a2, ALU.mult, ALU.add, kk, kk)
                    ts(kf, a2, float(np.pi), None, ALU.is_gt, None, kk, kk)
                    stt(a2, kf, -float(2 * np.pi), a2, ALU.mult, ALU.add, kk, kk)
                    ts(kf, a2, -float(np.pi), None, ALU.is_lt, None, kk, kk)
                    stt(a2, kf, float(2 * np.pi), a2, ALU.mult, ALU.add, kk, kk)
                    ts(a2, a2, 3.1415925, -3.1415925, ALU.min, ALU.max, kk, kk)
                    act(dst[:, off:off + tw], a2, AF.Sin, kk, [key])

            def ret_head(h, ch0, cpt, so):
                tw = cpt * 128
                if not so:
                    act(Rbf[:, h, :], Rst[:, h, :], AF.Copy, ["Rst"], ["Rbf"])
                segs = [(w_in, 0, 1024 + h * 256, 256)]
                if not so:
                    segs = [(w_in, 0, h * 256, 256)] + segs
                wi = wload(segs)
                kcol = 0 if so else 256
                held = {}

                def rope_cb(is_k):
                    def cb(mb, pb):
                        held[mb] = pb
                        if mb != 1:
                            return
                        p1, p2 = PS[held[0]][:, 0:tw], PS[held[1]][:, 0:tw]
                        k12 = ["ps%d" % held[0], "ps%d" % held[1]]
                        for half, (pa, pbb, op) in enumerate(((p1, p2, ALU.subtract), (p2, p1, ALU.add))):
                            tt(t1[:, 0:tw], pa, cosT[:, 0:tw], ALU.mult, k12 + ["cosT"], ["t1"])
                            tt(t2[:, 0:tw], pbb, sinT[:, 0:tw], ALU.mult, k12 + ["sinT"], ["t2"])
                            if is_k:
                                tt(kf32[:, half, 0:tw], t1[:, 0:tw], t2[:, 0:tw], op, ["t1", "t2"], ["kf32"])
                                if not so:
                                    act(kbf[:, half, 0:tw], kf32[:, half, 0:tw], AF.Copy, ["kf32"], ["kbf"])
                            else:
                                tt(t1[:, 0:tw], t1[:, 0:tw], t2[:, 0:tw], op, ["t1", "t2"], ["t1"])
                                act(qrot[:, half, 0:tw], t1[:, 0:tw], AF.Copy, ["t1"], ["qrot"])
                                tt(qhat[:, half, 0:tw].rearrange("p (c t) -> p c t", c=cpt), t1[:, 0:tw].rearrange("p (c t) -> p c t", c=cpt),
                                   bc_mid(K["xib"][:, h, :], cpt), ALU.mult, ["t1", "c_xib"], ["qhat"])
                    return cb
                gemm_fm(wi, kcol, 2, hT, "hT", range(KC), tw, rope_cb(True))
                if not so:
                    gemm_fm(wi, 0, 2, hT, "hT", range(KC), tw, rope_cb(False))
                wi2 = wload([(w_in, 0, 2048 + h * 256, 256)] + ([] if so else [(w_in, 0, 3072 + h * 256, 256)]))

                def vg_cb(c, pb):
                    gi = ch0 + c
                    act(vret[:, c, :], PS[pb][:, 0:256], AF.Identity, ["ps%d" % pb, "vmask"], ["vret"], scale=vmask[:, gi:gi + 1])
                    if not so:
                        act(sgr[:, c, :], PS[pb][:, 256:512], AF.Silu, ["ps%d" % pb], ["sgr"])
                gemm_tm(wi2, 0, 256 if so else 512, hT, "hT", range(KC), cpt, vg_cb)
                for c in range(cpt):
                    pb = bank()
                    tr_multi([(PS[pb][:, k2 * 128:(k2 + 1) * 128], kf32[:, k2, c * 128:(c + 1) * 128]) for k2 in range(2)], ["kf32"], "ps%d" % pb)
                    act(kzt[:, c, :], PS[pb][:, 0:256], AF.Identity, ["ps%d" % pb, "c_zeta"], ["kzt"], scale=K["zeta"][:, h:h + 1])
                for c in range(cpt):
                    cs = slice(c * 128, (c + 1) * 128)
                    if not so:
                        pbs = bank()
                        mm_group(PS[pbs][:, 0:128], [(kbf[:, k2, cs], qrot[:, k2, cs]) for k2 in range(2)], ["kbf", "qrot"], "ps%d" % pbs)
                        tt(inn[:], PS[pbs][:, 0:128], K["decT"][:, h, :], ALU.mult, ["ps%d" % pbs, "c_decT"], ["inn"])
                        pbo = bank()
                        pairs = [(inn[:], vret[:, c, :])] + [(qhat[:, k2, cs], Rbf[:, h, k2 * 256:(k2 + 1) * 256]) for k2 in range(2)]
                        mm_group(PS[pbo][:, 0:256], pairs, ["inn", "vret", "qhat", "Rbf"], "ps%d" % pbo)
                        pbr = bank()
                        mm_multi([(PS[pbr][:, k2 * 256:(k2 + 1) * 256], kzt[:, c, k2 * 128:(k2 + 1) * 128], vret[:, c, :]) for k2 in range(2)],
                                 ["kzt", "vret"], "ps%d" % pbr)
                        stt(Rst[:, h, :], Rst[:, h, :], gchunk[h], PS[pbr][:, 0:512], ALU.mult, ALU.add, ["Rst", "ps%d" % pbr], ["Rst"])
                        if not so:
                            act(Rbf[:, h, :], Rst[:, h, :], AF.Copy, ["Rst"], ["Rbf"])
                        st_ = small[:, 16:24]
                        o_ = osb[:, 0:256]
                        act(o_, PS[pbo][:, 0:256], AF.Copy, ["ps%d" % pbo], ["osb", "st0"], accum_out=st_[:, 0:1])
                        act(osq[:, 0:256], o_, AF.Square, ["osb"], ["osq", "st1"], accum_out=st_[:, 1:2])
                        ts(st_[:, 2:3], st_[:, 0:1], 1.0 / 256, None, ALU.mult, None, ["st0"], ["st2"])
                        tt(st_[:, 3:4], st_[:, 2:3], st_[:, 2:3], ALU.mult, ["st2"], ["st3"])
                        stt(st_[:, 4:5], st_[:, 1:2], 1.0 / 256, st_[:, 3:4], ALU.mult, ALU.subtract, ["st1", "st3"], ["st4"])
                        rsqrt_small(st_[:, 5:6], st_[:, 4:5], 1.0, HEPS, ["st4"], "st5")
                        ts(o_, o_, st_[:, 2:3], st_[:, 5:6], ALU.subtract, ALU.mult, ["osb", "st2", "st5"], ["osb"])
                        tt(o_, o_, sgr[:, c, :], ALU.mult, ["osb", "sgr"], ["osb"])
                        pbt = bank()
                        tr_multi([(PS[pbt][:, k2 * 128:(k2 + 1) * 128], osb[:, k2 * 128:(k2 + 1) * 128]) for k2 in range(2)], ["osb"], "ps%d" % pbt)
                        for k2 in range(2):
                            act(ogT[:, h * 2 + k2, cs], PS[pbt][:, k2 * 128:(k2 + 1) * 128], AF.Copy, ["ps%d" % pbt], ["ogT"])
                    if so:
                        pbr = bank()
                        mm_multi([(PS[pbr][:, k2 * 256:(k2 + 1) * 256], kzt[:, c, k2 * 128:(k2 + 1) * 128], vret[:, c, :]) for k2 in range(2)],
                                 ["kzt", "vret"], "ps%d" % pbr)
                        stt(Rst[:, h, :], Rst[:, h, :], gchunk[h], PS[pbr][:, 0:512], ALU.mult, ALU.add, ["Rst", "ps%d" % pbr], ["Rst"])
                        if not so:
                            act(Rbf[:, h, :], Rst[:, h, :], AF.Copy, ["Rst"], ["Rbf"])

            def hg_group(g, ch0, cpt, so):
                tw = cpt * 128
                c0 = 4096 + g * 512
                wif = wload([(w_in, 0, c0 + 1024, 512)])

                o1 = L.get("_off_t1")
                alt = [AR[:, o1 + i * 512: o1 + (i + 1) * 512] for i in range(4)]
                BS = [dict(sgm=sgm, ktm=ktm, logf=logf, ecb=ecb, k="", g=[]),
                      dict(sgm=alt[0], ktm=alt[1], logf=alt[2], ecb=alt[3], k="_b", g=["t1", "t2", "kf32"])]

                def hf_p1(c, pb, B, first):
                    kx, gd = B["k"], B["g"]
                    act(B["sgm"], PS[pb][:, 0:512], AF.Sigmoid, ["ps%d" % pb], ["sgm" + kx] + (gd if first else []))
                    tt(B["sgm"], B["sgm"], OMLb[:, g * 512:(g + 1) * 512], ALU.mult, ["sgm" + kx, "OMLb"] + gd, ["sgm" + kx])
                    tt(B["sgm"], B["sgm"], LBb[:, g * 512:(g + 1) * 512], ALU.add, ["sgm" + kx, "LBb"] + gd, ["sgm" + kx])
                    ts(B["ktm"], B["sgm"], -1.0, 1.0, ALU.mult, ALU.add, ["sgm" + kx] + gd, ["ktm" + kx])
                    act(B["logf"], B["sgm"], AF.Ln, ["sgm" + kx] + gd, ["logf" + kx])

                def hf_p2(c, B):
                    kx, gd = B["k"], B["g"]
                    pbc = bank()
                    mm_multi([(PS[pbc][:, 0:512], K["trirev"][:], B["logf"])], ["c_trirev", "logf" + kx] + gd, "ps%d" % pbc)
                    act(B["ecb"], PS[pbc][:, 0:512], AF.Exp, ["ps%d" % pbc] + gd, ["ecb" + kx])
                    tt(khat[:, c, :], B["ktm"], B["ecb"], ALU.mult, ["ktm" + kx, "ecb" + kx] + gd, ["khat"])
                    pbb = bank()
                    mm_multi([(PS[pbb][:, hh * 128:(hh + 1) * 128], B["logf"][:, hh * 128:(hh + 1) * 128], K["triinc"][:]) for hh in range(4)],
                             ["logf" + kx, "c_triinc"] + gd, "ps%d" % pbb)
                    act(eb[:, c, :], PS[pbb][:, 0:512], AF.Exp, ["ps%d" % pbb], ["eb"])
                    if not so:
                        act(enb[:, c, :], PS[pbb][:, 0:512], AF.Exp, ["ps%d" % pbb], ["enb"], scale=-1.0)

                use_alt = so and cpt >= 2 and o1 is not None
                seen_alt = [False]

                def hf_cb(c, pb):
                    B = BS[c % 2] if use_alt else BS[0]
                    first = False
                    if B is BS[1] and not seen_alt[0]:
                        first = True
                        seen_alt[0] = True
                    hf_p1(c, pb, B, first)
                    if use_alt:
                        if c >= 1:
                            hf_p2(c - 1, BS[(c - 1) % 2])
                        if c == cpt - 1:
                            hf_p2(c, B)
                    else:
                        hf_p2(c, B)
                gemm_tm(wif, 0, 512, hT, "hT", range(KC), cpt, hf_cb)
                if not so:
                    def hfT_cb(mb, pb):
                        act(t1[:, 0:tw], PS[pb][:, 0:tw], AF.Sigmoid, ["ps%d" % pb], ["t1"], scale=-1.0)
                        stt(kt[:, mb, 0:tw].rearrange("p (c t) -> p c t", c=cpt), t1[:, 0:tw].rearrange("p (c t) -> p c t", c=cpt),
                            omlf[:, g * 4 + mb: g * 4 + mb + 1], enb[:, 0:cpt, mb * 128:(mb + 1) * 128], ALU.mult, ALU.mult, ["t1", "omlf", "enb"], ["kt"])
                    gemm_fm(wif, 0, 4, hT, "hT", range(KC), tw, hfT_cb)
                    wiq = wload([(w_in, 0, c0, 512)])

                    def hqT_cb(mb, pb):
                        act(t2[:, 0:tw], PS[pb][:, 0:tw], AF.Silu, ["ps%d" % pb], ["t2"])
                        tt(qt[:, mb, 0:tw].rearrange("p (c t) -> p c t", c=cpt), t2[:, 0:tw].rearrange("p (c t) -> p c t", c=cpt),
                           eb[:, 0:cpt, mb * 128:(mb + 1) * 128], ALU.mult, ["t2", "eb"], ["qt"])
                    gemm_fm(wiq, 0, 4, hT, "hT", range(KC), tw, hqT_cb)
                wiv = wload([(w_in, 0, c0 + 2048, 512)])

                def v_cb(c, pb):
                    gi = ch0 + c
                    act(vhg[:, c, :], PS[pb][:, 0:512], AF.Identity, ["ps%d" % pb, "vmask"], ["vhg"], scale=vmask[:, gi:gi + 1])
                gemm_tm(wiv, 0, 512, hT, "hT", range(KC), cpt, v_cb)
                if not so:
                    wig = wload([(w_in, 0, c0 + 3072, 512)])

                    def g_cb(c, pb):
                        act(sgate[:, c, :], PS[pb][:, 0:512], AF.Silu, ["ps%d" % pb], ["sgate"])
                    gemm_tm(wig, 0, 512, hT, "hT", range(KC), cpt, g_cb)
                if not so:
                    act(Sbf[:, 0, :, :], Sst[:, g * 4:g * 4 + 4, :], AF.Copy, ["Sst"], ["Sbf0"])
                for c in range(cpt):
                    cs = slice(c * 128, (c + 1) * 128)
                    if not so:
                        pba = bank()
                        mm_multi([(PS[pba][:, hh * 128:(hh + 1) * 128], kt[:, hh, cs], qt[:, hh, cs]) for hh in range(4)], ["kt", "qt"], "ps%d" % pba)
                        tt(amt[:], PS[pba][:, 0:512].rearrange("p (h t) -> p h t", h=4), bc_mid(K["triinc"][:], 4), ALU.mult,
                           ["ps%d" % pba, "c_triinc"], ["amt"])
                        for j in range(4):
                            tt(qz[:, j, :, :], qt[:, :, cs], bc_mid(K["colmask"][:, j, :], 4), ALU.mult, ["qt", "c_colmask"], ["qz"], eng="pool")

                    def sub_update(j):
                        kzb, kzk = kz, "kz"
                        if so and (j % 2 == 1):
                            kzb, kzk = kz2, "kz2"
                        act(kzb, khat[:, c, :], AF.Identity, ["khat", "c_rowmask"], [kzk], scale=K["rowmask"][:, j:j + 1])
                        pbd = bank()
                        mm_multi([(PS[pbd][:, hh * 128:(hh + 1) * 128], kzb[:, hh * 128:(hh + 1) * 128], vhg[:, c, hh * 128:(hh + 1) * 128]) for hh in range(4)],
                                 [kzk, "vhg"], "ps%d" % pbd)
                        for hh in range(4):
                            hd = g * 4 + hh
                            col = hh * 128 + j * 32 + 31
                            stt(Sst[:, hd, :], Sst[:, hd, :], eb[:, c, col:col + 1], PS[pbd][:, hh * 128:(hh + 1) * 128], ALU.mult, ALU.add,
                                ["Sst", "eb", "ps%d" % pbd], ["Sst"])
                        jn = (j + 1) % 4
                        if not so:
                            act(Sbf[:, jn, :, :], Sst[:, g * 4:g * 4 + 4, :], AF.Copy, ["Sst"], ["Sbf%d" % jn])
                    for j in range(3):
                        sub_update(j)
                    if not so:
                        pbo = bank()

                        def f(e, pbo=pbo, c=c):
                            r = None
                            for hh in range(4):
                                o = PS[pbo][:, hh * 128:(hh + 1) * 128]
                                e.matmul(o, lhsT=amt[:, hh, :], rhs=vhg[:, c, hh * 128:(hh + 1) * 128], start=True, stop=False)
                                for j in range(4):
                                    r = e.matmul(o, lhsT=qz[:, j, hh, :], rhs=Sbf[:, j, hh, :], start=False, stop=(j == 3))
                            return r
                        S.op("pe", f, reads=["amt", "vhg", "qz", "Sbf0", "Sbf1", "Sbf2", "Sbf3"], writes=["ps%d" % pbo])
                        sub_update(3)
                        ss_ = small[:, 24:28]
                        rs_ = small[:, 28:32]
                        for hh in range(4):
                            act(osq[:, hh * 128:(hh + 1) * 128], PS[pbo][:, hh * 128:(hh + 1) * 128], AF.Square, ["ps%d" % pbo], ["osq", "hss"], accum_out=ss_[:, hh:hh + 1])
                        rsqrt_small(rs_, ss_, 1.0 / 128, HEPS, ["hss"], "hrs")
                        for hh in range(4):
                            stt(osb[:, hh * 128:(hh + 1) * 128], PS[pbo][:, hh * 128:(hh + 1) * 128], rs_[:, hh:hh + 1], sgate[:, c, hh * 128:(hh + 1) * 128],
                                ALU.mult, ALU.mult, ["ps%d" % pbo, "hrs", "sgate"], ["osb"])
                        pbt = bank()
                        tr_multi([(PS[pbt][:, hh * 128:(hh + 1) * 128], osb[:, hh * 128:(hh + 1) * 128]) for hh in range(4)], ["osb"], "ps%d" % pbt)
                        for hh in range(4):
                            act(ogT[:, 8 + g * 4 + hh, cs], PS[pbt][:, hh * 128:(hh + 1) * 128], AF.Copy, ["ps%d" % pbt], ["ogT"])
                    if so:
                        sub_update(3)


            def merge_and_out2(cpt):
                tw = cpt * 128
                for jb in range(4):
                    wa = wload([(w_in, 0, 8192 + jb * 512, 512)])
                    wb = wload([(w_in, 0, 10240 + jb * 512, 512)])
                    sig = []
                    for mb in range(4):
                        kcb = jb * 4 + mb
                        pga, pgb = bank(), bank()
                        mm_group(PS[pga][:, 0:tw], [(WS[wa][:, kc, mb * 128:(mb + 1) * 128], hT[:, kc, 0:tw]) for kc in range(KC)], wk(wa) + ["hT"], "ps%d" % pga)
                        mm_group(PS[pgb][:, 0:tw], [(WS[wb][:, kc, mb * 128:(mb + 1) * 128], hT[:, kc, 0:tw]) for kc in range(KC)], wk(wb) + ["hT"], "ps%d" % pgb)
                        act(sgA[:, mb, 0:tw], PS[pga][:, 0:tw], AF.Sigmoid, ["ps%d" % pga], ["eb"])
                        act(sgB[:, mb, 0:tw], PS[pgb][:, 0:tw], AF.Sigmoid, ["ps%d" % pgb], ["enb"])
                    wo = wload([(w_ro, 0, jb * 512, 512)], krows=1024)
                    srcb = WBF["w_hg_o"][:, jb * 512:(jb + 1) * 512].rearrange("(kc p) n -> p kc n", p=128)
                    S.dma("sp", "ws%d_1" % wo, (lambda e, wo=wo, srcb=srcb: e.dma_start(out=WS[wo][:, 8:16, :], in_=srcb)), writes=["ws%d_s1" % wo],
                          reads=["wb_w_hg_o_%d" % ((jb * 512) // PCB)])
                    for mb in range(4):
                        kcb = jb * 4 + mb
                        pya, pyb = bank(), bank()
                        mm_group(PS[pya][:, 0:tw], [(WS[wo][:, kc, mb * 128:(mb + 1) * 128], ogT[:, kc, 0:tw]) for kc in range(8)], wk(wo) + ["ogT"], "ps%d" % pya)
                        mm_group(PS[pyb][:, 0:tw], [(WS[wo][:, 8 + kc, mb * 128:(mb + 1) * 128], ogT[:, 8 + kc, 0:tw]) for kc in range(8)], wk(wo) + ["ogT"], "ps%d" % pyb)
                        tt(ua[:, 0:tw], sgA[:, mb, 0:tw], PS[pya][:, 0:tw], ALU.mult, ["eb", "ps%d" % pya], ["ua"])
                        tt(ub[:, 0:tw], sgB[:, mb, 0:tw], PS[pyb][:, 0:tw], ALU.mult, ["enb", "ps%d" % pyb], ["ub"])
                        tt(mgT[:, kcb, 0:tw], ua[:, 0:tw], ub[:, 0:tw], ALU.add, ["ua", "ub"], ["mgT"], eng="pool")
                load(Gb[:], mod_d[0:1, 2 * D:3 * D].partition_broadcast(128), "Gb", reads=["mod_d2"])
                for jb in range(4):
                    wo = wload([(w_out, 0, jb * 512, 512)])

                    gemm_tm(wo, 0, 512, mgT, "mgT", range(KC), cpt, lambda c, pb, jb=jb: resid_cb(c, pb, jb))

            def resid_cb(c, pb, jb):
                tt(osq[:, 0:512], PS[pb][:, 0:512], Gb[:, jb * 512:(jb + 1) * 512], ALU.mult, ["ps%d" % pb, "Gb"], ["osq"])
                tt(xt[:, c, jb * 512:(jb + 1) * 512], xt[:, c, jb * 512:(jb + 1) * 512], osq[:, 0:512], ALU.add, ["xt", "osq"], ["xt"], eng="pool")

            def ffn(ti, cpt, store=True):
                tw = cpt * 128
                for fb in range(11):
                    wa = wload([(w_up, 0, fb * 512, 512)])
                    wb = wload([(w_up, 0, DFF + fb * 512, 512)])
                    for mb in range(4):
                        blk = fb * 4 + mb
                        res = {}
                        for which, wi in ((0, wa), (1, wb)):
                            ch = which * NFB + blk
                            pb = bank()
                            mm_group(PS[pb][:, 0:tw], [(WS[wi][:, kc, mb * 128:(mb + 1) * 128], hT[:, kc, 0:tw]) for kc in range(KC)], wk(wi) + ["hT"], "ps%d" % pb)
                            if not store:
                                act(halo[:, ch, :], PS[pb][:, tw - 2:tw], AF.Copy, ["ps%d" % pb], ["halo"])
                                continue
                            yb = ybuf[:, which, :]
                            yk = "ybuf%d" % which
                            act(yb[:, 2:2 + tw], PS[pb][:, 0:tw], AF.Copy, ["ps%d" % pb], [yk])
                            ts(yb[:, 0:2], halo[:, ch, :], pmask[:, ti:ti + 1], None, ALU.mult, None, ["halo", "pmask"], [yk], eng="pool")
                            u = ua if which == 0 else ub
                            uk = "ua" if which == 0 else "ub"
                            act(u[:, 0:tw], yb[:, 2:2 + tw], AF.Identity, [yk, "cwf"], [uk], scale=cwf[:, ch, 2:3], bias=cwf[:, ch, 3:4])
                            stt(u[:, 0:tw], yb[:, 1:1 + tw], cwf[:, ch, 1:2], u[:, 0:tw], ALU.mult, ALU.add, [yk, "cwf", uk], [uk])
                            stt(u[:, 0:tw], yb[:, 0:tw], cwf[:, ch, 0:1], u[:, 0:tw], ALU.mult, ALU.add, [yk, "cwf", uk], [uk])
                            S.op("pool", lambda e, yb=yb, ch=ch: e.tensor_copy(halo[:, ch, :], yb[:, tw:tw + 2]), reads=[yk], writes=["halo"])
                        if store:
                            act(t1[:, 0:tw], ua[:, 0:tw], AF.Silu, ["ua"], ["t1"])
                            tt(gT[:, blk, 0:tw], t1[:, 0:tw], ub[:, 0:tw], ALU.mult, ["t1", "ub"], ["gT"])
                if ti == 0:
                    store_dbg("gT", gT[:], "gT")
                if not store:
                    return
                load(Gb[:], mod_d[0:1, 5 * D:6 * D].partition_broadcast(128), "Gb", reads=["mod_d5"])
                for jb in range(4):
                    pbs = [bank() for _ in range(cpt)]
                    parts = [(0, 16), (16, 16), (32, 12)]
                    for pi, (f0, nf) in enumerate(parts):
                        wd = wload([(w_dn, f0 * 128, jb * 512, 512)], krows=nf * 128)
                        for c in range(cpt):
                            def f(e, c=c, wd=wd, f0=f0, nf=nf, pi=pi, pbs=pbs):
                                r = None
                                for k in range(nf):
                                    r = e.matmul(PS[pbs[c]][:, 0:512], lhsT=gT[:, f0 + k, c * 128:(c + 1) * 128], rhs=WS[wd][:, k, 0:512],
                                                 start=(pi == 0 and k == 0), stop=(pi == 2 and k == nf - 1))
                                return r
                            S.op("pe", f, reads=wk(wd) + ["gT"], writes=["ps%d" % pbs[c]])
                    for c in range(cpt):
                        resid_cb(c, pbs[c], jb)

            def final_store(cpt, row0):
                if row0 == 0:
                    store_dbg("x2", xt[:], "xt")
                load(Gb[:], gf_d.partition_broadcast(128), "Gb")
                ss = small[:, 0:cpt]
                rs = small[:, 8:8 + cpt]
                for c in range(cpt):
                    act(sqjb[:, 0:2048], xt[:, c, :], AF.Square, ["xt"], ["sqj", "ss"], accum_out=ss[:, c:c + 1])
                rsqrt_small(rs, ss, 1.0 / D, EPS, ["ss"], "rs")
                for c in range(cpt):
                    stt(xt[:, c, :], xt[:, c, :], rs[:, c:c + 1], Gb[:], ALU.mult, ALU.mult, ["xt", "rs", "Gb"], ["xt"])
                    S.dma("sp", "st_out", lambda e, c=c: e.dma_start(out=out_d[row0 + c * 128: row0 + (c + 1) * 128, :], in_=xt[:, c, :]), reads=["xt"])


            return dict(norm_to_hT=norm_to_hT, rope_tables=rope_tables, ret_head=ret_head, hg_group=hg_group,
                        merge_and_out2=merge_and_out2, ffn=ffn, final_store=final_store, xt=xt)

        OPS = {"s": make_ops(LS), "f": make_ops(LF)}
        print("SBUF bytes/partition (final):", sbytes[0])
        ch0 = 0
        prev_mode = None
        S.barrier()
        for ti, (cpt, mode, row0) in enumerate(tiles):
            so = mode == "s"
            if mode == "f" and prev_mode != "f":
                while precast_todo:
                    precast(*precast_todo.pop(0))
                late_mod()
            if prev_mode is not None and prev_mode != mode:
                S.barrier()
            prev_mode = mode
            O = OPS[mode]
            for c in range(cpt):
                load(O["xt"][:, c, :], x_d[(ch0 + c) * 128:(ch0 + c + 1) * 128, :], "xt")
            O["norm_to_hT"](0, 1, cpt)
            if so and precast_todo:
                precast(*precast_todo.pop(0))
            O["rope_tables"](ch0 * 128, cpt * 128)
            if ti == 0 and "cos" in dbg_d:
                store_dbg("cos", (LS if so else LF)["cosT"], "cosT")
                store_dbg("sin", (LS if so else LF)["sinT"], "sinT")
                store_dbg("sqj", (LS if so else LF)["sqj"], "sqj")
            for h in range(RET_H):
                O["ret_head"](h, ch0, cpt, so)
                if ti == 0 and h == 0 and "kf32" in dbg_d:
                    store_dbg("kf32", (LS if so else LF)["kf32"], "kf32")
                    store_dbg("R0", Rst[:, 0, :], "Rst")
            for g in range(2):
                O["hg_group"](g, ch0, cpt, so)
            if not so and row0 == 0 and "ogT" in dbg_d:
                store_dbg("ogT", LF["ogT"], "ogT")
            if not so:
                O["merge_and_out2"](cpt)
                O["norm_to_hT"](2, 3, cpt)
                O["ffn"](ti, cpt, row0 is not None)
                if row0 is not None:
                    O["final_store"](cpt, row0)
            if so and gemv_todo:
                gemv_block(*gemv_todo.pop(0))
            ch0 += cpt

        fin = [(k, v) for k, v in S.dma_cnt.items() if k.startswith("st_") or k.startswith("dbg_")]
        S.wait_all("sp", fin)
        with nc.Block() as block:
            S.replay(block)
    return nc


CPS = 4


def make_tiles(nstate_chunks, pre_chunks, own_chunks):
    tiles = []
    n = nstate_chunks
    while n > 0:
        c = min(CPS, n)
        tiles.append((c, "s", None))
        n -= c
    n = pre_chunks
    while n > 0:
        c = min(CPTMAX, n)
        tiles.append((c, "f", None))
        n -= c
    row = 0
    n = own_chunks
    while n > 0:
        c = min(CPTMAX, n)
        tiles.append((c, "f", row))
        row += c * 128
        n -= c
    return tiles


_CACHE = {}


def kernel(x, c, positions, w_ada, b_ada, g_norm1, w_in, w_ret_o, w_hg_o, w_out,
           hg_lb, g_norm2, w_up, conv_w, conv_b, w_down, g_final):
    own = SEQ // NCORES // 128
    pre = 1
    nstate = (NCORES - 1) * own - pre
    tiles = make_tiles(nstate, pre, own)
    if "nc" not in _CACHE:
        _CACHE["nc"] = build(tiles)
    nc = _CACHE["nc"]
    nch = nstate + pre + own
    ntok = nch * 128
    xs = np.asarray(x, np.float32).reshape(SEQ, D)
    ps = np.asarray(positions, np.int32).reshape(SEQ)
    HC = host_consts()
    shared = {
        "c": np.asarray(c, np.float32).reshape(1, D),
        "w_ada": np.ascontiguousarray(np.asarray(w_ada, np.float32).reshape(D, 6 * D)),
        "b_ada": np.asarray(b_ada, np.float32).reshape(1, 6 * D),
        "g_norm1": np.asarray(g_norm1, np.float32).reshape(1, D),
        "w_in": np.ascontiguousarray(np.asarray(w_in, np.float32).reshape(D, IN_COLS)),
        "w_ret_o": np.ascontiguousarray(np.asarray(w_ret_o, np.float32).reshape(1024, D)),
        "w_hg_o": np.ascontiguousarray(np.asarray(w_hg_o, np.float32).reshape(1024, D)),
        "w_out": np.ascontiguousarray(np.asarray(w_out, np.float32).reshape(D, D)),
        "hg_lb": np.asarray(hg_lb, np.float32).reshape(2, 1024),
        "g_norm2": np.asarray(g_norm2, np.float32).reshape(1, D),
        "w_up": np.ascontiguousarray(np.asarray(w_up, np.float32).reshape(D, 2 * DFF)),
        "conv_w": np.asarray(conv_w, np.float32).reshape(3, 2 * DFF),
        "conv_b": np.asarray(conv_b, np.float32).reshape(1, 2 * DFF),
        "w_down": np.ascontiguousarray(np.asarray(w_down, np.float32).reshape(DFF, D)),
        "g_final": np.asarray(g_final, np.float32).reshape(1, D),
    }
    for k in CONST_SHAPES:
        shared["k_" + k] = np.ascontiguousarray(HC[k]).reshape(CONST_SHAPES[k])
    in_maps = []
    for core in range(NCORES):
        nreal = (core + 1) * own * 128
        xc = np.zeros((ntok, D), np.float32)
        xc[ntok - nreal:] = xs[:nreal]
        pc = np.zeros((1, ntok), np.int32)
        pc[0, ntok - nreal:] = ps[:nreal]
        tokmask = np.zeros(ntok, np.float32)
        tokmask[ntok - nreal:] = 1.0
        vm = np.ascontiguousarray(tokmask.reshape(nch, 128).T)
        pm = np.zeros((128, len(tiles)), np.float32)
        chs = 0
        for ti, (cpt, mode, row0) in enumerate(tiles):
            prev_real = 1.0 if (chs * 128 - 1) >= (ntok - nreal) else 0.0
            pm[:, ti] = prev_real
            chs += cpt
        m = dict(shared)
        m.update({"x": xc, "positions": pc, "vmask": vm, "pmask": pm})
        in_maps.append(m)
    res = run_bass_kernel_spmd(nc, in_maps, core_ids=list(range(NCORES)))
    outs = [np.asarray(r["out"])[: own * 128] for r in res.results]
    return np.concatenate(outs, axis=0).reshape(1, SEQ, D).astype(np.float32)
```

```python
import contextlib
import numpy as np
import concourse.bass as bass
import concourse.mybir as mybir
from concourse.bass_utils import run_bass_kernel_spmd

F32 = mybir.dt.float32
BF16 = mybir.dt.bfloat16
I32 = mybir.dt.int32
ALU = mybir.AluOpType
AF = mybir.ActivationFunctionType

D = 2048
KC = 16
SEQ = 16384
NCORES = 8
CPTMAX = 2
TMAX = 128 * CPTMAX
DFF = 5632
NFB = DFF // 128
RET_H, HG_H = 4, 8
IN_COLS = 12288
EPS = 1e-6
HEPS = 1e-5


class Sched:
    ENGS = ("pe", "act", "dve", "pool", "sp")

    def __init__(self, nc, stack):
        self.nc = nc
        self.stack = stack
        self.q = {e: [] for e in self.ENGS}
        self.cnt = {e: 0 for e in self.ENGS}
        self.known = {e: {} for e in self.ENGS}
        self.last_w = {}
        self.readers = {}
        self.sems = {}
        self.dma_cnt = {}
        for e in ("pe", "act", "dve", "pool"):
            self.sems[e] = stack.enter_context(nc.semaphore("c_" + e))

    def _sem(self, key):
        if key not in self.sems:
            self.sems[key] = self.stack.enter_context(self.nc.semaphore("d_" + str(key)))
            self.dma_cnt[key] = 0
        return self.sems[key]

    def _deps(self, eng, reads, writes, extra=()):
        need = {}

        def add(t):
            if t is None:
                return
            sk, v = t
            if need.get(sk, 0) < v:
                need[sk] = v
        for k in reads:
            add(self.last_w.get(k))
        for k in writes:
            add(self.last_w.get(k))
            for t in self.readers.get(k, ()):
                add(t)
        for t in extra:
            add(t)
        waits = []
        for sk, v in need.items():
            if sk == eng and eng == "pe":
                continue
            if self.known[eng].get(sk, 0) >= v:
                continue
            self.known[eng][sk] = v
            waits.append((sk, v))
        return waits

    def _commit(self, tok, reads, writes):
        for k in reads:
            self.readers.setdefault(k, []).append(tok)
        for k in writes:
            self.last_w[k] = tok
            self.readers[k] = []

    def op(self, eng, fn, reads=(), writes=()):
        waits = self._deps(eng, reads, writes)
        self.cnt[eng] += 1
        tok = (eng, self.cnt[eng])
        self.q[eng].append((fn, waits, (eng, 1)))
        self._commit(tok, reads, writes)
        return tok

    def dma(self, eng, slot, fn, reads=(), writes=()):
        self._sem(slot)
        waits = self._deps(eng, reads, writes)
        self.dma_cnt[slot] += 16
        tok = (slot, self.dma_cnt[slot])
        self.q[eng].append((fn, waits, (slot, 16)))
        self._commit(tok, reads, writes)
        return tok

    def barrier(self):
        toks = [(e, self.cnt[e]) for e in ("pe", "act", "dve", "pool") if self.cnt[e] > 0]
        toks += [(k, v) for k, v in self.dma_cnt.items() if v > 0]
        for e in self.ENGS:
            self.wait_all(e, toks)

    def wait_all(self, eng, toks):
        waits = self._deps(eng, (), (), toks)
        self.q[eng].append((None, waits, None))

    def replay(self, block):
        engmap = {"pe": block.tensor, "act": block.scalar, "dve": block.vector,
                  "pool": block.gpsimd, "sp": block.sync}
        sems = self.sems
        for e in self.ENGS:
            def body(engine, items=self.q[e]):
                for fn, waits, inc in items:
                    for sk, v in waits:
                        engine.wait_ge(sems[sk], v)
                    if fn is None:
                        continue
                    ins = fn(engine)
                    ins.then_inc(sems[inc[0]], inc[1])
            engmap[e](body)


def host_consts():
    c = {}
    c["ident"] = np.eye(128, dtype=np.float32)
    idx = np.arange(128, dtype=np.float64)
    gam = 1.0 - np.exp2(-5.0 - np.arange(RET_H, dtype=np.float64))
    lg = np.log(gam)
    diff = idx[None, :] - idx[:, None]
    dec = np.where(diff >= 0, np.exp(np.maximum(diff, 0)[None] * lg[:, None, None]), 0.0)
    c["decT"] = np.ascontiguousarray((dec * 256 ** -0.5).transpose(1, 0, 2)).astype(np.float32)
    xi = np.exp((idx[None, :] + 1.0) * lg[:, None])
    c["xib"] = np.ascontiguousarray(np.broadcast_to(xi[None], (128, RET_H, 128))).astype(np.float32)
    zeta = np.exp((127.0 - idx[None, :]) * lg[:, None]) * 256 ** -0.5
    c["zeta"] = np.ascontiguousarray(zeta.T).astype(np.float32)
    c["gchunk"] = np.exp(128 * lg)
    sub = (np.arange(128) // 32)
    same = sub[:, None] == sub[None, :]
    s = np.arange(128)
    c["triinc"] = (same & (s[:, None] <= s[None, :])).astype(np.float32)
    c["trirev"] = (same & (s[:, None] > s[None, :])).astype(np.float32)
    c["rowmask"] = (sub[:, None] == np.arange(4)[None, :]).astype(np.float32)
    c["colmask"] = np.ascontiguousarray(np.broadcast_to((sub[None, :] == np.arange(4)[:, None])[None], (128, 4, 128))).astype(np.float32)
    c["invf"] = (10000.0 ** (-np.arange(0, 256, 2, dtype=np.float32) / 256)).astype(np.float32)[:, None]
    c["neghalf"] = np.full((128, 8), -0.5, np.float32)
    return c


CONST_SHAPES = {"ident": [128, 128], "decT": [128, 4, 128], "xib": [128, 4, 128], "zeta": [128, 4],
                "triinc": [128, 128], "trirev": [128, 128], "rowmask": [128, 4], "colmask": [128, 4, 128],
                "invf": [128, 1], "neghalf": [128, 8]}


def bc_mid(a, n):
    return bass.AP(a.tensor, a.offset, [list(a.ap[0]), [0, n], list(a.ap[-1])])


def build(tiles, dbg=None):
    NCH = sum(t[0] for t in tiles)
    NTOK = NCH * 128
    NT = len(tiles)
    NOUT = sum(t[0] * 128 for t in tiles if t[1] == "f" and t[2] is not None)
    HC = host_consts()
    gchunk = [float(v) for v in HC["gchunk"]]
    nc = bass.Bass("TRN2", target_bir_lowering=False)
    din = lambda n, s, dt=F32: nc.dram_tensor(n, s, dt, kind="ExternalInput").ap()
    x_d = din("x", [NTOK, D])
    pos_d = din("positions", [1, NTOK], I32)
    vmask_d = din("vmask", [128, NCH])
    pmask_d = din("pmask", [128, NT])
    c_d = din("c", [1, D])
    w_ada = din("w_ada", [D, 6 * D])
    b_ada = din("b_ada", [1, 6 * D])
    g1_d = din("g_norm1", [1, D])
    w_in = din("w_in", [D, IN_COLS])
    w_ro = din("w_ret_o", [1024, D])
    w_ho = din("w_hg_o", [1024, D])
    w_out = din("w_out", [D, D])
    lb_d = din("hg_lb", [2, 1024])
    g2_d = din("g_norm2", [1, D])
    w_up = din("w_up", [D, 2 * DFF])
    cw_d = din("conv_w", [3, 2 * DFF])
    cb_d = din("conv_b", [1, 2 * DFF])
    w_dn = din("w_down", [DFF, D])
    gf_d = din("g_final", [1, D])
    cd = {k: din("k_" + k, s) for k, s in CONST_SHAPES.items()}
    out_d = nc.dram_tensor("out", [max(NOUT, 128), D], F32, kind="ExternalOutput").ap()
    mod_d = nc.dram_tensor("mod_scratch", [1, 6 * D], F32).ap()
    WSRC = {"w_in": w_in, "w_ret_o": w_ro, "w_hg_o": w_ho, "w_out": w_out, "w_up": w_up, "w_down": w_dn}
    WBF = {k: nc.dram_tensor("bf_" + k, list(v.shape), BF16).ap() for k, v in WSRC.items()}
    WNAME = {id(v): k for k, v in WSRC.items()}
    PCB = 1024
    dbg_d = {}
    if dbg:
        for k, (s, dt_) in dbg.items():
            dbg_d[k] = nc.dram_tensor("dbg_" + k, s, dt_, kind="ExternalOutput").ap()

    with contextlib.ExitStack() as st:
        S = Sched(nc, st)
        sbytes = [0]

        def sb(name, shape, dt=F32):
            n = int(np.prod(shape[1:])) * (4 if dt in (F32, I32) else 2)
            sbytes[0] += n
            return st.enter_context(nc.sbuf_tensor(name, shape, dt))
        K = {k: sb("c_" + k, s) for k, s in CONST_SHAPES.items()}
        WS = [sb("ws%d" % i, [128, KC, 512], BF16) for i in range(2)]
        AB = sb("AB", [128, 4, KC])
        LBb = sb("LBb", [128, 1024])
        OMLb = sb("OMLb", [128, 1024])
        omlf = sb("omlf", [128, 8])
        cwf = sb("cwf", [128, 2 * NFB, 4])
        halo = sb("halo", [128, 2 * NFB, 2])
        vmask = sb("vmask_s", [128, NCH])
        pmask = sb("pmask_s", [128, NT])
        Rst = sb("Rst", [128, RET_H, 512])
        Rbf = sb("Rbf", [128, RET_H, 512], BF16)
        Sst = sb("Sst", [128, HG_H, 128])
        Sbf = sb("Sbf", [128, 4, 4, 128], BF16)
        small = sb("small", [128, 96])

        def spec(cp, full):
            T_ = cp * 128
            L = [("xt", [128, cp, D], F32), ("hT", [128, KC, T_], BF16), ("cosT", [128, T_], F32), ("sinT", [128, T_], F32),
                 ("posi", [128, min(T_, 256)], I32), ("sqj", [128, 1024], F32), ("dgs", [128, cp, 128], F32),
                 ("t1", [128, T_], F32), ("t2", [128, T_], F32), ("kf32", [128, 2, T_], F32),
                 ("logf", [128, 512], F32), ("ktm", [128, 512], F32), ("sgm", [128, 512], F32), ("ecb", [128, 512], F32),
                 ("eb", [128, cp, 512], F32), ("kbf", [128, 2, T_], BF16), ("kzt", [128, cp, 256], BF16), ("vret", [128, cp, 256], BF16),
                 ("khat", [128, cp, 512], BF16), ("vhg", [128, cp, 512], BF16), ("kz", [128, 512], BF16)]
            if full:
                L += [("ogT", [128, KC, T_], BF16), ("mgT", [128, KC, T_], BF16), ("gT", [128, NFB, T_], BF16), ("Gb", [128, D], F32),
                      ("sgr", [128, cp, 256], F32), ("osb", [128, 512], F32), ("osq", [128, 512], F32), ("enb", [128, cp, 512], F32),
                      ("sgate", [128, cp, 512], F32), ("ybuf", [128, 2, T_ + 2], F32), ("ua", [128, T_], F32), ("ub", [128, T_], F32),
                      ("qrot", [128, 2, T_], BF16), ("qhat", [128, 2, T_], BF16), ("inn", [128, 128], BF16), ("qt", [128, 4, T_], BF16),
                      ("kt", [128, 4, T_], BF16), ("amt", [128, 4, 128], BF16), ("qz", [128, 4, 4, 128], BF16)]
            return L

        def nf32(shape, dt_):
            n = int(np.prod(shape[1:]))
            return (n + 1) // 2 if dt_ == BF16 else n

        CPS_ = max([t[0] for t in tiles if t[1] == "s"] + [1])
        CPF_ = max([t[0] for t in tiles if t[1] == "f"] + [1])
        ls_tot = sum(nf32(sh, d_) for _, sh, d_ in spec(CPS_, False))
        lf_tot = sum(nf32(sh, d_) for _, sh, d_ in spec(CPF_, True))
        need = max(ls_tot + 6656, lf_tot)
        AR = sb("arena", [128, need])

        def layout(cp, full):
            L = {}
            off = 0
            for name, shape, dt_ in spec(cp, full):
                n = int(np.prod(shape[1:]))
                w = nf32(shape, dt_)
                v = AR[:, off:off + w]
                if dt_ == BF16:
                    v = v.bitcast(BF16)[:, 0:n]
                elif dt_ == I32:
                    v = v.bitcast(I32)
                if len(shape) == 3:
                    v = v.rearrange("p (a b) -> p a b", a=shape[1])
                elif len(shape) == 4:
                    v = v.rearrange("p (a b c) -> p a b c", a=shape[1], b=shape[2])
                L[name] = v
                L["_off_" + name] = off
                off += w
            return L
        LF = layout(CPF_, True)
        LS = layout(CPS_, False)
        xt, Gb, sqj, gT = LF["xt"], LF["Gb"], LF["sqj"], LF["gT"]
        tgA = AR[:, ls_tot:ls_tot + 2048]
        tgB = AR[:, ls_tot + 2048:ls_tot + 4096]
        mrow = AR[0:1, ls_tot + 4096:ls_tot + 6144]
        brow2 = AR[0:1, ls_tot + 6144:ls_tot + 6656]
        kz2 = AR[:, ls_tot + 6656:ls_tot + 6912].bitcast(BF16)
        assert need >= ls_tot + 6912
        sqjb = sqj.bitcast(BF16)
        rowb = Gb
        print("SBUF bytes/partition:", sbytes[0])
        PS = [st.enter_context(nc.psum_tensor("ps%d" % i, [128, 512], F32)) for i in range(8)]
        psn = [0]

        def bank():
            i = psn[0] % 8
            psn[0] += 1
            return i

        def load(dst_ap, src_ap, key, eng="sp", reads=()):
            return S.dma(eng, "ld_" + key, lambda e: e.dma_start(out=dst_ap, in_=src_ap, allow_slow_non_contiguous=True), writes=[key], reads=reads)

        wsn = [0]

        def wload(segs, krows=D):
            i = wsn[0] % 2
            wsn[0] += 1
            ws = WS[i]
            off = 0
            kc = krows // 128
            for si, (w, row0, c0, n) in enumerate(segs):
                name = WNAME[id(w)]
                src = WBF[name][row0:row0 + krows, c0:c0 + n].rearrange("(kc p) n -> p kc n", p=128)
                dst = ws[:, 0:kc, off:off + n]
                rk = sorted(set(["wb_%s_%d" % (name, c // PCB) for c in (c0, c0 + n - 1)]))
                S.dma("sp", "ws%d_%d" % (i, si), (lambda e, dst=dst, src=src: e.dma_start(out=dst, in_=src)), writes=["ws%d_s%d" % (i, si)], reads=rk)
                off += n
            return i

        def wk(i):
            return ["ws%d_s0" % i, "ws%d_s1" % i]

        def mm_group(out_ap, pairs, reads, wkey):
            def f(e):
                r = None
                n = len(pairs)
                for i, (l, rh) in enumerate(pairs):
                    r = e.matmul(out_ap, lhsT=l, rhs=rh, start=(i == 0), stop=(i == n - 1))
                return r
            return S.op("pe", f, reads=reads, writes=[wkey])

        def mm_multi(items, reads, wkey):
            def f(e):
                r = None
                for (o, l, rh) in items:
                    r = e.matmul(o, lhsT=l, rhs=rh, start=True, stop=True)
                return r
            return S.op("pe", f, reads=reads, writes=[wkey])

        def tr_multi(items, reads, wkey):
            def f(e):
                r = None
                for (o, i_) in items:
                    r = e.transpose(o, i_, K["ident"][:])
                return r
            return S.op("pe", f, reads=list(reads) + ["c_ident"], writes=[wkey])

        def act(out, in_, func, reads, writes, **kw):
            return S.op("act", lambda e: e.activation(out, in_, func, **kw), reads=reads, writes=writes)

        def tt(out, a, b, op, reads, writes, eng="dve"):
            return S.op(eng, lambda e: e.tensor_tensor(out, a, b, op), reads=reads, writes=writes)

        def ts(out, a, s1, s2, op0, op1, reads, writes, eng="dve"):
            if op1 is None:
                return S.op(eng, lambda e: e.tensor_scalar(out, a, s1, s2, op0), reads=reads, writes=writes)
            return S.op(eng, lambda e: e.tensor_scalar(out, a, s1, s2, op0, op1), reads=reads, writes=writes)

        def stt(out, a, s, b, op0, op1, reads, writes, eng="dve"):
            return S.op(eng, lambda e: e.scalar_tensor_tensor(out, a, s, b, op0, op1), reads=reads, writes=writes)

        def rsqrt_small(dst, src, scale, eps, reads, wkey):
            n = src.shape[1]
            ts(dst, src, scale, eps, ALU.mult, ALU.add, reads, [wkey])
            S.op("pool", lambda e: e.tensor_tensor(dst, dst, K["neghalf"][:, 0:n], ALU.pow), reads=[wkey, "c_neghalf"], writes=[wkey])

        def store_dbg(name, ap, key):
            if name in dbg_d:
                S.dma("sp", "dbg_" + name, lambda e: e.dma_start(out=dbg_d[name], in_=ap), reads=[key])

        for k in CONST_SHAPES:
            load(K[k][:], cd[k], "c_" + k)
        def precast(name, cb):
            src = WSRC[name]
            ncols = src.shape[1]
            c0 = cb * PCB
            n = min(PCB, ncols - c0)
            key = "wb_%s_%d" % (name, cb)
            S.dma("pool", "pc_" + key, (lambda e, src=src, name=name, c0=c0, n=n: e.dma_start(out=WBF[name][:, c0:c0 + n], in_=src[:, c0:c0 + n])), writes=[key])

        precast_todo = [("w_in", cb) for cb in (0, 3, 4, 7, 8, 9, 10, 11)]
        for name in ("w_ret_o", "w_hg_o", "w_out"):
            precast_todo += [(name, cb) for cb in range(2)]
        precast_todo += [("w_up", cb) for cb in range(11)] + [("w_down", cb) for cb in range(2)]
        for cb in (1, 2, 5, 6):
            precast("w_in", cb)
        if not any(t[1] == "s" for t in tiles):
            while precast_todo:
                precast(*precast_todo.pop(0))
        load(vmask[:], vmask_d, "vmask")
        load(pmask[:], pmask_d, "pmask")
        S.op("pool", lambda e: e.memset(halo[:], 0.0), writes=["halo"])
        S.op("pool", lambda e: e.memset(Rst[:], 0.0), writes=["Rst"])
        S.op("pool", lambda e: e.memset(Rbf[:], 0.0), writes=["Rbf"])
        S.op("pool", lambda e: e.memset(Sst[:], 0.0), writes=["Sst"])
        S.op("pool", lambda e: e.memset(Sbf[:], 0.0), writes=["Sbf0", "Sbf1", "Sbf2", "Sbf3"])
        cf = small[:, 0:16]
        g1f = small[:, 16:32]
        g2f = small[:, 32:48]
        scf = small[:, 48:64]
        lbf = small[:, 64:80]
        with nc.allow_non_contiguous_dma(reason="tiny one-time strided loads"):
            for j in range(3):
                load(cwf[:, :, j:j + 1], cw_d[j:j + 1, :].rearrange("o (b p) -> p b o", p=128), "cwf")
            load(cwf[:, :, 3:4], cb_d.rearrange("o (b p) -> p b o", p=128), "cwf")
            load(cf, c_d.rearrange("o (k p) -> p (o k)", p=128), "small_c")
            load(g1f, g1_d.rearrange("o (k p) -> p (o k)", p=128), "small_g1")
            load(g2f, g2_d.rearrange("o (k p) -> p (o k)", p=128), "small_g2")
            load(lbf.rearrange("p (l h) -> p l h", l=2), lb_d.rearrange("l (h p) -> p l h", p=128), "small_lb")
        act(scf, cf, AF.Silu, ["small_c"], ["small_sc"])
        sn_ = [0]

        def gemv_block(g, nb):
            col = g * 2048 + nb * 512
            load(brow2, b_ada[0:1, col:col + 512], "brow2")
            pb = bank()
            for kq in range(4):
                buf = (tgA, tgB)[sn_[0] % 2]
                key = "tg%d" % (sn_[0] % 2)
                sn_[0] += 1
                src = w_ada[kq * 512:(kq + 1) * 512, col:col + 512].rearrange("(k p) n -> p k n", p=128)
                load(buf.rearrange("p (k n) -> p k n", k=4), src, key)

                def f(e, kq=kq, pb=pb, buf=buf):
                    r = None
                    for k4 in range(4):
                        kc = kq * 4 + k4
                        r = e.matmul(PS[pb][0:1, :], lhsT=scf[:, kc:kc + 1], rhs=buf[:, k4 * 512:(k4 + 1) * 512],
                                     start=(kc == 0), stop=(kc == 15))
                    return r
                S.op("pe", f, reads=[key, "small_sc"], writes=["ps%d" % pb])
            tt(mrow[:, nb * 512:(nb + 1) * 512], PS[pb][0:1, :], brow2, ALU.add, ["ps%d" % pb, "brow2"], ["mrow"])
            if nb == 3:
                S.dma("sp", "st_mod", lambda e, g=g: e.dma_start(out=mod_d[0:1, g * 2048:(g + 1) * 2048], in_=mrow), reads=["mrow"], writes=["mod_d%d" % g])

        for g in range(2):
            for nb in range(4):
                gemv_block(g, nb)
        gemv_todo = [(g, nb) for g in range(2, 6) for nb in range(4)]
        s1f = small[:, 80:96]
        fmv = lambda g: mod_d[0:1, g * 2048:(g + 1) * 2048].rearrange("o (k p) -> p (o k)", p=128)
        load(AB[:, 1, :], fmv(0), "AB1", reads=["mod_d0"])
        load(s1f, fmv(1), "small_s1", reads=["mod_d1"])
        stt(AB[:, 0, :], s1f, 1.0, g1f, ALU.add, ALU.mult, ["small_s1", "small_g1"], ["AB0"])

        def late_mod():
            while gemv_todo:
                gemv_block(*gemv_todo.pop(0))
            load(AB[:, 3, :], fmv(3), "AB3", reads=["mod_d3"])
            load(s1f, fmv(4), "small_s1", reads=["mod_d4"])
            stt(AB[:, 2, :], s1f, 1.0, g2f, ALU.add, ALU.mult, ["small_s1", "small_g2"], ["AB2"])
        load(tgA[:, 0:1024], lb_d[0:1, :].partition_broadcast(128), "tg0")
        load(tgA[:, 1024:2048], lb_d[1:2, :].partition_broadcast(128), "tg0")
        act(tgA, tgA, AF.Exp, ["tg0"], ["tg0"])
        tt(tgB[:, 0:1024], tgA[:, 0:1024], tgA[:, 1024:2048], ALU.add, ["tg0"], ["tg1"])
        S.op("dve", lambda e: e.reciprocal(tgB[:, 0:1024], tgB[:, 0:1024]), reads=["tg1"], writes=["tg1"])
        tt(LBb[:], tgA[:, 0:1024], tgB[:, 0:1024], ALU.mult, ["tg0", "tg1"], ["LBb"])
        ts(OMLb[:], LBb[:], -1.0, 1.0, ALU.mult, ALU.add, ["LBb"], ["OMLb"])
        act(lbf, lbf, AF.Exp, ["small_lb"], ["small_lb"])
        tt(omlf[:], lbf[:, 0:8], lbf[:, 8:16], ALU.add, ["small_lb"], ["omlf"])
        S.op("dve", lambda e: e.reciprocal(omlf[:], omlf[:]), reads=["omlf"], writes=["omlf"])
        tt(omlf[:], omlf[:], lbf[:, 8:16], ALU.mult, ["omlf", "small_lb"], ["omlf"])

        def make_ops(L):
            xt = L.get('xt')
            hT = L.get('hT')
            cosT = L.get('cosT')
            sinT = L.get('sinT')
            posi = L.get('posi')
            sqj = L.get('sqj')
            dgs = L.get('dgs')
            t1 = L.get('t1')
            t2 = L.get('t2')
            kf32 = L.get('kf32')
            logf = L.get('logf')
            ktm = L.get('ktm')
            sgm = L.get('sgm')
            ecb = L.get('ecb')
            eb = L.get('eb')
            kbf = L.get('kbf')
            kzt = L.get('kzt')
            vret = L.get('vret')
            khat = L.get('khat')
            vhg = L.get('vhg')
            kz = L.get('kz')
            ogT = L.get('ogT')
            mgT = L.get('mgT')
            gT = L.get('gT')
            Gb = L.get('Gb')
            sgr = L.get('sgr')
            osb = L.get('osb')
            osq = L.get('osq')
            enb = L.get('enb')
            sgate = L.get('sgate')
            ybuf = L.get('ybuf')
            ua = L.get('ua')
            ub = L.get('ub')
            qrot = L.get('qrot')
            qhat = L.get('qhat')
            inn = L.get('inn')
            qt = L.get('qt')
            kt = L.get('kt')
            amt = L.get('amt')
            qz = L.get('qz')
            sqjb = sqj.bitcast(BF16)
            sgA = sgB = None
            if enb is not None:
                tq = eb.shape[1] * 512 // 4
                sgA = eb.rearrange("p c n -> p (c n)").rearrange("p (m t) -> p m t", m=4)
                sgB = enb.rearrange("p c n -> p (c n)").rearrange("p (m t) -> p m t", m=4)
            def norm_to_hT(ai, bi, cpt):
                ss = small[:, 0:cpt]
                rs = small[:, 8:8 + cpt]
                for c in range(cpt):
                    act(sqjb[:, 0:2048], xt[:, c, :], AF.Square, ["xt"], ["sqj", "ss"], accum_out=ss[:, c:c + 1])
                rsqrt_small(rs, ss, 1.0 / D, EPS, ["ss"], "rs")
                for c in range(cpt):
                    ts(dgs[:, c, :], K["ident"][:], rs[:, c:c + 1], None, ALU.mult, None, ["c_ident", "rs"], ["dgs%d" % c])
                    for q in range(4):
                        pb = bank()
                        mm_multi([(PS[pb][:, k4 * 128:(k4 + 1) * 128], xt[:, c, (q * 4 + k4) * 128:(q * 4 + k4 + 1) * 128], dgs[:, c, :]) for k4 in range(4)],
                                 ["xt", "dgs%d" % c], "ps%d" % pb)
                        for k4 in range(4):
                            kc = q * 4 + k4
                            act(hT[:, kc, c * 128:(c + 1) * 128], PS[pb][:, k4 * 128:(k4 + 1) * 128], AF.Identity,
                                ["ps%d" % pb, "AB%d" % ai, "AB%d" % bi], ["hT"], scale=AB[:, ai, kc:kc + 1], bias=AB[:, bi, kc:kc + 1])

            def gemm_fm(wi, col0, nmb, actT, akey, kcs, tw, cb):
                for mb in range(nmb):
                    pb = bank()
                    pairs = [(WS[wi][:, kci, col0 + mb * 128: col0 + (mb + 1) * 128], actT[:, kc, 0:tw]) for kci, kc in enumerate(kcs)]
                    mm_group(PS[pb][:, 0:tw], pairs, wk(wi) + [akey], "ps%d" % pb)
                    cb(mb, pb)

            def gemm_tm(wi, col0, ncols, actT, akey, kcs, cpt, cb):
                for c in range(cpt):
                    pb = bank()
                    pairs = [(actT[:, kc, c * 128:(c + 1) * 128], WS[wi][:, kci, col0:col0 + ncols]) for kci, kc in enumerate(kcs)]
                    mm_group(PS[pb][:, 0:ncols], pairs, wk(wi) + [akey], "ps%d" % pb)
                    cb(c, pb)

            def rope_tables(tok0, tw_all):
                for off in range(0, tw_all, 256):
                    rope_piece(tok0 + off, off, min(256, tw_all - off))

            def rope_piece(tok0, off, tw):
                load(posi[:, 0:tw], pos_d[0:1, tok0:tok0 + tw].partition_broadcast(128), "posi")
                ang = sqj[:, 0:tw]
                kf = sqj[:, 256:256 + tw]
                ki = sqj[:, 512:512 + tw].bitcast(I32)
                a2 = sqj[:, 768:768 + tw]
                kk = ["sqj"]
                S.op("dve", lambda e: e.tensor_copy(ang, posi[:, 0:tw]), reads=["posi"], writes=kk)
                ts(ang, ang, K["invf"][:, 0:1], None, ALU.mult, None, kk + ["c_invf"], kk)
                for shift, dst, key in ((0.0, sinT, "sinT"), (float(np.pi / 2), cosT, "cosT")):
                    ts(a2, ang, shift, None, ALU.add, None, kk, kk)
                    ts(kf, a2, float(1.0 / (2 * np.pi)), None, ALU.mult, None, kk, kk)
                    S.op("dve", lambda e: e.tensor_copy(ki, kf), reads=kk, writes=kk)
                    S.op("dve", lambda e: e.tensor_copy(kf, ki), reads=kk, writes=kk)
                    stt(a2, kf, -6.28125, a2, ALU.mult, ALU.add, kk, kk)
                    stt(a2, kf, -float(2 * np.pi - 6.28125), a2, ALU.mult, ALU.add, kk, kk)
                    ts(kf, a2, float(np.pi), None, ALU.is_gt, None, kk, kk)
                    stt(a2, kf, -float(2 * np.pi), a2, ALU.mult, ALU.add, kk, kk)
                    ts(kf, a2, -float(np.pi), None, ALU.is_lt, None, kk, kk)
                    stt(a2, kf, float(2 * np.pi), a2, ALU.mult, ALU.add, kk, kk)
                    ts(a2, a2, 3.1415925, -3.1415925, ALU.min, ALU.max, kk, kk)
                    act(dst[:, off:off + tw], a2, AF.Sin, kk, [key])

            def ret_head(h, ch0, cpt, so):
                tw = cpt * 128
                if not so:
                    act(Rbf[:, h, :], Rst[:, h, :], AF.Copy, ["Rst"], ["Rbf"])
                segs = [(w_in, 0, 1024 + h * 256, 256)]
                if not so:
                    segs = [(w_in, 0, h * 256, 256)] + segs
                wi = wload(segs)
                kcol = 0 if so else 256
                held = {}

                def rope_cb(is_k):
                    def cb(mb, pb):
                        held[mb] = pb
                        if mb != 1:
                            return
                        p1, p2 = PS[held[0]][:, 0:tw], PS[held[1]][:, 0:tw]
                        k12 = ["ps%d" % held[0], "ps%d" % held[1]]
                        for half, (pa, pbb, op) in enumerate(((p1, p2, ALU.subtract), (p2, p1, ALU.add))):
                            tt(t1[:, 0:tw], pa, cosT[:, 0:tw], ALU.mult, k12 + ["cosT"], ["t1"])
                            tt(t2[:, 0:tw], pbb, sinT[:, 0:tw], ALU.mult, k12 + ["sinT"], ["t2"])
                            if is_k:
                                tt(kf32[:, half, 0:tw], t1[:, 0:tw], t2[:, 0:tw], op, ["t1", "t2"], ["kf32"])
                                if not so:
                                    act(kbf[:, half, 0:tw], kf32[:, half, 0:tw], AF.Copy, ["kf32"], ["kbf"])
                            else:
                                tt(t1[:, 0:tw], t1[:, 0:tw], t2[:, 0:tw], op, ["t1", "t2"], ["t1"])
                                act(qrot[:, half, 0:tw], t1[:, 0:tw], AF.Copy, ["t1"], ["qrot"])
                                tt(qhat[:, half, 0:tw].rearrange("p (c t) -> p c t", c=cpt), t1[:, 0:tw].rearrange("p (c t) -> p c t", c=cpt),
                                   bc_mid(K["xib"][:, h, :], cpt), ALU.mult, ["t1", "c_xib"], ["qhat"])
                    return cb
                gemm_fm(wi, kcol, 2, hT, "hT", range(KC), tw, rope_cb(True))
                if not so:
                    gemm_fm(wi, 0, 2, hT, "hT", range(KC), tw, rope_cb(False))
                wi2 = wload([(w_in, 0, 2048 + h * 256, 256)] + ([] if so else [(w_in, 0, 3072 + h * 256, 256)]))

                def vg_cb(c, pb):
                    gi = ch0 + c
                    act(vret[:, c, :], PS[pb][:, 0:256], AF.Identity, ["ps%d" % pb, "vmask"], ["vret"], scale=vmask[:, gi:gi + 1])
                    if not so:
                        act(sgr[:, c, :], PS[pb][:, 256:512], AF.Silu, ["ps%d" % pb], ["sgr"])
                gemm_tm(wi2, 0, 256 if so else 512, hT, "hT", range(KC), cpt, vg_cb)
                for c in range(cpt):
                    pb = bank()
                    tr_multi([(PS[pb][:, k2 * 128:(k2 + 1) * 128], kf32[:, k2, c * 128:(c + 1) * 128]) for k2 in range(2)], ["kf32"], "ps%d" % pb)
                    act(kzt[:, c, :], PS[pb][:, 0:256], AF.Identity, ["ps%d" % pb, "c_zeta"], ["kzt"], scale=K["zeta"][:, h:h + 1])
                for c in range(cpt):
                    cs = slice(c * 128, (c + 1) * 128)
                    if not so:
                        pbs = bank()
                        mm_group(PS[pbs][:, 0:128], [(kbf[:, k2, cs], qrot[:, k2, cs]) for k2 in range(2)], ["kbf", "qrot"], "ps%d" % pbs)
                        tt(inn[:], PS[pbs][:, 0:128], K["decT"][:, h, :], ALU.mult, ["ps%d" % pbs, "c_decT"], ["inn"])
                        pbo = bank()
                        pairs = [(inn[:], vret[:, c, :])] + [(qhat[:, k2, cs], Rbf[:, h, k2 * 256:(k2 + 1) * 256]) for k2 in range(2)]
                        mm_group(PS[pbo][:, 0:256], pairs, ["inn", "vret", "qhat", "Rbf"], "ps%d" % pbo)
                        st_ = small[:, 16:24]
                        o_ = osb[:, 0:256]
                        act(o_, PS[pbo][:, 0:256], AF.Copy, ["ps%d" % pbo], ["osb", "st0"], accum_out=st_[:, 0:1])
                        act(osq[:, 0:256], o_, AF.Square, ["osb"], ["osq", "st1"], accum_out=st_[:, 1:2])
                        ts(st_[:, 2:3], st_[:, 0:1], 1.0 / 256, None, ALU.mult, None, ["st0"], ["st2"])
                        tt(st_[:, 3:4], st_[:, 2:3], st_[:, 2:3], ALU.mult, ["st2"], ["st3"])
                        stt(st_[:, 4:5], st_[:, 1:2], 1.0 / 256, st_[:, 3:4], ALU.mult, ALU.subtract, ["st1", "st3"], ["st4"])
                        rsqrt_small(st_[:, 5:6], st_[:, 4:5], 1.0, HEPS, ["st4"], "st5")
                        ts(o_, o_, st_[:, 2:3], st_[:, 5:6], ALU.subtract, ALU.mult, ["osb", "st2", "st5"], ["osb"])
                        tt(o_, o_, sgr[:, c, :], ALU.mult, ["osb", "sgr"], ["osb"])
                        pbt = bank()
                        tr_multi([(PS[pbt][:, k2 * 128:(k2 + 1) * 128], osb[:, k2 * 128:(k2 + 1) * 128]) for k2 in range(2)], ["osb"], "ps%d" % pbt)
                        for k2 in range(2):
                            act(ogT[:, h * 2 + k2, cs], PS[pbt][:, k2 * 128:(k2 + 1) * 128], AF.Copy, ["ps%d" % pbt], ["ogT"])
                    pbr = bank()
                    mm_multi([(PS[pbr][:, k2 * 256:(k2 + 1) * 256], kzt[:, c, k2 * 128:(k2 + 1) * 128], vret[:, c, :]) for k2 in range(2)],
                             ["kzt", "vret"], "ps%d" % pbr)
                    stt(Rst[:, h, :], Rst[:, h, :], gchunk[h], PS[pbr][:, 0:512], ALU.mult, ALU.add, ["Rst", "ps%d" % pbr], ["Rst"])
                    if not so:
                        act(Rbf[:, h, :], Rst[:, h, :], AF.Copy, ["Rst"], ["Rbf"])

            def hg_group(g, ch0, cpt, so):
                tw = cpt * 128
                c0 = 4096 + g * 512
                wif = wload([(w_in, 0, c0 + 1024, 512)])

                o1 = L.get("_off_t1")
                alt = [AR[:, o1 + i * 512: o1 + (i + 1) * 512] for i in range(4)]
                BS = [dict(sgm=sgm, ktm=ktm, logf=logf, ecb=ecb, k="", g=[]),
                      dict(sgm=alt[0], ktm=alt[1], logf=alt[2], ecb=alt[3], k="_b", g=["t1", "t2", "kf32"])]

                def hf_p1(c, pb, B, first):
                    kx, gd = B["k"], B["g"]
                    act(B["sgm"], PS[pb][:, 0:512], AF.Sigmoid, ["ps%d" % pb], ["sgm" + kx] + (gd if first else []))
                    tt(B["sgm"], B["sgm"], OMLb[:, g * 512:(g + 1) * 512], ALU.mult, ["sgm" + kx, "OMLb"] + gd, ["sgm" + kx])
                    tt(B["sgm"], B["sgm"], LBb[:, g * 512:(g + 1) * 512], ALU.add, ["sgm" + kx, "LBb"] + gd, ["sgm" + kx])
                    ts(B["ktm"], B["sgm"], -1.0, 1.0, ALU.mult, ALU.add, ["sgm" + kx] + gd, ["ktm" + kx])
                    act(B["logf"], B["sgm"], AF.Ln, ["sgm" + kx] + gd, ["logf" + kx])

                def hf_p2(c, B):
                    kx, gd = B["k"], B["g"]
                    pbc = bank()
                    mm_multi([(PS[pbc][:, 0:512], K["trirev"][:], B["logf"])], ["c_trirev", "logf" + kx] + gd, "ps%d" % pbc)
                    act(B["ecb"], PS[pbc][:, 0:512], AF.Exp, ["ps%d" % pbc] + gd, ["ecb" + kx])
                    tt(khat[:, c, :], B["ktm"], B["ecb"], ALU.mult, ["ktm" + kx, "ecb" + kx] + gd, ["khat"])
                    pbb = bank()
                    mm_multi([(PS[pbb][:, hh * 128:(hh + 1) * 128], B["logf"][:, hh * 128:(hh + 1) * 128], K["triinc"][:]) for hh in range(4)],
                             ["logf" + kx, "c_triinc"] + gd, "ps%d" % pbb)
                    act(eb[:, c, :], PS[pbb][:, 0:512], AF.Exp, ["ps%d" % pbb], ["eb"])
                    if not so:
                        act(enb[:, c, :], PS[pbb][:, 0:512], AF.Exp, ["ps%d" % pbb], ["enb"], scale=-1.0)

                use_alt = so and cpt >= 2 and o1 is not None
                seen_alt = [False]

                def hf_cb(c, pb):
                    B = BS[c % 2] if use_alt else BS[0]
                    first = False
                    if B is BS[1] and not seen_alt[0]:
                        first = True
                        seen_alt[0] = True
                    hf_p1(c, pb, B, first)
                    if use_alt:
                        if c >= 1:
                            hf_p2(c - 1, BS[(c - 1) % 2])
                        if c == cpt - 1:
                            hf_p2(c, B)
                    else:
                        hf_p2(c, B)
                gemm_tm(wif, 0, 512, hT, "hT", range(KC), cpt, hf_cb)
                if not so:
                    def hfT_cb(mb, pb):
                        act(t1[:, 0:tw], PS[pb][:, 0:tw], AF.Sigmoid, ["ps%d" % pb], ["t1"], scale=-1.0)
                        stt(kt[:, mb, 0:tw].rearrange("p (c t) -> p c t", c=cpt), t1[:, 0:tw].rearrange("p (c t) -> p c t", c=cpt),
                            omlf[:, g * 4 + mb: g * 4 + mb + 1], enb[:, 0:cpt, mb * 128:(mb + 1) * 128], ALU.mult, ALU.mult, ["t1", "omlf", "enb"], ["kt"])
                    gemm_fm(wif, 0, 4, hT, "hT", range(KC), tw, hfT_cb)
                    wiq = wload([(w_in, 0, c0, 512)])

                    def hqT_cb(mb, pb):
                        act(t2[:, 0:tw], PS[pb][:, 0:tw], AF.Silu, ["ps%d" % pb], ["t2"])
                        tt(qt[:, mb, 0:tw].rearrange("p (c t) -> p c t", c=cpt), t2[:, 0:tw].rearrange("p (c t) -> p c t", c=cpt),
                           eb[:, 0:cpt, mb * 128:(mb + 1) * 128], ALU.mult, ["t2", "eb"], ["qt"])
                    gemm_fm(wiq, 0, 4, hT, "hT", range(KC), tw, hqT_cb)
                wiv = wload([(w_in, 0, c0 + 2048, 512)])

                def v_cb(c, pb):
                    gi = ch0 + c
                    act(vhg[:, c, :], PS[pb][:, 0:512], AF.Identity, ["ps%d" % pb, "vmask"], ["vhg"], scale=vmask[:, gi:gi + 1])
                gemm_tm(wiv, 0, 512, hT, "hT", range(KC), cpt, v_cb)
                if not so:
                    wig = wload([(w_in, 0, c0 + 3072, 512)])

                    def g_cb(c, pb):
                        act(sgate[:, c, :], PS[pb][:, 0:512], AF.Silu, ["ps%d" % pb], ["sgate"])
                    gemm_tm(wig, 0, 512, hT, "hT", range(KC), cpt, g_cb)
                if not so:
                    act(Sbf[:, 0, :, :], Sst[:, g * 4:g * 4 + 4, :], AF.Copy, ["Sst"], ["Sbf0"])
                for c in range(cpt):
                    cs = slice(c * 128, (c + 1) * 128)
                    if not so:
                        pba = bank()
                        mm_multi([(PS[pba][:, hh * 128:(hh + 1) * 128], kt[:, hh, cs], qt[:, hh, cs]) for hh in range(4)], ["kt", "qt"], "ps%d" % pba)
                        tt(amt[:], PS[pba][:, 0:512].rearrange("p (h t) -> p h t", h=4), bc_mid(K["triinc"][:], 4), ALU.mult,
                           ["ps%d" % pba, "c_triinc"], ["amt"])
                        for j in range(4):
                            tt(qz[:, j, :, :], qt[:, :, cs], bc_mid(K["colmask"][:, j, :], 4), ALU.mult, ["qt", "c_colmask"], ["qz"], eng="pool")

                    def sub_update(j):
                        kzb, kzk = kz, "kz"
                        if so and (j % 2 == 1):
                            kzb, kzk = kz2, "kz2"
                        act(kzb, khat[:, c, :], AF.Identity, ["khat", "c_rowmask"], [kzk], scale=K["rowmask"][:, j:j + 1])
                        pbd = bank()
                        mm_multi([(PS[pbd][:, hh * 128:(hh + 1) * 128], kzb[:, hh * 128:(hh + 1) * 128], vhg[:, c, hh * 128:(hh + 1) * 128]) for hh in range(4)],
                                 [kzk, "vhg"], "ps%d" % pbd)
                        for hh in range(4):
                            hd = g * 4 + hh
                            col = hh * 128 + j * 32 + 31
                            stt(Sst[:, hd, :], Sst[:, hd, :], eb[:, c, col:col + 1], PS[pbd][:, hh * 128:(hh + 1) * 128], ALU.mult, ALU.add,
                                ["Sst", "eb", "ps%d" % pbd], ["Sst"])
                        jn = (j + 1) % 4
                        if not so:
                            act(Sbf[:, jn, :, :], Sst[:, g * 4:g * 4 + 4, :], AF.Copy, ["Sst"], ["Sbf%d" % jn])
                    for j in range(3):
                        sub_update(j)
                    if not so:
                        pbo = bank()

                        def f(e, pbo=pbo, c=c):
                            r = None
                            for hh in range(4):
                                o = PS[pbo][:, hh * 128:(hh + 1) * 128]
                                e.matmul(o, lhsT=amt[:, hh, :], rhs=vhg[:, c, hh * 128:(hh + 1) * 128], start=True, stop=False)
                                for j in range(4):
                                    r = e.matmul(o, lhsT=qz[:, j, hh, :], rhs=Sbf[:, j, hh, :], start=False, stop=(j == 3))
                            return r
                        S.op("pe", f, reads=["amt", "vhg", "qz", "Sbf0", "Sbf1", "Sbf2", "Sbf3"], writes=["ps%d" % pbo])
                        ss_ = small[:, 24:28]
                        rs_ = small[:, 28:32]
                        for hh in range(4):
                            act(osq[:, hh * 128:(hh + 1) * 128], PS[pbo][:, hh * 128:(hh + 1) * 128], AF.Square, ["ps%d" % pbo], ["osq", "hss"], accum_out=ss_[:, hh:hh + 1])
                        rsqrt_small(rs_, ss_, 1.0 / 128, HEPS, ["hss"], "hrs")
                        for hh in range(4):
                            stt(osb[:, hh * 128:(hh + 1) * 128], PS[pbo][:, hh * 128:(hh + 1) * 128], rs_[:, hh:hh + 1], sgate[:, c, hh * 128:(hh + 1) * 128],
                                ALU.mult, ALU.mult, ["ps%d" % pbo, "hrs", "sgate"], ["osb"])
                        pbt = bank()
                        tr_multi([(PS[pbt][:, hh * 128:(hh + 1) * 128], osb[:, hh * 128:(hh + 1) * 128]) for hh in range(4)], ["osb"], "ps%d" % pbt)
                        for hh in range(4):
                            act(ogT[:, 8 + g * 4 + hh, cs], PS[pbt][:, hh * 128:(hh + 1) * 128], AF.Copy, ["ps%d" % pbt], ["ogT"])
                    sub_update(3)


            def merge_and_out2(cpt):
                tw = cpt * 128
                for jb in range(4):
                    wa = wload([(w_in, 0, 8192 + jb * 512, 512)])
                    wb = wload([(w_in, 0, 10240 + jb * 512, 512)])
                    sig = []
                    for mb in range(4):
                        kcb = jb * 4 + mb
                        pga, pgb = bank(), bank()
                        mm_group(PS[pga][:, 0:tw], [(WS[wa][:, kc, mb * 128:(mb + 1) * 128], hT[:, kc, 0:tw]) for kc in range(KC)], wk(wa) + ["hT"], "ps%d" % pga)
                        mm_group(PS[pgb][:, 0:tw], [(WS[wb][:, kc, mb * 128:(mb + 1) * 128], hT[:, kc, 0:tw]) for kc in range(KC)], wk(wb) + ["hT"], "ps%d" % pgb)
                        act(sgA[:, mb, 0:tw], PS[pga][:, 0:tw], AF.Sigmoid, ["ps%d" % pga], ["eb"])
                        act(sgB[:, mb, 0:tw], PS[pgb][:, 0:tw], AF.Sigmoid, ["ps%d" % pgb], ["enb"])
                    wo = wload([(w_ro, 0, jb * 512, 512)], krows=1024)
                    srcb = WBF["w_hg_o"][:, jb * 512:(jb + 1) * 512].rearrange("(kc p) n -> p kc n", p=128)
                    S.dma("sp", "ws%d_1" % wo, (lambda e, wo=wo, srcb=srcb: e.dma_start(out=WS[wo][:, 8:16, :], in_=srcb)), writes=["ws%d_s1" % wo],
                          reads=["wb_w_hg_o_%d" % ((jb * 512) // PCB)])
                    for mb in range(4):
                        kcb = jb * 4 + mb
                        pya, pyb = bank(), bank()
                        mm_group(PS[pya][:, 0:tw], [(WS[wo][:, kc, mb * 128:(mb + 1) * 128], ogT[:, kc, 0:tw]) for kc in range(8)], wk(wo) + ["ogT"], "ps%d" % pya)
                        mm_group(PS[pyb][:, 0:tw], [(WS[wo][:, 8 + kc, mb * 128:(mb + 1) * 128], ogT[:, 8 + kc, 0:tw]) for kc in range(8)], wk(wo) + ["ogT"], "ps%d" % pyb)
                        tt(ua[:, 0:tw], sgA[:, mb, 0:tw], PS[pya][:, 0:tw], ALU.mult, ["eb", "ps%d" % pya], ["ua"])
                        tt(ub[:, 0:tw], sgB[:, mb, 0:tw], PS[pyb][:, 0:tw], ALU.mult, ["enb", "ps%d" % pyb], ["ub"])
                        tt(mgT[:, kcb, 0:tw], ua[:, 0:tw], ub[:, 0:tw], ALU.add, ["ua", "ub"], ["mgT"], eng="pool")
                load(Gb[:], mod_d[0:1, 2 * D:3 * D].partition_broadcast(128), "Gb", reads=["mod_d2"])
                for jb in range(4):
                    wo = wload([(w_out, 0, jb * 512, 512)])

                    gemm_tm(wo, 0, 512, mgT, "mgT", range(KC), cpt, lambda c, pb, jb=jb: resid_cb(c, pb, jb))

            def resid_cb(c, pb, jb):
                tt(osq[:, 0:512], PS[pb][:, 0:512], Gb[:, jb * 512:(jb + 1) * 512], ALU.mult, ["ps%d" % pb, "Gb"], ["osq"])
                tt(xt[:, c, jb * 512:(jb + 1) * 512], xt[:, c, jb * 512:(jb + 1) * 512], osq[:, 0:512], ALU.add, ["xt", "osq"], ["xt"], eng="pool")

            def ffn(ti, cpt, store=True):
                tw = cpt * 128
                for fb in range(11):
                    wa = wload([(w_up, 0, fb * 512, 512)])
                    wb = wload([(w_up, 0, DFF + fb * 512, 512)])
                    for mb in range(4):
                        blk = fb * 4 + mb
                        res = {}
                        for which, wi in ((0, wa), (1, wb)):
                            ch = which * NFB + blk
                            pb = bank()
                            mm_group(PS[pb][:, 0:tw], [(WS[wi][:, kc, mb * 128:(mb + 1) * 128], hT[:, kc, 0:tw]) for kc in range(KC)], wk(wi) + ["hT"], "ps%d" % pb)
                            if not store:
                                act(halo[:, ch, :], PS[pb][:, tw - 2:tw], AF.Copy, ["ps%d" % pb], ["halo"])
                                continue
                            yb = ybuf[:, which, :]
                            yk = "ybuf%d" % which
                            act(yb[:, 2:2 + tw], PS[pb][:, 0:tw], AF.Copy, ["ps%d" % pb], [yk])
                            ts(yb[:, 0:2], halo[:, ch, :], pmask[:, ti:ti + 1], None, ALU.mult, None, ["halo", "pmask"], [yk], eng="pool")
                            u = ua if which == 0 else ub
                            uk = "ua" if which == 0 else "ub"
                            act(u[:, 0:tw], yb[:, 2:2 + tw], AF.Identity, [yk, "cwf"], [uk], scale=cwf[:, ch, 2:3], bias=cwf[:, ch, 3:4])
                            stt(u[:, 0:tw], yb[:, 1:1 + tw], cwf[:, ch, 1:2], u[:, 0:tw], ALU.mult, ALU.add, [yk, "cwf", uk], [uk])
                            stt(u[:, 0:tw], yb[:, 0:tw], cwf[:, ch, 0:1], u[:, 0:tw], ALU.mult, ALU.add, [yk, "cwf", uk], [uk])
                            S.op("pool", lambda e, yb=yb, ch=ch: e.tensor_copy(halo[:, ch, :], yb[:, tw:tw + 2]), reads=[yk], writes=["halo"])
                        if store:
                            act(t1[:, 0:tw], ua[:, 0:tw], AF.Silu, ["ua"], ["t1"])
                            tt(gT[:, blk, 0:tw], t1[:, 0:tw], ub[:, 0:tw], ALU.mult, ["t1", "ub"], ["gT"])
                if ti == 0:
                    store_dbg("gT", gT[:], "gT")
                if not store:
                    return
                load(Gb[:], mod_d[0:1, 5 * D:6 * D].partition_broadcast(128), "Gb", reads=["mod_d5"])
                for jb in range(4):
                    pbs = [bank() for _ in range(cpt)]
                    parts = [(0, 16), (16, 16), (32, 12)]
                    for pi, (f0, nf) in enumerate(parts):
                        wd = wload([(w_dn, f0 * 128, jb * 512, 512)], krows=nf * 128)
                        for c in range(cpt):
                            def f(e, c=c, wd=wd, f0=f0, nf=nf, pi=pi, pbs=pbs):
                                r = None
                                for k in range(nf):
                                    r = e.matmul(PS[pbs[c]][:, 0:512], lhsT=gT[:, f0 + k, c * 128:(c + 1) * 128], rhs=WS[wd][:, k, 0:512],
                                                 start=(pi == 0 and k == 0), stop=(pi == 2 and k == nf - 1))
                                return r
                            S.op("pe", f, reads=wk(wd) + ["gT"], writes=["ps%d" % pbs[c]])
                    for c in range(cpt):
                        resid_cb(c, pbs[c], jb)

            def final_store(cpt, row0):
                if row0 == 0:
                    store_dbg("x2", xt[:], "xt")
                load(Gb[:], gf_d.partition_broadcast(128), "Gb")
                ss = small[:, 0:cpt]
                rs = small[:, 8:8 + cpt]
                for c in range(cpt):
                    act(sqjb[:, 0:2048], xt[:, c, :], AF.Square, ["xt"], ["sqj", "ss"], accum_out=ss[:, c:c + 1])
                rsqrt_small(rs, ss, 1.0 / D, EPS, ["ss"], "rs")
                for c in range(cpt):
                    stt(xt[:, c, :], xt[:, c, :], rs[:, c:c + 1], Gb[:], ALU.mult, ALU.mult, ["xt", "rs", "Gb"], ["xt"])
                    S.dma("pool", "st_out", lambda e, c=c: e.dma_start(out=out_d[row0 + c * 128: row0 + (c + 1) * 128, :], in_=xt[:, c, :]), reads=["xt"])


            return dict(norm_to_hT=norm_to_hT, rope_tables=rope_tables, ret_head=ret_head, hg_group=hg_group,
                        merge_and_out2=merge_and_out2, ffn=ffn, final_store=final_store, xt=xt)

        OPS = {"s": make_ops(LS), "f": make_ops(LF)}
        print("SBUF bytes/partition (final):", sbytes[0])
        ch0 = 0
        prev_mode = None
        S.barrier()
        for ti, (cpt, mode, row0) in enumerate(tiles):
            so = mode == "s"
            if mode == "f" and prev_mode != "f":
                while precast_todo:
                    precast(*precast_todo.pop(0))
                late_mod()
            if prev_mode is not None and prev_mode != mode:
                S.barrier()
            prev_mode = mode
            O = OPS[mode]
            for c in range(cpt):
                load(O["xt"][:, c, :], x_d[(ch0 + c) * 128:(ch0 + c + 1) * 128, :], "xt", eng="pool")
            O["norm_to_hT"](0, 1, cpt)
            if so and precast_todo:
                precast(*precast_todo.pop(0))
            O["rope_tables"](ch0 * 128, cpt * 128)
            if ti == 0 and "cos" in dbg_d:
                store_dbg("cos", (LS if so else LF)["cosT"], "cosT")
                store_dbg("sin", (LS if so else LF)["sinT"], "sinT")
                store_dbg("sqj", (LS if so else LF)["sqj"], "sqj")
            for h in range(RET_H):
                O["ret_head"](h, ch0, cpt, so)
                if ti == 0 and h == 0 and "kf32" in dbg_d:
                    store_dbg("kf32", (LS if so else LF)["kf32"], "kf32")
                    store_dbg("R0", Rst[:, 0, :], "Rst")
            for g in range(2):
                O["hg_group"](g, ch0, cpt, so)
            if not so and row0 == 0 and "ogT" in dbg_d:
                store_dbg("ogT", LF["ogT"], "ogT")
            if not so:
                O["merge_and_out2"](cpt)
                O["norm_to_hT"](2, 3, cpt)
                O["ffn"](ti, cpt, row0 is not None)
                if row0 is not None:
                    O["final_store"](cpt, row0)
            if so and gemv_todo:
                gemv_block(*gemv_todo.pop(0))
            ch0 += cpt

        fin = [(k, v) for k, v in S.dma_cnt.items() if k.startswith("st_") or k.startswith("dbg_")]
        S.wait_all("sp", fin)
        with nc.Block() as block:
            S.replay(block)
    return nc


CPS = 4


def make_tiles(nstate_chunks, pre_chunks, own_chunks):
    tiles = []
    n = nstate_chunks
    while n > 0:
        c = min(CPS, n)
        tiles.append((c, "s", None))
        n -= c
    n = pre_chunks
    while n > 0:
        c = min(CPTMAX, n)
        tiles.append((c, "f", None))
        n -= c
    row = 0
    n = own_chunks
    while n > 0:
        c = min(CPTMAX, n)
        tiles.append((c, "f", row))
        row += c * 128
        n -= c
    return tiles


_CACHE = {}


def kernel(x, c, positions, w_ada, b_ada, g_norm1, w_in, w_ret_o, w_hg_o, w_out,
           hg_lb, g_norm2, w_up, conv_w, conv_b, w_down, g_final):
    own = SEQ // NCORES // 128
    pre = 1
    nstate = (NCORES - 1) * own - pre
    tiles = make_tiles(nstate, pre, own)
    if "nc" not in _CACHE:
        _CACHE["nc"] = build(tiles)
    nc = _CACHE["nc"]
    nch = nstate + pre + own
    ntok = nch * 128
    xs = np.asarray(x, np.float32).reshape(SEQ, D)
    ps = np.asarray(positions, np.int32).reshape(SEQ)
    HC = host_consts()
    shared = {
        "c": np.asarray(c, np.float32).reshape(1, D),
        "w_ada": np.ascontiguousarray(np.asarray(w_ada, np.float32).reshape(D, 6 * D)),
        "b_ada": np.asarray(b_ada, np.float32).reshape(1, 6 * D),
        "g_norm1": np.asarray(g_norm1, np.float32).reshape(1, D),
        "w_in": np.ascontiguousarray(np.asarray(w_in, np.float32).reshape(D, IN_COLS)),
        "w_ret_o": np.ascontiguousarray(np.asarray(w_ret_o, np.float32).reshape(1024, D)),
        "w_hg_o": np.ascontiguousarray(np.asarray(w_hg_o, np.float32).reshape(1024, D)),
        "w_out": np.ascontiguousarray(np.asarray(w_out, np.float32).reshape(D, D)),
        "hg_lb": np.asarray(hg_lb, np.float32).reshape(2, 1024),
        "g_norm2": np.asarray(g_norm2, np.float32).reshape(1, D),
        "w_up": np.ascontiguousarray(np.asarray(w_up, np.float32).reshape(D, 2 * DFF)),
        "conv_w": np.asarray(conv_w, np.float32).reshape(3, 2 * DFF),
        "conv_b": np.asarray(conv_b, np.float32).reshape(1, 2 * DFF),
        "w_down": np.ascontiguousarray(np.asarray(w_down, np.float32).reshape(DFF, D)),
        "g_final": np.asarray(g_final, np.float32).reshape(1, D),
    }
    for k in CONST_SHAPES:
        shared["k_" + k] = np.ascontiguousarray(HC[k]).reshape(CONST_SHAPES[k])
    in_maps = []
    for core in range(NCORES):
        nreal = (core + 1) * own * 128
        xc = np.zeros((ntok, D), np.float32)
        xc[ntok - nreal:] = xs[:nreal]
        pc = np.zeros((1, ntok), np.int32)
        pc[0, ntok - nreal:] = ps[:nreal]
        tokmask = np.zeros(ntok, np.float32)
        tokmask[ntok - nreal:] = 1.0
        vm = np.ascontiguousarray(tokmask.reshape(nch, 128).T)
        pm = np.zeros((128, len(tiles)), np.float32)
        chs = 0
        for ti, (cpt, mode, row0) in enumerate(tiles):
            prev_real = 1.0 if (chs * 128 - 1) >= (ntok - nreal) else 0.0
            pm[:, ti] = prev_real
            chs += cpt
        m = dict(shared)
        m.update({"x": xc, "positions": pc, "vmask": vm, "pmask": pm})
        in_maps.append(m)
    res = run_bass_kernel_spmd(nc, in_maps, core_ids=list(range(NCORES)))
    outs = [np.asarray(r["out"])[: own * 128] for r in res.results]
    return np.concatenate(outs, axis=0).reshape(1, SEQ, D).astype(np.float32)
```
